# Optimizing a Trainium2 kernel written in Bass

```python
import jax, jax.numpy as jnp
from jax import lax
import numpy as np

D_MODEL = 1024
BATCH = 8
SEQ = 2048
DEPTH = 4
DEC_BATCH = 128
DEC_SEQ = 8
PAST_LEN = 16384
PAGE_SIZE = 128

N_MIXERS = 2
N_A_LAYERS = (DEPTH + 1) // 2
N_B_LAYERS = DEPTH // 2
CONV_WIDTH = 31
POOL_WINDOWS = (2, 4, 8, 16)
POOL_GROUPS = len(POOL_WINDOWS)
POOL_GROUP_DIM = D_MODEL // POOL_GROUPS
POOL_PREFIX = max(POOL_WINDOWS) - 1
N_MEM = 256
N_MEM_HEADS = 4
MEM_HEAD_DIM = D_MODEL // N_MEM_HEADS
D_FF = 2816
FFN_CONV_WIDTH = 3
N_NORMS = 7
RMS_EPS = 1e-6
LN_EPS = 1e-5

kernel_name = 'hybrid_conformer_pool_memxattn_decoder_step'


def _rmsnorm(x, g):
    xf = x.astype(jnp.float32)
    y = xf * lax.rsqrt(jnp.mean(xf * xf, axis=-1, keepdims=True) + RMS_EPS)
    return (y * g.astype(jnp.float32)).astype(x.dtype)


def _layernorm(x, g, b):
    xf = x.astype(jnp.float32)
    mu = jnp.mean(xf, axis=-1, keepdims=True)
    xc = xf - mu
    y = xc * lax.rsqrt(jnp.mean(xc * xc, axis=-1, keepdims=True) + LN_EPS)
    return (y * g.astype(jnp.float32) + b.astype(jnp.float32)).astype(x.dtype)


def _causal_dwconv(ext, w, b):
    c = ext.shape[-1]
    y = lax.conv_general_dilated(ext, w[:, None, :].astype(ext.dtype), (1,), 'VALID',
                                 dimension_numbers=('NWC', 'WIO', 'NWC'),
                                 feature_group_count=c)
    return y + b.astype(y.dtype)


def _conformer_conv(h, prefix, w_in, b_in, w_dw, b_dw, ln_g, ln_b, w_out, b_out):
    u = h @ w_in + b_in
    a, gate = jnp.split(u, 2, axis=-1)
    glu = a * jax.nn.sigmoid(gate)
    ext = jnp.concatenate([prefix.astype(glu.dtype), glu], axis=1)
    c = _causal_dwconv(ext, w_dw, b_dw)
    c = jax.nn.silu(_layernorm(c, ln_g, ln_b))
    out = c @ w_out + b_out
    return out, ext[:, -(CONV_WIDTH - 1):]


def _multiscale_pool(h, prefix, pos0, w_group, scale):
    L = h.shape[1]
    z_in = jnp.concatenate([prefix.astype(h.dtype), h], axis=1)
    zf = z_in.astype(jnp.float32)
    csum = jnp.cumsum(zf, axis=1)
    csum = jnp.concatenate([jnp.zeros_like(csum[:, :1]), csum], axis=1)
    pos = (pos0 + jnp.arange(L)).astype(jnp.float32)
    off = POOL_PREFIX + 1
    means = []
    for g, w in enumerate(POOL_WINDOWS):
        sl = slice(g * POOL_GROUP_DIM, (g + 1) * POOL_GROUP_DIM)
        s = csum[:, off:off + L, sl] - csum[:, off - w:off - w + L, sl]
        cnt = jnp.minimum(jnp.float32(w), pos + 1.0)
        means.append(s / cnt[None, :, None])
    pooled = (jnp.concatenate(means, axis=-1) - zf[:, POOL_PREFIX:]).astype(h.dtype)
    b = h.shape[0]
    pg = pooled.reshape(b, L, POOL_GROUPS, POOL_GROUP_DIM)
    out = jnp.einsum('blgc,gcd->blgd', pg, w_group).reshape(b, L, D_MODEL) * scale
    return out, z_in[:, -POOL_PREFIX:]


def _mem_kv(mem, g_mem, w_kv):
    b = mem.shape[0]
    kv = (_rmsnorm(mem, g_mem) @ w_kv).reshape(b, N_MEM, 2, N_MEM_HEADS, MEM_HEAD_DIM)
    return kv[:, :, 0], kv[:, :, 1]


def _cross_attn(h, k, v, w_q, w_o):
    b, L, _ = h.shape
    q = (h @ w_q).reshape(b, L, N_MEM_HEADS, MEM_HEAD_DIM)
    s = jnp.einsum('blhd,bmhd->bhlm', q, k.astype(q.dtype)).astype(jnp.float32) * (MEM_HEAD_DIM ** -0.5)
    p = jax.nn.softmax(s, axis=-1).astype(v.dtype)
    o = jnp.einsum('bhlm,bmhd->blhd', p, v).reshape(b, L, D_MODEL)
    return o.astype(h.dtype) @ w_o


def _conv_ffn(h, prefix, w_up, w_dw, b_dw, w_down):
    u = h @ w_up
    ext = jnp.concatenate([prefix.astype(u.dtype), u], axis=1)
    c = _causal_dwconv(ext, w_dw, b_dw)
    g, val = jnp.split(c, 2, axis=-1)
    out = (jax.nn.silu(g) * val) @ w_down
    return out, ext[:, -(FFN_CONV_WIDTH - 1):]


def _trunk(x, pos0, conv_prefix, pool_prefix, ffn_prefix, mem_k, mem_v, norm_gains,
           a_w_in, a_b_in, a_w_dw, a_b_dw, a_ln_g, a_ln_b, a_w_out, a_b_out,
           p_w_group, p_scale, c_w_q, c_w_o, f_w_up, f_w_dw, f_b_dw, f_w_down):
    conv_states, pool_states, ffn_states = [], [], []
    for i in range(DEPTH):
        g = norm_gains[i]
        j = i // N_MIXERS
        h = _rmsnorm(x, g[0])
        if i % N_MIXERS == 0:
            out, st = _conformer_conv(h, conv_prefix[j], a_w_in[j], a_b_in[j], a_w_dw[j], a_b_dw[j],
                                      a_ln_g[j], a_ln_b[j], a_w_out[j], a_b_out[j])
            conv_states.append(st)
        else:
            out, st = _multiscale_pool(h, pool_prefix[j], pos0, p_w_group[j], p_scale[j])
            pool_states.append(st)
        x = x + _rmsnorm(out, g[1])
        h = _rmsnorm(x, g[2])
        x = x + _rmsnorm(_cross_attn(h, mem_k[i], mem_v[i], c_w_q[i], c_w_o[i]), g[3])
        h = _rmsnorm(x, g[4])
        out, st = _conv_ffn(h, ffn_prefix[i], f_w_up[i], f_w_dw[i], f_b_dw[i], f_w_down[i])
        ffn_states.append(st)
        x = x + _rmsnorm(out, g[5])
    return x, jnp.stack(conv_states), jnp.stack(pool_states), jnp.stack(ffn_states)


def setup_inputs(seed: int = 0) -> dict:
    key = jax.random.key(seed)
    ks = iter(jax.random.split(key, 40))

    def nrm(shape, scale=1.0):
        return jax.random.normal(next(ks), shape, jnp.float32) * scale

    d, f2 = D_MODEL, 2 * D_FF
    return {
        'x_prompt': nrm((BATCH, SEQ, d)),
        'x_sample': nrm((DEC_BATCH, DEC_SEQ, d)),
        'mem_prompt': nrm((BATCH, N_MEM, d)),
        'state_conv': nrm((N_A_LAYERS, DEC_BATCH, CONV_WIDTH - 1, d), 0.5),
        'state_pool': nrm((N_B_LAYERS, DEC_BATCH, POOL_PREFIX, d)),
        'state_ffn': nrm((DEPTH, DEC_BATCH, FFN_CONV_WIDTH - 1, f2)),
        'cache_mem_k': nrm((DEPTH, DEC_BATCH, N_MEM, N_MEM_HEADS, MEM_HEAD_DIM)),
        'cache_mem_v': nrm((DEPTH, DEC_BATCH, N_MEM, N_MEM_HEADS, MEM_HEAD_DIM)),
        'norm_gains': 1.0 + nrm((DEPTH, N_NORMS, d), 0.05),
        'a_w_in': nrm((N_A_LAYERS, d, 2 * d), d ** -0.5),
        'a_b_in': nrm((N_A_LAYERS, 2 * d), 0.02),
        'a_w_dw': nrm((N_A_LAYERS, CONV_WIDTH, d), CONV_WIDTH ** -0.5),
        'a_b_dw': nrm((N_A_LAYERS, d), 0.02),
        'a_ln_g': 1.0 + nrm((N_A_LAYERS, d), 0.05),
        'a_ln_b': nrm((N_A_LAYERS, d), 0.02),
        'a_w_out': nrm((N_A_LAYERS, d, d), d ** -0.5),
        'a_b_out': nrm((N_A_LAYERS, d), 0.02),
        'p_w_group': nrm((N_B_LAYERS, POOL_GROUPS, POOL_GROUP_DIM, POOL_GROUP_DIM), POOL_GROUP_DIM ** -0.5),
        'p_scale': 1.0 + nrm((N_B_LAYERS, d), 0.1),
        'c_w_q': nrm((DEPTH, d, d), d ** -0.5),
        'c_w_kv': nrm((DEPTH, d, 2 * d), d ** -0.5),
        'c_w_o': nrm((DEPTH, d, d), d ** -0.5),
        'f_w_up': nrm((DEPTH, d, f2), d ** -0.5),
        'f_w_dw': nrm((DEPTH, FFN_CONV_WIDTH, f2), FFN_CONV_WIDTH ** -0.5),
        'f_b_dw': nrm((DEPTH, f2), 0.02),
        'f_w_down': nrm((DEPTH, D_FF, d), D_FF ** -0.5),
    }


def reference(x_prompt, x_sample, mem_prompt, state_conv, state_pool, state_ffn, cache_mem_k, cache_mem_v,
              norm_gains, a_w_in, a_b_in, a_w_dw, a_b_dw, a_ln_g, a_ln_b, a_w_out, a_b_out,
              p_w_group, p_scale, c_w_q, c_w_kv, c_w_o, f_w_up, f_w_dw, f_b_dw, f_w_down):
    b = x_prompt.shape[0]
    mks, mvs = [], []
    for i in range(DEPTH):
        k, v = _mem_kv(mem_prompt, norm_gains[i, N_NORMS - 1], c_w_kv[i])
        mks.append(k)
        mvs.append(v)
    mem_k_prompt = jnp.stack(mks)
    mem_v_prompt = jnp.stack(mvs)
    zc = jnp.zeros((N_A_LAYERS, b, CONV_WIDTH - 1, D_MODEL), x_prompt.dtype)
    zp = jnp.zeros((N_B_LAYERS, b, POOL_PREFIX, D_MODEL), x_prompt.dtype)
    zf = jnp.zeros((DEPTH, b, FFN_CONV_WIDTH - 1, 2 * D_FF), x_prompt.dtype)
    y_prompt, conv_p, pool_p, ffn_p = _trunk(
        x_prompt, 0, zc, zp, zf, mem_k_prompt, mem_v_prompt, norm_gains,
        a_w_in, a_b_in, a_w_dw, a_b_dw, a_ln_g, a_ln_b, a_w_out, a_b_out,
        p_w_group, p_scale, c_w_q, c_w_o, f_w_up, f_w_dw, f_b_dw, f_w_down)
    y_sample, conv_s, pool_s, ffn_s = _trunk(
        x_sample, PAST_LEN, state_conv, state_pool, state_ffn, cache_mem_k, cache_mem_v, norm_gains,
        a_w_in, a_b_in, a_w_dw, a_b_dw, a_ln_g, a_ln_b, a_w_out, a_b_out,
        p_w_group, p_scale, c_w_q, c_w_o, f_w_up, f_w_dw, f_b_dw, f_w_down)
    return (y_prompt, y_sample, conv_p, pool_p, ffn_p, mem_k_prompt, mem_v_prompt, conv_s, pool_s, ffn_s)
```

```python
import contextlib
import numpy as np
import concourse.bass as bass
import concourse.mybir as mybir
from concourse.bass_utils import run_bass_kernel_spmd

F32 = mybir.dt.float32
BF16 = mybir.dt.bfloat16
AF = mybir.ActivationFunctionType
ALU = mybir.AluOpType

D = 1024
NCH = 8
SEQ = 2048
DEPTH = 4
NSEQ = 16
DSEQ = 8
NMEM = 256
NH = 4
DFF = 2816
F2 = 5632
NFC = 22
CW = 31
G = 2
TP = SEQ // G
SG_ = NSEQ // G
TS = SG_ * DSEQ
TG = TP + TS
TILES = [(0, 512), (512, 512), (1024, TS)]
NT = len(TILES)
ETILES = [(0, 512, (0,)), (512, TG - 512, (1, 2))]
ST = 2
RMS_EPS = 1e-6
LN_EPS = 1e-5
NSLOT = 4
SLOT_BYTES = 8192

def _pc_gain(i, k, c): return (i * 7 + k) * 8 + c
PC_BIN = 224
PC_A = 256
PC_PS = 320
PC_FB = 336
PC_FW = 512
PC_AW = 1040
NPRM = 1536


class Buf:
    __slots__ = ("name", "lw", "rd", "dsem", "dcnt")

    def __init__(self, name):
        self.name = name
        self.lw = None
        self.rd = {}
        self.dsem = None
        self.dcnt = 0


class Eng:
    def __init__(self, name):
        self.name = name
        self.ops = []
        self.count = 0
        self.seen = {}
        self.semkey = "E_" + name


class FW:
    def __init__(self, nc, dry=False):
        self.nc = nc
        self.dry = dry
        self.eng = {n: Eng(n) for n in ("pe", "act", "dve", "pool", "sp")}
        self.semkeys = [e.semkey for e in self.eng.values()]
        self.ndsem = 0
        self.all_dma = {}

    def _deps(self, e, reads, writes):
        need = {}
        for b in reads:
            if b.lw is not None:
                k, v = b.lw
                if need.get(k, 0) < v:
                    need[k] = v
        for b in writes:
            if b.lw is not None:
                k, v = b.lw
                if need.get(k, 0) < v:
                    need[k] = v
            for k, v in b.rd.items():
                if need.get(k, 0) < v:
                    need[k] = v
        waits = []
        for k, v in need.items():
            if k == e.semkey and e.name == "pe":
                continue
            if e.seen.get(k, 0) < v:
                e.seen[k] = v
                waits.append((k, v))
        return waits

    def op(self, engname, fn, reads=(), writes=()):
        if self.dry:
            return
        e = self.eng[engname]
        waits = self._deps(e, reads, writes)
        e.count += 1
        t = (e.semkey, e.count)
        e.ops.append((fn, waits, e.semkey, 1, True))
        for b in writes:
            b.lw = t
            b.rd = {}
        for b in reads:
            if b.rd.get(t[0], 0) < t[1]:
                b.rd[t[0]] = t[1]

    def dma(self, qname, fn, reads=(), writes=(), sem_of=None):
        if self.dry:
            return
        e = self.eng[qname]
        owner = sem_of if sem_of is not None else (writes[0] if writes else reads[0])
        if owner.dsem is None:
            owner.dsem = "D_%d" % self.ndsem
            self.ndsem += 1
            self.semkeys.append(owner.dsem)
        waits = self._deps(e, reads, writes)
        owner.dcnt += 16
        t = (owner.dsem, owner.dcnt)
        e.ops.append((fn, waits, owner.dsem, 16, False))
        self.all_dma[owner.dsem] = owner.dcnt
        for b in writes:
            b.lw = t
            b.rd = {}
        for b in reads:
            if b.rd.get(t[0], 0) < t[1]:
                b.rd[t[0]] = t[1]

    def fence(self, bufs_old, bufs_new):
        need = {}
        for b in bufs_old:
            if b.lw is not None:
                k, v = b.lw
                need[k] = max(need.get(k, 0), v)
            for k, v in b.rd.items():
                need[k] = max(need.get(k, 0), v)
        for b in bufs_new:
            b.lw = None
            b.rd = dict(need)

    def run(self):
        nc = self.nc
        with contextlib.ExitStack() as st:
            sems = {}
            for k in self.semkeys:
                sems[k] = st.enter_context(nc.semaphore(k))
            block = st.enter_context(nc.Block())
            fin = self.eng["sp"]
            handles = {"pe": "tensor", "act": "scalar", "dve": "vector", "pool": "gpsimd", "sp": "sync"}

            marked = {}
            for e in self.eng.values():
                for (fn, waits, isem, iamt, attach) in e.ops:
                    for (k, v) in waits:
                        if k.startswith("E_"):
                            marked.setdefault(k, set()).add(v)
            rank = {k: {v: i + 1 for i, v in enumerate(sorted(vs))} for k, vs in marked.items()}

            def wv(k, v):
                return rank[k][v] if k.startswith("E_") else v

            def make(e):
                def body(h):
                    ordinal = 0
                    for (fn, waits, isem, iamt, attach) in e.ops:
                        if attach and waits:
                            for (k, v) in waits[:-1]:
                                h.wait_ge(sems[k], wv(k, v))
                            r = fn(h)
                            first, last = r if isinstance(r, tuple) else (r, r)
                            k, v = waits[-1]
                            first._wait_ge(sems[k], wv(k, v))
                        else:
                            for (k, v) in waits:
                                h.wait_ge(sems[k], wv(k, v))
                            r = fn(h)
                            first, last = r if isinstance(r, tuple) else (r, r)
                        if isem.startswith("E_"):
                            ordinal += 1
                            if ordinal in rank.get(isem, ()):
                                last.then_inc(sems[isem], 1)
                        else:
                            last.then_inc(sems[isem], iamt)
                    if e is fin:
                        for k, v in self.all_dma.items():
                            h.wait_ge(sems[k], v)
                return body

            for name, e in self.eng.items():
                if not e.ops and e is not fin:
                    continue
                getattr(block, handles[name])(make(e))


class Arena:
    def __init__(self, nc, nbytes):
        self.nbytes = nbytes
        self.t = nc.alloc_sbuf_tensor("arena", [128, nbytes // 2], BF16)
        self.top = 0

    def at(self, off, shape, dt):
        n = int(np.prod(shape))
        esz = 4 if dt == F32 else 2
        assert off % 4 == 0 and off + n * esz <= self.nbytes, (off, shape, self.nbytes)
        v = self.t[:, off // 2: off // 2 + n * esz // 2]
        if dt == F32:
            v = v.bitcast(F32)
        if len(shape) == 2:
            v = v.rearrange("p (a b) -> p a b", a=shape[0])
        elif len(shape) == 3:
            v = v.rearrange("p (a b c) -> p a b c", a=shape[0], b=shape[1])
        elif len(shape) == 4:
            v = v.rearrange("p (a b c d) -> p a b c d", a=shape[0], b=shape[1], c=shape[2])
        return v

    def alloc(self, shape, dt):
        n = int(np.prod(shape)) * (4 if dt == F32 else 2)
        off = self.top
        self.top = (off + n + 63) // 64 * 64
        assert self.top <= self.nbytes, ("arena overflow", self.top, self.nbytes)
        return self.at(off, shape, dt), off


class Builder:
    def __init__(self, nc, dram, dry, slab_log):
        self.nc = nc
        self.d = dram
        self.fw = FW(nc, dry=dry)
        self.dry = dry
        self.slab_log = slab_log
        self.slab_idx = 0
        self.slab_loaded = 0
        self.pending_post = None
        self.nmm = 0
        self.marks = []
        self._alloc()

    def _alloc(self):
        nc = self.nc
        self.ps_cls = {"st": [0, 1], "mm": [2, 3, 4, 5], "aux": [6, 7], "mm6": [2, 3, 4, 5, 6, 7]}
        self.ps_i = {"st": 0, "mm": 0, "aux": 0, "mm6": 0}
        if self.dry:
            return
        AR = Arena(nc, 211200)
        self.AR = AR
        B = Buf
        self.X, _ = AR.alloc([NCH, TG], F32)
        self.XB = [[B("x") for _ in range(NT)] for _ in range(NCH)]
        self.PRM, _ = AR.alloc([NPRM], F32)
        self.PRMB = B("prm")
        self.IDENT, _ = AR.alloc([128], BF16)
        self.ONESM, _ = AR.alloc([128], BF16)
        self.ONES1, _ = AR.alloc([128], BF16)
        self.CONB = B("const")
        self.RC, _ = AR.alloc([16], F32)
        self.CARRY_A, _ = AR.alloc([2, NCH, 30], BF16)
        self.CARRY_P, _ = AR.alloc([2, NCH, 16], F32)
        self.CARRY_F, _ = AR.alloc([DEPTH, 44, 2], BF16)
        self.CAB = [B("ca") for _ in range(2)]
        self.CPB = [B("cp") for _ in range(2)]
        self.CFB = [B("cf") for _ in range(DEPTH)]
        self.KT, _ = AR.alloc([NCH, NMEM], BF16)
        self.VV, _ = AR.alloc([2, D], BF16)
        self.KTB = B("kt")
        self.VVB = B("vv")
        self.slot_off = []
        for s in range(NSLOT):
            _, off = AR.alloc([SLOT_BYTES // 2], BF16)
            self.slot_off.append(off)
        self.SLB = [B("slot%d" % s) for s in range(NSLOT)]
        self.HB, _ = AR.alloc([NCH, TG], BF16)
        self.HBB = [[B("h") for _ in range(NT)] for _ in range(NCH)]
        self.CO, _ = AR.alloc([NCH, TG], F32)
        self.COB = [[B("co") for _ in range(NT)] for _ in range(NCH)]
        self.MS, _ = AR.alloc([TG], F32)
        self.R, _ = AR.alloc([TG], F32)
        self.MEAN, self.mean_off = AR.alloc([TG], F32)
        self.XTRA, self.xtra_off = AR.alloc([TG], F32)
        self.MSB = [B("ms") for _ in range(NT)]
        self.RB = [B("r") for _ in range(NT)]
        self.MEANB = [B("mean") for _ in range(NT)]
        self.XTRAB = B("xtra")
        self.NSQ = 4
        self.SQ = [AR.alloc([TG - 512], BF16)[0] for _ in range(self.NSQ)]
        self.SQB = [B("sq") for _ in range(self.NSQ)]
        self.sq_i = 0
        self.NTMP = 2
        self.TMP = [AR.alloc([512], F32)[0] for _ in range(self.NTMP)]
        self.TMPB = [B("tmp") for _ in range(self.NTMP)]
        self.tmp_i = 0
        self.NSTG = 2
        self.STG = [AR.alloc([512], F32)[0] for _ in range(self.NSTG)]
        self.STGB = [B("stg") for _ in range(self.NSTG)]
        self.stg_i = 0
        xt = [self.AR.at(self.xtra_off + 2048 * i, [512], F32) for i in range(2)]
        self.FT = self.TMP + self.STG + xt
        self.FTB = self.TMPB + self.STGB + [B("xt0"), B("xt1")]
        self.ft_i = 0
        self.WSIZE = 43008
        _, self.w_off = AR.alloc([self.WSIZE // 2], BF16)
        self.WB_cur = []
        self.PS = [nc.alloc_psum_tensor("ps%d" % i, [128, 512], F32) for i in range(8)]
        self.PSB = [B("ps%d" % i) for i in range(8)]
        self.DRB = B("dram_passthru")

    def W(self, off, shape, dt):
        return self.AR.at(self.w_off + off, shape, dt)

    def new_view(self, bufs):
        if self.dry:
            return
        shared = [self.MEANB[t] for t in range(NT)] + [self.XTRAB, self.FTB[4], self.FTB[5]]
        self.fw.fence(self.WB_cur + shared, list(bufs) + shared)
        self.WB_cur = list(bufs)

    def ps(self, cls):
        lst = self.ps_cls[cls]
        i = lst[self.ps_i[cls] % len(lst)]
        self.ps_i[cls] += 1
        return i

    def sq(self):
        i = self.sq_i % self.NSQ
        self.sq_i += 1
        return i

    def tmp(self):
        i = self.tmp_i % self.NTMP
        self.tmp_i += 1
        return i

    def ft(self):
        i = self.ft_i % len(self.FT)
        self.ft_i += 1
        return i

    def stg(self):
        i = self.stg_i % self.NSTG
        self.stg_i += 1
        return i

    def slab(self, spec, first=True):
        if self.dry:
            self.slab_log.append(spec)
            return 0, None
        idx = self.slab_idx
        self.slab_idx += 1
        if first:
            self.slab_base = idx
        while self.slab_loaded < min(len(self.slab_log), self.slab_base + NSLOT):
            self._load_slab(self.slab_loaded)
            self.slab_loaded += 1
        return idx % NSLOT, self.SLB[idx % NSLOT]

    def _load_slab(self, i):
        spec = self.slab_log[i]
        s = i % NSLOT
        for (boff, shape, dt, src, q) in spec(self.d):
            dst = self.AR.at(self.slot_off[s] + boff, shape, dt)
            self.fw.dma(q, (lambda h, dst=dst, src=src: h.dma_start(out=dst, in_=src)), writes=[self.SLB[s]])

    def slot(self, s, boff, shape, dt):
        return self.AR.at(self.slot_off[s] + boff, shape, dt)

    def prm(self, col):
        return self.PRM[:, col:col + 1]

    def mm(self, cls_or_bank, n, terms, reads, col0=0):
        bank = cls_or_bank if isinstance(cls_or_bank, int) else self.ps(cls_or_bank)
        if self.dry:
            return bank
        out = self.PS[bank][:, col0:col0 + n]
        nt = len(terms)
        self.nmm += nt

        def fn(h):
            first = None
            ins = None
            for i, (l, r) in enumerate(terms):
                ins = h.matmul(out, l, r, start=(i == 0), stop=(i == nt - 1))
                if first is None:
                    first = ins
            return first, ins
        self.fw.op("pe", fn, reads=reads, writes=[self.PSB[bank]])
        return bank

    def act(self, out, in_, func, reads, writes, bias=None, scale=None):
        kw = {}
        if bias is not None:
            kw["bias"] = bias
        if scale is not None:
            kw["scale"] = scale
        self.fw.op("act", lambda h: h.activation(out=out, in_=in_, func=func, **kw), reads=reads, writes=writes)

    def stt(self, out, in0, scalar, in1, op0, op1, reads, writes):
        self.fw.op("dve", lambda h: h.scalar_tensor_tensor(out=out, in0=in0, scalar=scalar, in1=in1, op0=op0, op1=op1),
                   reads=reads, writes=writes)

    def tt(self, out, in0, in1, op, reads, writes):
        self.fw.op("dve", lambda h: h.tensor_tensor(out=out, in0=in0, in1=in1, op=op), reads=reads, writes=writes)

    def ttp(self, out, in0, in1, op, reads, writes):
        self.fw.op("pool", lambda h: h.tensor_tensor(out=out, in0=in0, in1=in1, op=op), reads=reads, writes=writes)

    def ts(self, out, in0, s1, op0, reads, writes):
        self.fw.op("dve", lambda h: h.tensor_scalar(out=out, in0=in0, scalar1=s1, scalar2=None, op0=op0), reads=reads, writes=writes)

    def cp(self, out, in_, reads, writes):
        self.fw.op("dve", lambda h: h.tensor_copy(out=out, in_=in_), reads=reads, writes=writes)

    def rms_stats(self, src3, srcbufs, e):
        e0, en, tl = ETILES[e]
        banks = {t: self.ps("st") for t in tl}
        for c in range(NCH):
            q = self.sq()
            self.act(self.SQ[q][:, :en], src3[:, c, e0:e0 + en], AF.Square, reads=[srcbufs[c][t] for t in tl], writes=[self.SQB[q]])
            for t in tl:
                c0, n = TILES[t]
                out = self.PS[banks[t]][:, :n]
                rhs = self.SQ[q][:, c0 - e0:c0 - e0 + n]
                self.nmm += 1
                self.fw.op("pe", (lambda h, out=out, rhs=rhs, c=c: h.matmul(out, self.ONESM, rhs, start=(c == 0), stop=(c == NCH - 1))),
                           reads=[self.SQB[q], self.CONB], writes=[self.PSB[banks[t]]])
        for t in tl:
            c0, n = TILES[t]
            self.ts(self.MS[:, c0:c0 + n], self.PS[banks[t]][:, :n], RMS_EPS, ALU.add, reads=[self.PSB[banks[t]]], writes=[self.MSB[t]])
        self.ln_exp_rstd(e0, en, tl)

    def ln_exp_rstd(self, e0, en, tl):
        self.act(self.R[:, e0:e0 + en], self.MS[:, e0:e0 + en], AF.Ln, reads=[self.MSB[t] for t in tl], writes=[self.RB[t] for t in tl])
        self.act(self.R[:, e0:e0 + en], self.R[:, e0:e0 + en], AF.Exp, reads=[self.RB[t] for t in tl], writes=[self.RB[t] for t in tl], scale=-0.5)

    def post_e(self, L, k, e):
        e0, en, tl = ETILES[e]
        self.rms_stats(self.CO, self.COB, e)
        for c in range(NCH):
            o = self.CO[:, c, e0:e0 + en]
            cb = [self.COB[c][t] for t in tl]
            xb = [self.XB[c][t] for t in tl]
            self.stt(o, o, self.prm(_pc_gain(L, k, c)), self.R[:, e0:e0 + en], ALU.mult, ALU.mult,
                     reads=cb + [self.RB[t] for t in tl] + [self.PRMB], writes=cb)
            x = self.X[:, c, e0:e0 + en]
            self.tt(x, x, o, ALU.add, reads=xb + cb, writes=xb)

    def pre_e_hb(self, L, k, e):
        e0, en, tl = ETILES[e]
        self.rms_stats(self.X, self.XB, e)
        for c in range(NCH):
            self.stt(self.HB[:, c, e0:e0 + en], self.X[:, c, e0:e0 + en], self.prm(_pc_gain(L, k, c)), self.R[:, e0:e0 + en],
                     ALU.mult, ALU.mult, reads=[self.XB[c][t] for t in tl] + [self.RB[t] for t in tl] + [self.PRMB],
                     writes=[self.HBB[c][t] for t in tl])

    def boundary(self, pre_fn):
        if self.dry:
            return
        pend = self.pending_post
        self.pending_post = None
        for e in range(len(ETILES)):
            if pend is not None:
                self.post_e(pend[0], pend[1], e)
            if pre_fn is not None:
                pre_fn(e)

    def postnorm(self, L, k):
        self.pending_post = (L, k)

    def hb(self, c, t):
        c0, n = TILES[t]
        return self.HB[:, c, c0:c0 + n]

    def init_consts(self):
        if self.dry:
            return
        fw = self.fw
        d = self.d
        fw.dma("sp", lambda h: h.dma_start(out=self.PRM, in_=d["prm"][:, :]), writes=[self.PRMB])
        fw.dma("pool", lambda h: h.dma_start(out=self.IDENT, in_=d["ident"][:, :]), writes=[self.CONB])
        fw.op("pool", lambda h: h.memset(self.ONESM, 1.0 / D), writes=[self.CONB])
        fw.op("pool", lambda h: h.memset(self.ONES1, 1.0), writes=[self.CONB])
        for i in range(16):
            fw.op("pool", (lambda h, i=i: h.memset(self.RC[:, i:i + 1], 1.0 / (i + 1))), writes=[self.CONB])
        for j in range(2):
            fw.op("pool", (lambda h, j=j: h.memset(self.CARRY_A[:, j], 0.0)), writes=[self.CAB[j]])
            fw.op("pool", (lambda h, j=j: h.memset(self.CARRY_P[:, j], 0.0)), writes=[self.CPB[j]])
        for i in range(DEPTH):
            fw.op("pool", (lambda h, i=i: h.memset(self.CARRY_F[:, i], 0.0)), writes=[self.CFB[i]])

    def load_x(self, g):
        if self.dry:
            return
        xT = self.d["xT"].rearrange("(c p) t -> p c t", p=128)
        allx = [self.XB[c][t] for c in range(NCH) for t in range(NT)]
        self.fw.dma("sp", lambda h: h.dma_start(out=self.X[:, :, 0:TP], in_=xT[:, :, TP * g:TP * g + TP]),
                    writes=allx, sem_of=self.XB[0][0])
        self.fw.dma("sp", lambda h: h.dma_start(out=self.X[:, :, TP:TG], in_=xT[:, :, SEQ + TS * g:SEQ + TS * g + TS]),
                    writes=allx, sem_of=self.XB[0][0])

    def store_x(self, g):
        if self.dry:
            return
        yT = self.d["yT"].rearrange("(c p) t -> p c t", p=128)
        allx = [self.XB[c][t] for c in range(NCH) for t in range(NT)]
        self.fw.dma("sp", lambda h: h.dma_start(out=yT[:, :, TP * g:TP * g + TP], in_=self.X[:, :, 0:TP]),
                    reads=allx, sem_of=self.XB[0][0])
        self.fw.dma("sp", lambda h: h.dma_start(out=yT[:, :, SEQ + TS * g:SEQ + TS * g + TS], in_=self.X[:, :, TP:TG]),
                    reads=allx, sem_of=self.XB[0][0])

    def a_mixer(self, L, g):
        j = L // 2
        fw = self.fw
        d = self.d
        dry = self.dry
        B = Buf
        if not dry:
            self.boundary(lambda e: self.pre_e_hb(L, 0, e))
            GLUX = self.W(0, [NCH, TP + 30], BF16)
            GLUS = self.W(16896, [NCH, SG_, 38], BF16)
            DG = self.W(21760, [2, CW, 128], BF16)
            GLUF = self.W(37632, [NCH, 30], F32)
            GLUSF = self.W(38592, [NCH, SG_, DSEQ], F32)
            GXH = [B("gxh") for _ in range(NCH)]
            GXB = [[B("gx") for _ in range(2)] for _ in range(NCH)]
            GSB = [B("gs") for _ in range(NCH)]
            DGB = [B("dg") for _ in range(2)]
            GFB = B("gluf")
            GSFB = B("glusf")
            self.new_view([b for l in GXB for b in l] + GXH + GSB + DGB + [GFB, GSFB])
            self.cp(GLUX[:, :, 0:30], self.CARRY_A[:, j], reads=[self.CAB[j]], writes=GXH)
        s, sb = self.slab(lambda d, j=j, g=g: [(0, [NCH * SG_ * 30], BF16, d["sconv"][j, g], "pool")])
        if not dry:
            src = self.slot(s, 0, [NCH, SG_, 30], BF16)
            self.cp(GLUS[:, :, :, 0:30], src, reads=[sb], writes=GSB)
            for c in range(NCH):
                fw.dma("sp", (lambda h, c=c: h.dma_start(out=d["ncs"][j, g, :, c, :, 0:22], in_=d["sconv4"][j, g, :, c, :, 8:30])),
                       sem_of=self.DRB)
        last_g = (g == G - 1)
        for sl in range(4):
            s, sb = self.slab(lambda d, j=j, sl=sl: [
                (0, [NCH, 256], BF16, d["a_w_in"][j].rearrange("(kc p) n -> p kc n", p=128)[:, :, 256 * sl:256 * sl + 256], "pool"),
                (NCH * 256 * 2, [NCH, 256], BF16, d["a_w_in"][j].rearrange("(kc p) n -> p kc n", p=128)[:, :, D + 256 * sl:D + 256 * sl + 256], "pool")])
            if dry:
                continue
            wa = self.slot(s, 0, [NCH, 256], BF16)
            wg = self.slot(s, NCH * 256 * 2, [NCH, 256], BF16)
            for cc in range(2):
                c = 2 * sl + cc
                banks = []
                for t, (c0, n) in enumerate(TILES):
                    hr = [self.HBB[kc][t] for kc in range(NCH)] + [sb]
                    ba = self.mm("mm", n, [(wa[:, kc, 128 * cc:128 * cc + 128], self.hb(kc, t)) for kc in range(NCH)], hr)
                    bg = self.mm("mm", n, [(wg[:, kc, 128 * cc:128 * cc + 128], self.hb(kc, t)) for kc in range(NCH)], hr)
                    if t == 1:
                        self._glu_tile(j, c, 0, banks[0][0], banks[0][1], GLUX, GLUS, GLUF, GLUSF, GXB, GSB, GFB, GSFB, last_g)
                    banks.append((ba, bg))
                self._glu_tile(j, c, 1, banks[1][0], banks[1][1], GLUX, GLUS, GLUF, GLUSF, GXB, GSB, GFB, GSFB, last_g)
                self._glu_tile(j, c, 2, banks[2][0], banks[2][1], GLUX, GLUS, GLUF, GLUSF, GXB, GSB, GFB, GSFB, last_g)
        if not dry:
            if last_g:
                fw.dma("sp", lambda h: h.dma_start(out=d["ncp"][j].rearrange("(c p) k -> p c k", p=128), in_=GLUF), reads=[GFB])
            fw.dma("sp", lambda h: h.dma_start(out=d["ncs"][j, g, :, :, :, 22:30], in_=GLUSF), reads=[GSFB])
            for c in range(NCH):
                db = c % 2
                for k in range(CW):
                    self.ts(DG[:, db, k], self.IDENT, self.prm(PC_AW + j * 248 + c * CW + k), ALU.mult,
                            reads=[self.CONB, self.PRMB], writes=[DGB[db]])
                for t, (c0, n) in enumerate(TILES):
                    if t < ST:
                        terms = [(DG[:, db, k], GLUX[:, c, c0 + k:c0 + k + n]) for k in range(CW)]
                        rd = [DGB[db], GXH[c], GXB[c][0]] + ([GXB[c][1]] if t == 1 else [])
                    else:
                        terms = [(DG[:, db, k], GLUS[:, c, :, k:k + DSEQ]) for k in range(CW)]
                        rd = [DGB[db], GSB[c]]
                    bank = self.mm("aux", n, terms, rd)
                    self.act(self.CO[:, c, c0:c0 + n], self.PS[bank][:, :n], AF.Identity, reads=[self.PSB[bank], self.PRMB],
                             writes=[self.COB[c][t]], bias=self.prm(PC_A + j * 32 + 0 + c))
            if not last_g:
                self.cp(self.CARRY_A[:, j], GLUX[:, :, TP:TP + 30], reads=[GXB[c][1] for c in range(NCH)], writes=[self.CAB[j]])
            for e, (e0, en, tl) in enumerate(ETILES):
                bms = {}
                bqs = {}
                for i_, t in enumerate(tl):
                    cls = "st" if i_ == 0 else "aux"
                    bms[t] = self.ps(cls)
                    bqs[t] = self.ps(cls)
                for c in range(NCH):
                    src = self.CO[:, c, e0:e0 + en]
                    cb = [self.COB[c][t] for t in tl]
                    q1 = self.sq()
                    self.act(self.SQ[q1][:, :en], src, AF.Copy, reads=cb, writes=[self.SQB[q1]])
                    q2 = self.sq()
                    self.act(self.SQ[q2][:, :en], src, AF.Square, reads=cb, writes=[self.SQB[q2]])
                    for t in tl:
                        c0, n = TILES[t]
                        for (bk, q) in ((bms[t], q1), (bqs[t], q2)):
                            out = self.PS[bk][:, :n]
                            rhs = self.SQ[q][:, c0 - e0:c0 - e0 + n]
                            self.nmm += 1
                            fw.op("pe", (lambda h, out=out, rhs=rhs, c=c: h.matmul(out, self.ONESM, rhs, start=(c == 0), stop=(c == NCH - 1))),
                                  reads=[self.SQB[q], self.CONB], writes=[self.PSB[bk]])
                for t in tl:
                    c0, n = TILES[t]
                    mean = self.MEAN[:, c0:c0 + n]
                    ms = self.MS[:, c0:c0 + n]
                    self.cp(mean, self.PS[bms[t]][:, :n], reads=[self.PSB[bms[t]]], writes=[self.MEANB[t]])
                    self.stt(ms, mean, -1.0, mean, ALU.mult, ALU.mult, reads=[self.MEANB[t]], writes=[self.MSB[t]])
                    self.stt(ms, ms, LN_EPS, self.PS[bqs[t]][:, :n], ALU.add, ALU.add, reads=[self.MSB[t], self.PSB[bqs[t]]], writes=[self.MSB[t]])
                self.ln_exp_rstd(e0, en, tl)
                for c in range(NCH):
                    cc_ = self.CO[:, c, e0:e0 + en]
                    cb = [self.COB[c][t] for t in tl]
                    self.tt(cc_, cc_, self.MEAN[:, e0:e0 + en], ALU.subtract, reads=cb + [self.MEANB[t] for t in tl], writes=cb)
                    self.tt(cc_, cc_, self.R[:, e0:e0 + en], ALU.mult, reads=cb + [self.RB[t] for t in tl], writes=cb)
                    self.act(self.HB[:, c, e0:e0 + en], cc_, AF.Silu, reads=cb + [self.PRMB], writes=[self.HBB[c][t] for t in tl],
                             scale=self.prm(PC_A + j * 32 + 8 + c), bias=self.prm(PC_A + j * 32 + 16 + c))
        sl_h = []
        for sl in range(2):
            sl_h.append(self.slab(lambda d, j=j, sl=sl: [
                (0, [NCH, 512], BF16, d["a_w_out"][j].rearrange("(kc p) n -> p kc n", p=128)[:, :, 512 * sl:512 * sl + 512], "pool")], first=(sl == 0)))
        if not dry:
            for t, (c0, n) in enumerate(TILES):
                for sl in range(2):
                    s, sb = sl_h[sl]
                    w = self.slot(s, 0, [NCH, 512], BF16)
                    for cc in range(4):
                        c = 4 * sl + cc
                        bank = self.mm("mm", n, [(w[:, kc, 128 * cc:128 * cc + 128], self.hb(kc, t)) for kc in range(NCH)],
                                       [self.HBB[kc][t] for kc in range(NCH)] + [sb])
                        self.act(self.CO[:, c, c0:c0 + n], self.PS[bank][:, :n], AF.Identity, reads=[self.PSB[bank], self.PRMB],
                                 writes=[self.COB[c][t]], bias=self.prm(PC_A + j * 32 + 24 + c))
        self.postnorm(L, 1)

    def _glu_tile(self, j, c, t, ba, bg, GLUX, GLUS, GLUF, GLUSF, GXB, GSB, GFB, GSFB, last_g):
        c0, n = TILES[t]
        k = self.tmp()
        sg = self.TMP[k][:, :n]
        self.act(sg, self.PS[bg][:, :n], AF.Sigmoid, reads=[self.PSB[bg], self.PRMB], writes=[self.TMPB[k]],
                 bias=self.prm(PC_BIN + j * 16 + 8 + c))
        ba_col = self.prm(PC_BIN + j * 16 + c)
        if t < ST:
            self.stt(GLUX[:, c, 30 + c0:30 + c0 + n], self.PS[ba][:, :n], ba_col, sg, ALU.add, ALU.mult,
                     reads=[self.PSB[ba], self.TMPB[k], self.PRMB], writes=[GXB[c][t]])
            if last_g and t == ST - 1:
                self.stt(GLUF[:, c, :], self.PS[ba][:, n - 30:n], ba_col, sg[:, n - 30:n], ALU.add, ALU.mult,
                         reads=[self.PSB[ba], self.TMPB[k], self.PRMB], writes=[GFB])
        else:
            pa = self.PS[ba][:, :n].rearrange("p (s l) -> p s l", l=DSEQ)
            sg3 = sg.rearrange("p (s l) -> p s l", l=DSEQ)
            self.stt(GLUS[:, c, :, 30:38], pa, ba_col, sg3, ALU.add, ALU.mult,
                     reads=[self.PSB[ba], self.TMPB[k], self.PRMB], writes=[GSB[c]])
            self.stt(GLUSF[:, c], pa, ba_col, sg3, ALU.add, ALU.mult,
                     reads=[self.PSB[ba], self.TMPB[k], self.PRMB], writes=[GSFB])

    def b_mixer(self, L, g):
        j = L // 2
        fw = self.fw
        d = self.d
        dry = self.dry
        B = Buf
        XW = TP + 16
        if not dry:
            HFX = self.W(0, [NCH, XW], F32)
            HFS = self.W(33280, [NCH, SG_, 24], F32)
            T1S = self.W(39424, [SG_, 24], F32)
            T2S = self.W(40192, [SG_, 24], F32)
            T1 = self.AR.at(self.mean_off, [XW], F32)
            T2 = self.AR.at(self.xtra_off, [XW], F32)
            HXH = [B("hxh") for _ in range(NCH)]
            HXB = [[B("hx") for _ in range(2)] for _ in range(NCH)]
            HSB = [B("hs") for _ in range(NCH)]
            TB = B("t12")
            TSB = B("t12s")
            self.new_view([b for l in HXB for b in l] + HXH + HSB + [TB, TSB])
            self.cp(HFX[:, :, 0:16], self.CARRY_P[:, j], reads=[self.CPB[j]], writes=HXH)
        s, sb = self.slab(lambda d, j=j, g=g: [(0, [NCH * SG_ * 15], F32, d["spool"][j, g], "pool")])
        if not dry:
            src = self.slot(s, 0, [NCH, SG_, 15], F32)
            fw.op("dve", lambda h: h.memset(HFS[:, :, :, 0:1], 0.0), writes=HSB)
            self.cp(HFS[:, :, :, 1:16], src, reads=[sb], writes=HSB)

            def dst_of(c, t):
                c0, n = TILES[t]
                if t < ST:
                    return HFX[:, c, 16 + c0:16 + c0 + n]
                return HFS[:, c, :, 16:24]

            def dstb(c, t):
                return [HXB[c][t]] if t < ST else [HSB[c]]
            def pre_b(e):
                e0, en, tl = ETILES[e]
                self.rms_stats(self.X, self.XB, e)
                for t in tl:
                    c0, n = TILES[t]
                    for c in range(NCH):
                        x = self.X[:, c, c0:c0 + n]
                        r = self.R[:, c0:c0 + n]
                        if t == ST:
                            x = x.rearrange("p (s l) -> p s l", l=DSEQ)
                            r = r.rearrange("p (s l) -> p s l", l=DSEQ)
                        self.stt(dst_of(c, t), x, self.prm(_pc_gain(L, 0, c)), r, ALU.mult, ALU.mult,
                                 reads=[self.XB[c][t], self.RB[t], self.PRMB], writes=dstb(c, t))
            self.boundary(pre_b)
            last_g = (g == G - 1)
            if last_g:
                fw.dma("sp", lambda h: h.dma_start(out=d["npp"][j].rearrange("(c p) k -> p c k", p=128), in_=HFX[:, :, XW - 15:XW]),
                       reads=[HXB[c][1] for c in range(NCH)], sem_of=HXB[0][1])
            else:
                self.cp(self.CARRY_P[:, j], HFX[:, :, XW - 16:XW], reads=[HXB[c][1] for c in range(NCH)], writes=[self.CPB[j]])
            fw.dma("sp", lambda h: h.dma_start(out=d["nps"][j, g], in_=HFS[:, :, :, 9:24]), reads=HSB, sem_of=HSB[0])
            for c in range(NCH):
                lw = c // 2 + 1
                w = 1 << lw
                z = HFX[:, c, :]
                zs = HFS[:, c]
                rdz = [HXH[c], HXB[c][0], HXB[c][1]]
                cur, curs = z, zs
                bufs = [T1, T2]
                bufss = [T1S, T2S]
                for st in range(lw):
                    sh = 1 << st
                    o = bufs[st % 2]
                    os_ = bufss[st % 2]
                    lo = 2 * sh - 1
                    self.tt(o[:, lo:XW], cur[:, lo:XW], cur[:, lo - sh:XW - sh], ALU.add, reads=rdz + [TB], writes=[TB])
                    self.tt(os_[:, :, lo:24], curs[:, :, lo:24], curs[:, :, lo - sh:24 - sh], ALU.add, reads=[HSB[c], TSB], writes=[TSB])
                    cur, curs = o, os_
                inv = 1.0 / w
                self.stt(self.HB[:, c, 0:TP], cur[:, 16:XW], inv, z[:, 16:XW], ALU.mult, ALU.subtract,
                         reads=rdz + [TB], writes=[self.HBB[c][0], self.HBB[c][1]])
                if g == 0:
                    k = self.tmp()
                    tm = self.TMP[k][:, 0:w - 1]
                    self.tt(tm, cur[:, 16:16 + w - 1], self.RC[:, 0:w - 1], ALU.mult, reads=[TB, self.CONB], writes=[self.TMPB[k]])
                    self.tt(self.HB[:, c, 0:w - 1], tm, z[:, 16:16 + w - 1], ALU.subtract, reads=[self.TMPB[k]] + rdz, writes=[self.HBB[c][0]])
                self.stt(self.HB[:, c, TP:TG].rearrange("p (s l) -> p s l", l=DSEQ), curs[:, :, 16:24], inv, zs[:, :, 16:24],
                         ALU.mult, ALU.subtract, reads=[HSB[c], TSB], writes=[self.HBB[c][ST]])
        s, sb = self.slab(lambda d, j=j: [(0, [NCH, 256], BF16, d["p_w_group"][j].rearrange("g (kc p) n -> p (g kc) n", p=128), "pool")])
        if not dry:
            w = self.slot(s, 0, [NCH, 256], BF16)
            for t, (c0, n) in enumerate(TILES):
                for oc in range(NCH):
                    gi, oh = oc // 2, oc % 2
                    bank = self.mm("mm", n, [(w[:, 2 * gi + kc, 128 * oh:128 * oh + 128], self.hb(2 * gi + kc, t)) for kc in range(2)],
                                   [self.HBB[2 * gi][t], self.HBB[2 * gi + 1][t], sb])
                    self.act(self.CO[:, oc, c0:c0 + n], self.PS[bank][:, :n], AF.Copy, reads=[self.PSB[bank], self.PRMB],
                             writes=[self.COB[oc][t]], scale=self.prm(PC_PS + j * 8 + oc))
        self.postnorm(L, 1)

    def cross(self, L, g):
        fw = self.fw
        d = self.d
        dry = self.dry
        B = Buf
        SC = 1.0 / 16.0
        if not dry:
            Q = self.W(0, [NCH, TG], BF16)
            E = self.W(17408, [NH, 2, 512], BF16)
            RS = self.W(25600, [NH, 512], F32)
            ES = self.W(33792, [2, NH, TS], BF16)
            RSS = self.W(34816, [NH, TS], F32)
            MEMN = self.W(35840, [NCH, NMEM], BF16)
            MSM = self.W(39936, [NMEM], F32)
            QB = [[B("q") for _ in range(NT)] for _ in range(NCH)]
            EB = [B("e") for _ in range(NH)]
            RSB = [B("rs") for _ in range(NH)]
            ESB = B("es")
            RSSB = B("rss")
            MNB = B("memn")
            MSMB = B("msm")
            self.new_view([b for l in QB for b in l] + EB + RSB + [ESB, RSSB, MNB, MSMB])
        s, sb = self.slab(lambda d: [(0, [NCH, NMEM], F32, d["memT"].rearrange("(c p) m -> p c m", p=128), "pool")])
        if not dry:
            MEMT = self.slot(s, 0, [NCH, NMEM], F32)
            bank = self.ps("st")
            for c in range(NCH):
                q = self.sq()
                self.act(self.SQ[q][:, :NMEM], MEMT[:, c], AF.Square, reads=[sb], writes=[self.SQB[q]])
                self.nmm += 1
                fw.op("pe", (lambda h, bank=bank, q=q, c=c: h.matmul(self.PS[bank][:, :NMEM], self.ONESM, self.SQ[q][:, :NMEM], start=(c == 0), stop=(c == NCH - 1))),
                      reads=[self.SQB[q], self.CONB], writes=[self.PSB[bank]])
            self.ts(MSM, self.PS[bank][:, :NMEM], RMS_EPS, ALU.add, reads=[self.PSB[bank]], writes=[MSMB])
            self.act(MSM, MSM, AF.Ln, reads=[MSMB], writes=[MSMB])
            self.act(MSM, MSM, AF.Exp, reads=[MSMB], writes=[MSMB], scale=-0.5)
            for c in range(NCH):
                self.stt(MEMN[:, c], MEMT[:, c], self.prm(_pc_gain(L, 6, c)), MSM, ALU.mult, ALU.mult, reads=[sb, MSMB, self.PRMB], writes=[MNB])
        import os
        CC = int(os.environ.get("CROSS_CUT", "99"))
        for sl in range(4):
            if CC <= 1:
                break
            s, sb = self.slab(lambda d, L=L, sl=sl: [
                (0, [NCH, 512], BF16, d["c_w_kv"][L].rearrange("(kc p) n -> p kc n", p=128)[:, :, 512 * sl:512 * sl + 512], "pool")])
            if dry:
                continue
            w = self.slot(s, 0, [NCH, 512], BF16)
            if sl < 2:
                for cc in range(4):
                    c = 4 * sl + cc
                    bank = self.mm("mm", NMEM, [(w[:, kc, 128 * cc:128 * cc + 128], MEMN[:, kc]) for kc in range(NCH)], [MNB, sb])
                    self.act(self.KT[:, c], self.PS[bank][:, :NMEM], AF.Copy, reads=[self.PSB[bank]], writes=[self.KTB])
            if sl >= 2 or g == 0:
                for mc in range(2):
                    bank = self.mm("mm", 512, [(MEMN[:, kc, 128 * mc:128 * mc + 128], w[:, kc]) for kc in range(NCH)], [MNB, sb])
                    if sl >= 2:
                        self.act(self.VV[:, mc, 512 * (sl - 2):512 * (sl - 2) + 512], self.PS[bank][:, :], AF.Copy, reads=[self.PSB[bank]], writes=[self.VVB])
                    if g == 0 and int(os.environ.get("NO_STG", "0")) == 0:
                        k = self.stg()
                        self.act(self.STG[k], self.PS[bank][:, :], AF.Copy, reads=[self.PSB[bank]], writes=[self.STGB[k]])
                        dst = d["mk"] if sl < 2 else d["mv"]
                        col = 512 * (sl % 2)
                        if int(os.environ.get("NO_STGDMA", "0")):
                            continue
                        fw.dma(os.environ.get("STG_Q", "sp"), (lambda h, dst=dst, mc=mc, col=col, k=k: h.dma_start(out=dst[L, 128 * mc:128 * mc + 128, col:col + 512], in_=self.STG[k])),
                               reads=[self.STGB[k]])
        self.boundary(lambda e: self.pre_e_hb(L, 2, e))
        for sl in range(2):
            if CC <= 2:
                break
            s, sb = self.slab(lambda d, L=L, sl=sl: [
                (0, [NCH, 512], BF16, d["c_w_q"][L].rearrange("(kc p) n -> p kc n", p=128)[:, :, 512 * sl:512 * sl + 512], "pool")])
            if dry:
                continue
            w = self.slot(s, 0, [NCH, 512], BF16)
            for t, (c0, n) in enumerate(TILES):
                for cc in range(4):
                    c = 4 * sl + cc
                    bank = self.mm("mm", n, [(w[:, kc, 128 * cc:128 * cc + 128], self.hb(kc, t)) for kc in range(NCH)],
                                   [self.HBB[kc][t] for kc in range(NCH)] + [sb])
                    self.act(Q[:, c, c0:c0 + n], self.PS[bank][:, :n], AF.Copy, reads=[self.PSB[bank]], writes=[QB[c][t]])
        if not dry and CC > 3:
            for t in range(ST):
                c0, n = TILES[t]
                for hh in range(NH):
                    for mc in range(2):
                        bank = self.mm("mm", n, [(self.KT[:, 2 * hh + dc, 128 * mc:128 * mc + 128], Q[:, 2 * hh + dc, c0:c0 + n]) for dc in range(2)],
                                       [self.KTB, QB[2 * hh][t], QB[2 * hh + 1][t]])
                        self.act(E[:, hh, mc, :n], self.PS[bank][:, :n], AF.Exp, reads=[self.PSB[bank]], writes=[EB[hh]], scale=SC)
                for hh in range(NH):
                    bs = self.mm("st", n, [(self.ONES1, E[:, hh, mc, :n]) for mc in range(2)], [EB[hh], self.CONB])
                    self.act(RS[:, hh, :n], self.PS[bs][:, :n], AF.Ln, reads=[self.PSB[bs]], writes=[RSB[hh]])
                    self.act(RS[:, hh, :n], RS[:, hh, :n], AF.Exp, reads=[RSB[hh]], writes=[RSB[hh]], scale=-1.0)
                    for dc in range(2):
                        c = 2 * hh + dc
                        bo = self.mm("aux", n, [(self.VV[:, mc, 128 * c:128 * c + 128], E[:, hh, mc, :n]) for mc in range(2)], [self.VVB, EB[hh]])
                        self.tt(self.HB[:, c, c0:c0 + n], self.PS[bo][:, :n], RS[:, hh, :n], ALU.mult,
                                reads=[self.PSB[bo], RSB[hh]], writes=[self.HBB[c][t]])
        c0, n = TILES[ST]
        bsc = self.ps("mm")
        if CC <= 4:
            self.postnorm(L, 3)
            return
        for sq_ in range(SG_):
            s, sb = self.slab(lambda d, L=L, g=g, sq_=sq_: [
                (0, [NCH, NMEM], BF16, d["kT"][L, g * SG_ + sq_].rearrange("(c p) m -> p c m", p=128), "pool")])
            if dry:
                continue
            kt = self.slot(s, 0, [NCH, NMEM], BF16)

            def fn(h, kt=kt, sq_=sq_, c0=c0, bsc=bsc):
                first = None
                ins = None
                for hh in range(NH):
                    for mc in range(2):
                        col = mc * NH * TS + hh * TS + DSEQ * sq_
                        for dc in range(2):
                            ins = h.matmul(self.PS[bsc][:, col:col + DSEQ], kt[:, 2 * hh + dc, 128 * mc:128 * mc + 128],
                                           Q[:, 2 * hh + dc, c0 + DSEQ * sq_:c0 + DSEQ * sq_ + DSEQ], start=(dc == 0), stop=(dc == 1))
                            if first is None:
                                first = ins
                return first, ins
            self.nmm += 16
            fw.op("pe", fn, reads=[sb] + [QB[c][ST] for c in range(NCH)], writes=[self.PSB[bsc]])
        if not dry:
            self.act(ES, self.PS[bsc][:, :].rearrange("p (a b c) -> p a b c", a=2, b=NH), AF.Exp, reads=[self.PSB[bsc]], writes=[ESB], scale=SC)
            bs = self.mm("st", NH * TS, [(self.ONES1, ES[:, mc].rearrange("p a b -> p (a b)")) for mc in range(2)], [ESB, self.CONB])
            self.act(RSS, self.PS[bs][:, :NH * TS].rearrange("p (a b) -> p a b", a=NH), AF.Ln, reads=[self.PSB[bs]], writes=[RSSB])
            self.act(RSS, RSS, AF.Exp, reads=[RSSB], writes=[RSSB], scale=-1.0)
        bo = self.ps("aux")
        for sq_ in range(SG_):
            s, sb = self.slab(lambda d, L=L, g=g, sq_=sq_: [
                (0, [2, D], BF16, d["v"][L, g * SG_ + sq_].rearrange("(mc p) f -> p mc f", p=128), "pool")])
            if dry:
                continue
            vs = self.slot(s, 0, [2, D], BF16)

            def fn2(h, vs=vs, sq_=sq_, bo=bo):
                first = None
                ins = None
                for c in range(NCH):
                    hh = c // 2
                    col = c * TS + DSEQ * sq_
                    for mc in range(2):
                        ins = h.matmul(self.PS[bo][:, col:col + DSEQ], vs[:, mc, 128 * c:128 * c + 128],
                                       ES[:, mc, hh, DSEQ * sq_:DSEQ * sq_ + DSEQ], start=(mc == 0), stop=(mc == 1))
                        if first is None:
                            first = ins
                return first, ins
            self.nmm += 16
            fw.op("pe", fn2, reads=[sb, ESB], writes=[self.PSB[bo]])
        if not dry:
            for c in range(NCH):
                self.tt(self.HB[:, c, c0:c0 + n], self.PS[bo][:, c * TS:c * TS + TS], RSS[:, c // 2], ALU.mult,
                        reads=[self.PSB[bo], RSSB], writes=[self.HBB[c][ST]])
        sl_h = []
        for sl in range(2):
            sl_h.append(self.slab(lambda d, L=L, sl=sl: [
                (0, [NCH, 512], BF16, d["c_w_o"][L].rearrange("(kc p) n -> p kc n", p=128)[:, :, 512 * sl:512 * sl + 512], "pool")], first=(sl == 0)))
        if not dry:
            for t, (c0, n) in enumerate(TILES):
                for sl in range(2):
                    s, sb = sl_h[sl]
                    w = self.slot(s, 0, [NCH, 512], BF16)
                    for cc in range(4):
                        c = 4 * sl + cc
                        bank = self.mm("mm", n, [(w[:, kc, 128 * cc:128 * cc + 128], self.hb(kc, t)) for kc in range(NCH)],
                                       [self.HBB[kc][t] for kc in range(NCH)] + [sb])
                        self.act(self.CO[:, c, c0:c0 + n], self.PS[bank][:, :n], AF.Copy, reads=[self.PSB[bank]], writes=[self.COB[c][t]])
        self.postnorm(L, 3)

    def ffn(self, L, g):
        fw = self.fw
        d = self.d
        dry = self.dry
        B = Buf
        NJ = NFC // 2
        last_g = (g == G - 1)
        if not dry:
            self.boundary(lambda e: self.pre_e_hb(L, 4, e))
            A = self.W(0, [NJ, TG], BF16)
            UX = self.W(23936, [2, 2, TP + 2], BF16)
            UXS = self.W(32192, [44, SG_, 10], BF16)
            DG3 = self.W(39232, [2, 6, 128], BF16)
            NF = self.W(42304, [44, 2], F32)
            NFS = self.AR.at(self.mean_off, [44, SG_, 2], F32)
            AB = [[B("a") for _ in range(NT)] for _ in range(NJ)]
            UXB = [[B("ux") for _ in range(3)] for _ in range(2)]
            UXSB = B("uxs")
            DG3B = [B("dg3") for _ in range(2)]
            NFB = B("nf")
            NFSB = B("nfs")
            self.new_view([b for l in AB for b in l] + [b for l in UXB for b in l] + [UXSB, NFB, NFSB] + DG3B)
        s, sb = self.slab(lambda d, L=L, g=g: [(0, [44 * SG_ * 2], BF16, d["sffn"][L, g], "pool")])
        if not dry:
            src = self.slot(s, 0, [44, SG_, 2], BF16)
            self.cp(UXS[:, :, :, 0:2], src, reads=[sb], writes=[UXSB])
        import os
        CUT = int(os.environ.get("FFN_CUT", "99"))
        for hf in range(2):
            if CUT <= 1:
                break
            for jj in range(NJ):
                if CUT <= 2 and jj >= 1:
                    break
                jf = NJ * hf + jj
                s, sb = self.slab(lambda d, L=L, jf=jf: [
                    (0, [NCH, 128], BF16, d["f_w_up"][L].rearrange("(kc p) n -> p kc n", p=128)[:, :, 128 * jf:128 * jf + 128], "pool"),
                    (NCH * 128 * 2, [NCH, 128], BF16, d["f_w_up"][L].rearrange("(kc p) n -> p kc n", p=128)[:, :, DFF + 128 * jf:DFF + 128 * jf + 128], "pool")])
                if dry:
                    continue
                wg = self.slot(s, 0, [NCH, 128], BF16)
                wv = self.slot(s, NCH * 128 * 2, [NCH, 128], BF16)
                ub = jf % 2
                chs = (jf, NFC + jf)
                for gv in range(2):
                    self.cp(UX[:, ub, gv, 0:2], self.CARRY_F[:, L, chs[gv]], reads=[self.CFB[L]], writes=[UXB[ub][0]])
                for t, (c0, n) in enumerate(TILES):
                    hr = [self.HBB[kc][t] for kc in range(NCH)] + [sb]
                    t0 = []
                    for gv, wsl in enumerate((wg, wv)):
                        ch = chs[gv]
                        bank = self.mm("mm6", n, [(wsl[:, kc], self.hb(kc, t)) for kc in range(NCH)], hr)
                        p = self.PS[bank][:, :n]
                        ti = self.ft()
                        T0 = self.FT[ti][:, :n]
                        w0c = self.prm(PC_FW + (L * 3 + 0) * 44 + ch)
                        w1c = self.prm(PC_FW + (L * 3 + 1) * 44 + ch)
                        w2c = self.prm(PC_FW + (L * 3 + 2) * 44 + ch)
                        self.act(T0, p, AF.Identity, reads=[self.PSB[bank], self.PRMB], writes=[self.FTB[ti]],
                                 scale=w2c, bias=self.prm(PC_FB + L * 44 + ch))
                        if t < ST:
                            self.act(UX[:, ub, gv, 2 + c0:2 + c0 + n], p, AF.Copy, reads=[self.PSB[bank]], writes=[UXB[ub][1 + t]])
                            if last_g and t == ST - 1:
                                self.act(NF[:, ch], p[:, n - 2:n], AF.Copy, reads=[self.PSB[bank]], writes=[NFB])
                            rd = [UXB[ub][0], UXB[ub][1]] + ([UXB[ub][2]] if t == 1 else [])
                            self.stt(T0, UX[:, ub, gv, c0 + 1:c0 + 1 + n], w1c, T0, ALU.mult, ALU.add, reads=rd + [self.FTB[ti], self.PRMB], writes=[self.FTB[ti]])
                            self.stt(T0, UX[:, ub, gv, c0:c0 + n], w0c, T0, ALU.mult, ALU.add, reads=rd + [self.FTB[ti], self.PRMB], writes=[self.FTB[ti]])
                        else:
                            p3 = p.rearrange("p (s l) -> p s l", l=DSEQ)
                            T3 = T0.rearrange("p (s l) -> p s l", l=DSEQ)
                            self.act(UXS[:, ch, :, 2:10], p3, AF.Copy, reads=[self.PSB[bank]], writes=[UXSB])
                            self.act(NFS[:, ch], p3[:, :, DSEQ - 2:DSEQ], AF.Copy, reads=[self.PSB[bank]], writes=[NFSB])
                            self.stt(T3, UXS[:, ch, :, 1:9], w1c, T3, ALU.mult, ALU.add, reads=[UXSB, self.FTB[ti], self.PRMB], writes=[self.FTB[ti]])
                            self.stt(T3, UXS[:, ch, :, 0:8], w0c, T3, ALU.mult, ALU.add, reads=[UXSB, self.FTB[ti], self.PRMB], writes=[self.FTB[ti]])
                        t0.append(ti)
                    tg, tv = t0
                    G_ = self.FT[tg][:, :n]
                    self.act(G_, G_, AF.Silu, reads=[self.FTB[tg]], writes=[self.FTB[tg]])
                    self.tt(A[:, jj, c0:c0 + n], self.FT[tv][:, :n], G_, ALU.mult, reads=[self.FTB[tv], self.FTB[tg]], writes=[AB[jj][t]])
                if not last_g:
                    for gv in range(2):
                        self.cp(self.CARRY_F[:, L, chs[gv]], UX[:, ub, gv, TP:TP + 2], reads=[UXB[ub][2]], writes=[self.CFB[L]])
            dn = []
            for oc2 in range(4):
                dn.append(self.slab(lambda d, L=L, hf=hf, oc2=oc2: [
                    (0, [NJ, 256], BF16, d["f_w_down"][L][NJ * 128 * hf:NJ * 128 * hf + NJ * 128, :].rearrange("(jj p) n -> p jj n", p=128)[:, :, 256 * oc2:256 * oc2 + 256], "pool")],
                    first=(hf == 0 or oc2 == 0)))
                if dry or hf == 1:
                    continue
                self._down(dn[oc2], oc2, list(range(NT)), hf, A, AB, NJ)
            if not dry and hf == 1:
                for t in range(NT):
                    for oc2 in range(4):
                        self._down(dn[oc2], oc2, [t], hf, A, AB, NJ)
        if not dry:
            if last_g:
                fw.dma("sp", lambda h: h.dma_start(out=d["nfp"][L].rearrange("(c p) k -> p c k", p=128), in_=NF), reads=[NFB])
            fw.dma("sp", lambda h: h.dma_start(out=d["nfs"][L, g], in_=NFS), reads=[NFSB])
        self.postnorm(L, 5)

    def _down(self, sh, oc2, tiles, hf, A, AB, NJ):
        s, sb = sh
        w = self.slot(s, 0, [NJ, 256], BF16)
        for t in tiles:
            c0, n = TILES[t]
            for cc in range(2):
                c = 2 * oc2 + cc
                bank = self.mm("mm", n, [(w[:, jj, 128 * cc:128 * cc + 128], A[:, jj, c0:c0 + n]) for jj in range(NJ)],
                               [AB[jj][t] for jj in range(NJ)] + [sb])
                o = self.CO[:, c, c0:c0 + n]
                if hf == 0:
                    self.act(o, self.PS[bank][:, :n], AF.Copy, reads=[self.PSB[bank]], writes=[self.COB[c][t]])
                else:
                    self.tt(o, o, self.PS[bank][:, :n], ALU.add, reads=[self.PSB[bank], self.COB[c][t]], writes=[self.COB[c][t]])

    def build(self, nstage=None, ngroups=G):
        self.init_consts()
        k = 0
        for g in range(ngroups):
            self.load_x(g)
            for L in range(DEPTH):
                for st in range(3):
                    if nstage is not None and k >= nstage:
                        continue
                    k += 1
                    self.marks.append(("g%d L%d %s" % (g, L, ("mix", "cross", "ffn")[st]), self.nmm))
                    if st == 0:
                        if L % 2 == 0:
                            self.a_mixer(L, g)
                        else:
                            self.b_mixer(L, g)
                    elif st == 1:
                        self.cross(L, g)
                    else:
                        self.ffn(L, g)
            self.boundary(None)
            self.store_x(g)
        self.marks.append(("end", self.nmm))
        if not self.dry:
            import os as _os2, json as _json
            if _os2.environ.get("KMARKS"):
                _json.dump(self.marks, open(_os2.environ["KMARKS"], "w"))
            self.fw.run()


def _declare(nc):
    d = {}

    def inp(name, shape):
        d[name] = nc.dram_tensor(name, list(shape), F32, kind="ExternalInput").ap()

    def outp(name, shape):
        d[name] = nc.dram_tensor(name, list(shape), F32, kind="ExternalOutput").ap()

    inp("xT", [D, SEQ + NSEQ * DSEQ])
    inp("memT", [D, NMEM])
    inp("sconv", [2, G, 128, NCH * SG_ * 30])
    inp("spool", [2, G, 128, NCH * SG_ * 15])
    inp("sffn", [DEPTH, G, 128, 44 * SG_ * 2])
    inp("kT", [DEPTH, NSEQ, D, NMEM])
    inp("v", [DEPTH, NSEQ, NMEM, D])
    inp("prm", [128, NPRM])
    inp("ident", [128, 128])
    inp("a_w_in", [2, D, 2 * D])
    inp("a_w_out", [2, D, D])
    inp("p_w_group", [2, 4, 256, 256])
    inp("c_w_q", [DEPTH, D, D])
    inp("c_w_kv", [DEPTH, D, 2 * D])
    inp("c_w_o", [DEPTH, D, D])
    inp("f_w_up", [DEPTH, D, F2])
    inp("f_w_down", [DEPTH, DFF, D])
    d["sconv4"] = d["sconv"].rearrange("j g p (c s k) -> j g p c s k", c=NCH, s=SG_)
    outp("yT", [D, SEQ + NSEQ * DSEQ])
    outp("ncp", [2, D, 30])
    outp("npp", [2, D, 15])
    outp("nfp", [DEPTH, F2, 2])
    outp("mk", [DEPTH, NMEM, D])
    outp("mv", [DEPTH, NMEM, D])
    outp("ncs_", [2, G, 128, NCH * SG_ * 30])
    outp("nps_", [2, G, 128, NCH * SG_ * 15])
    outp("nfs_", [DEPTH, G, 128, 44 * SG_ * 2])
    d["ncs"] = d["ncs_"].rearrange("j g p (c s k) -> j g p c s k", c=NCH, s=SG_)
    d["nps"] = d["nps_"].rearrange("j g p (c s k) -> j g p c s k", c=NCH, s=SG_)
    d["nfs"] = d["nfs_"].rearrange("l g p (c s k) -> l g p c s k", c=44, s=SG_)
    return d


_BF = None


def _bf16_dtype():
    import ml_dtypes
    return ml_dtypes.bfloat16


def build_program(nstage=None, ngroups=G):
    nc = bass.Bass("TRN2", target_bir_lowering=False)
    d = _declare(nc)
    log = []
    Builder(nc, d, True, log).build(nstage, ngroups)
    Builder(nc, d, False, log).build(nstage, ngroups)
    return nc


def _fm(vec):
    v = np.asarray(vec, np.float32)
    sh = v.shape
    v = v.reshape(sh[:-1] + (sh[-1] // 128, 128))
    return np.moveaxis(v, -1, 0)


def _pack_params(inp):
    P = np.zeros((128, NPRM), np.float32)
    P[:, 0:224] = _fm(inp["norm_gains"]).reshape(128, 224)
    P[:, PC_BIN:PC_BIN + 32] = _fm(inp["a_b_in"]).reshape(128, 32)
    for j in range(2):
        for o, nm in ((0, "a_b_dw"), (8, "a_ln_g"), (16, "a_ln_b"), (24, "a_b_out")):
            P[:, PC_A + j * 32 + o:PC_A + j * 32 + o + 8] = _fm(inp[nm][j])
    P[:, PC_PS:PC_PS + 16] = _fm(inp["p_scale"]).reshape(128, 16)
    P[:, PC_FB:PC_FB + 176] = _fm(inp["f_b_dw"]).reshape(128, 176)
    P[:, PC_FW:PC_FW + 528] = _fm(inp["f_w_dw"]).reshape(128, 528)
    aw = _fm(inp["a_w_dw"])
    P[:, PC_AW:PC_AW + 496] = np.transpose(aw, (0, 1, 3, 2)).reshape(128, 496)
    return P


def _state_fm(st, nch):
    J, S, K, F = st.shape
    v = st.reshape(J, G, SG_, K, nch, 128)
    v = np.transpose(v, (0, 1, 5, 4, 2, 3))
    return np.ascontiguousarray(v).reshape(J, G, 128, nch * SG_ * K)


def _state_back(dev, nch, K):
    J = dev.shape[0]
    v = dev.reshape(J, G, 128, nch, SG_, K)
    v = np.transpose(v, (0, 1, 4, 5, 3, 2))
    return v.reshape(J, G * SG_, K, nch * 128)


_PROG = None


def kernel(_ncore=8, _nstage=None, _ngroups=G, **inputs):
    global _PROG
    inp = {k: np.asarray(v) for k, v in inputs.items()}
    ncore = _ncore
    if _PROG is None:
        _PROG = build_program(_nstage, _ngroups)
    nc = _PROG
    prm = _pack_params(inp)
    ident = np.eye(128, dtype=np.float32)
    shared = {k: np.ascontiguousarray(inp[k], dtype=np.float32) for k in
              ("a_w_in", "a_w_out", "p_w_group", "c_w_q", "c_w_kv", "c_w_o", "f_w_up", "f_w_down")}
    in_maps = []
    import os as _os
    _same = int(_os.environ.get("SAME_DATA", "0"))
    for i_ in range(ncore):
        i = 0 if _same else i_
        sl = slice(NSEQ * i, NSEQ * i + NSEQ)
        xs = inp["x_sample"][sl].reshape(NSEQ * DSEQ, D)
        xT = np.ascontiguousarray(np.concatenate([inp["x_prompt"][i], xs], axis=0).T)
        m = {
            "xT": xT,
            "memT": np.ascontiguousarray(inp["mem_prompt"][i].T),
            "sconv": _state_fm(inp["state_conv"][:, sl], NCH),
            "spool": _state_fm(inp["state_pool"][:, sl], NCH),
            "sffn": _state_fm(inp["state_ffn"][:, sl], 44),
            "kT": np.ascontiguousarray(np.transpose(inp["cache_mem_k"][:, sl].reshape(DEPTH, NSEQ, NMEM, D), (0, 1, 3, 2))),
            "v": np.ascontiguousarray(inp["cache_mem_v"][:, sl].reshape(DEPTH, NSEQ, NMEM, D)),
            "prm": prm,
            "ident": ident,
        }
        m.update(shared)
        in_maps.append(m)
    res = run_bass_kernel_spmd(nc, in_maps, core_ids=list(range(ncore)))
    R = res.results
    y_prompt = np.stack([R[i]["yT"][:, :SEQ].T for i in range(ncore)])
    y_sample = np.concatenate([R[i]["yT"][:, SEQ:].T.reshape(NSEQ, DSEQ, D) for i in range(ncore)])
    ncp = np.stack([np.transpose(R[i]["ncp"], (0, 2, 1)) for i in range(ncore)], axis=1)
    npp = np.stack([np.transpose(R[i]["npp"], (0, 2, 1)) for i in range(ncore)], axis=1)
    nfp = np.stack([np.transpose(R[i]["nfp"], (0, 2, 1)) for i in range(ncore)], axis=1)
    mk = np.stack([R[i]["mk"] for i in range(ncore)], axis=1).reshape(DEPTH, ncore, NMEM, NH, D // NH)
    mv = np.stack([R[i]["mv"] for i in range(ncore)], axis=1).reshape(DEPTH, ncore, NMEM, NH, D // NH)
    ncs = np.concatenate([_state_back(R[i]["ncs_"], NCH, 30) for i in range(ncore)], axis=1)
    nps = np.concatenate([_state_back(R[i]["nps_"], NCH, 15) for i in range(ncore)], axis=1)
    nfs = np.concatenate([_state_back(R[i]["nfs_"], 44, 2) for i in range(ncore)], axis=1)
    f = lambda a: np.ascontiguousarray(a, dtype=np.float32)
    return (f(y_prompt), f(y_sample), f(ncp), f(npp), f(nfp), f(mk), f(mv), f(ncs), f(nps), f(nfs))
```

```python
import contextlib
import numpy as np
import concourse.bass as bass
import concourse.mybir as mybir
from concourse.bass_utils import run_bass_kernel_spmd

F32 = mybir.dt.float32
BF16 = mybir.dt.bfloat16
AF = mybir.ActivationFunctionType
ALU = mybir.AluOpType

D = 1024
NCH = 8
SEQ = 2048
DEPTH = 4
NSEQ = 16
DSEQ = 8
NMEM = 256
NH = 4
DFF = 2816
F2 = 5632
NFC = 22
CW = 31
G = 2
TP = SEQ // G
SG_ = NSEQ // G
TS = SG_ * DSEQ
TG = TP + TS
TILES = [(0, 512), (512, 512), (1024, TS)]
NT = len(TILES)
ETILES = [(0, 512, (0,)), (512, TG - 512, (1, 2))]
ST = 2
RMS_EPS = 1e-6
LN_EPS = 1e-5
NSLOT = 4
SLOT_BYTES = 8192

def _pc_gain(i, k, c): return (i * 7 + k) * 8 + c
PC_BIN = 224
PC_A = 256
PC_PS = 320
PC_FB = 336
PC_FW = 512
PC_AW = 1040
NPRM = 1536


class Buf:
    __slots__ = ("name", "lw", "rd", "dsem", "dcnt")

    def __init__(self, name):
        self.name = name
        self.lw = None
        self.rd = {}
        self.dsem = None
        self.dcnt = 0


class Eng:
    def __init__(self, name):
        self.name = name
        self.ops = []
        self.count = 0
        self.seen = {}
        self.semkey = "E_" + name


class FW:
    def __init__(self, nc, dry=False):
        self.nc = nc
        self.dry = dry
        self.eng = {n: Eng(n) for n in ("pe", "act", "dve", "pool", "sp")}
        self.semkeys = [e.semkey for e in self.eng.values()]
        self.ndsem = 0
        self.all_dma = {}

    def _deps(self, e, reads, writes):
        need = {}
        for b in reads:
            if b.lw is not None:
                k, v = b.lw
                if need.get(k, 0) < v:
                    need[k] = v
        for b in writes:
            if b.lw is not None:
                k, v = b.lw
                if need.get(k, 0) < v:
                    need[k] = v
            for k, v in b.rd.items():
                if need.get(k, 0) < v:
                    need[k] = v
        waits = []
        for k, v in need.items():
            if k == e.semkey and e.name == "pe":
                continue
            if e.seen.get(k, 0) < v:
                e.seen[k] = v
                waits.append((k, v))
        return waits

    def op(self, engname, fn, reads=(), writes=()):
        if self.dry:
            return
        e = self.eng[engname]
        waits = self._deps(e, reads, writes)
        e.count += 1
        t = (e.semkey, e.count)
        e.ops.append((fn, waits, e.semkey, 1, True))
        for b in writes:
            b.lw = t
            b.rd = {}
        for b in reads:
            if b.rd.get(t[0], 0) < t[1]:
                b.rd[t[0]] = t[1]

    def dma(self, qname, fn, reads=(), writes=(), sem_of=None):
        if self.dry:
            return
        e = self.eng[qname]
        owner = sem_of if sem_of is not None else (writes[0] if writes else reads[0])
        if owner.dsem is None:
            owner.dsem = "D_%d" % self.ndsem
            self.ndsem += 1
            self.semkeys.append(owner.dsem)
        waits = self._deps(e, reads, writes)
        owner.dcnt += 16
        t = (owner.dsem, owner.dcnt)
        e.ops.append((fn, waits, owner.dsem, 16, False))
        self.all_dma[owner.dsem] = owner.dcnt
        for b in writes:
            b.lw = t
            b.rd = {}
        for b in reads:
            if b.rd.get(t[0], 0) < t[1]:
                b.rd[t[0]] = t[1]

    def fence(self, bufs_old, bufs_new):
        need = {}
        for b in bufs_old:
            if b.lw is not None:
                k, v = b.lw
                need[k] = max(need.get(k, 0), v)
            for k, v in b.rd.items():
                need[k] = max(need.get(k, 0), v)
        for b in bufs_new:
            b.lw = None
            b.rd = dict(need)

    def run(self):
        nc = self.nc
        with contextlib.ExitStack() as st:
            sems = {}
            for k in self.semkeys:
                sems[k] = st.enter_context(nc.semaphore(k))
            block = st.enter_context(nc.Block())
            fin = self.eng["sp"]
            handles = {"pe": "tensor", "act": "scalar", "dve": "vector", "pool": "gpsimd", "sp": "sync"}

            marked = {}
            for e in self.eng.values():
                for (fn, waits, isem, iamt, attach) in e.ops:
                    for (k, v) in waits:
                        if k.startswith("E_"):
                            marked.setdefault(k, set()).add(v)
            rank = {k: {v: i + 1 for i, v in enumerate(sorted(vs))} for k, vs in marked.items()}

            def wv(k, v):
                return rank[k][v] if k.startswith("E_") else v

            def make(e):
                def body(h):
                    ordinal = 0
                    for (fn, waits, isem, iamt, attach) in e.ops:
                        if attach and waits:
                            for (k, v) in waits[:-1]:
                                h.wait_ge(sems[k], wv(k, v))
                            r = fn(h)
                            first, last = r if isinstance(r, tuple) else (r, r)
                            k, v = waits[-1]
                            first._wait_ge(sems[k], wv(k, v))
                        else:
                            for (k, v) in waits:
                                h.wait_ge(sems[k], wv(k, v))
                            r = fn(h)
                            first, last = r if isinstance(r, tuple) else (r, r)
                        if isem.startswith("E_"):
                            ordinal += 1
                            if ordinal in rank.get(isem, ()):
                                last.then_inc(sems[isem], 1)
                        else:
                            last.then_inc(sems[isem], iamt)
                    if e is fin:
                        for k, v in self.all_dma.items():
                            h.wait_ge(sems[k], v)
                return body

            for name, e in self.eng.items():
                if not e.ops and e is not fin:
                    continue
                getattr(block, handles[name])(make(e))


class Arena:
    def __init__(self, nc, nbytes):
        self.nbytes = nbytes
        self.t = nc.alloc_sbuf_tensor("arena", [128, nbytes // 2], BF16)
        self.top = 0

    def at(self, off, shape, dt):
        n = int(np.prod(shape))
        esz = 4 if dt == F32 else 2
        assert off % 4 == 0 and off + n * esz <= self.nbytes, (off, shape, self.nbytes)
        v = self.t[:, off // 2: off // 2 + n * esz // 2]
        if dt == F32:
            v = v.bitcast(F32)
        if len(shape) == 2:
            v = v.rearrange("p (a b) -> p a b", a=shape[0])
        elif len(shape) == 3:
            v = v.rearrange("p (a b c) -> p a b c", a=shape[0], b=shape[1])
        elif len(shape) == 4:
            v = v.rearrange("p (a b c d) -> p a b c d", a=shape[0], b=shape[1], c=shape[2])
        return v

    def alloc(self, shape, dt):
        n = int(np.prod(shape)) * (4 if dt == F32 else 2)
        off = self.top
        self.top = (off + n + 63) // 64 * 64
        assert self.top <= self.nbytes, ("arena overflow", self.top, self.nbytes)
        return self.at(off, shape, dt), off


class Builder:
    def __init__(self, nc, dram, dry, slab_log):
        self.nc = nc
        self.d = dram
        self.fw = FW(nc, dry=dry)
        self.dry = dry
        self.slab_log = slab_log
        self.slab_idx = 0
        self.slab_loaded = 0
        self.pending_post = None
        self.post_stats_done = False
        self.nmm = 0
        self.marks = []
        self._alloc()

    def _alloc(self):
        nc = self.nc
        self.ps_cls = {"st": [0, 1], "mm": [2, 3, 4, 5], "aux": [6, 7], "mm6": [2, 3, 4, 5, 6, 7]}
        self.ps_i = {"st": 0, "mm": 0, "aux": 0, "mm6": 0}
        if self.dry:
            return
        AR = Arena(nc, 211200)
        self.AR = AR
        B = Buf
        self.X, _ = AR.alloc([NCH, TG], F32)
        self.XB = [[B("x") for _ in range(NT)] for _ in range(NCH)]
        self.PRM, _ = AR.alloc([NPRM], F32)
        self.PRMB = B("prm")
        self.IDENT, _ = AR.alloc([128], BF16)
        self.ONESM, _ = AR.alloc([128], BF16)
        self.ONES1, _ = AR.alloc([128], BF16)
        self.CONB = B("const")
        self.RC, _ = AR.alloc([16], F32)
        self.CARRY_A, _ = AR.alloc([2, NCH, 30], BF16)
        self.CARRY_P, _ = AR.alloc([2, NCH, 16], F32)
        self.CARRY_F, _ = AR.alloc([DEPTH, 44, 2], BF16)
        self.CAB = [B("ca") for _ in range(2)]
        self.CPB = [B("cp") for _ in range(2)]
        self.CFB = [B("cf") for _ in range(DEPTH)]
        self.KT, _ = AR.alloc([NCH, NMEM], BF16)
        self.VV, _ = AR.alloc([2, D], BF16)
        self.KTB = B("kt")
        self.VVB = B("vv")
        self.slot_off = []
        for s in range(NSLOT):
            _, off = AR.alloc([SLOT_BYTES // 2], BF16)
            self.slot_off.append(off)
        self.SLB = [B("slot%d" % s) for s in range(NSLOT)]
        self.HB, _ = AR.alloc([NCH, TG], BF16)
        self.HBB = [[B("h") for _ in range(NT)] for _ in range(NCH)]
        self.CO, _ = AR.alloc([NCH, TG], F32)
        self.COB = [[B("co") for _ in range(NT)] for _ in range(NCH)]
        self.MS, _ = AR.alloc([TG], F32)
        self.R, _ = AR.alloc([TG], F32)
        self.MEAN, self.mean_off = AR.alloc([TG], F32)
        self.XTRA, self.xtra_off = AR.alloc([TG], F32)
        self.MSB = [B("ms") for _ in range(NT)]
        self.RB = [B("r") for _ in range(NT)]
        self.MEANB = [B("mean") for _ in range(NT)]
        self.XTRAB = B("xtra")
        self.NSQ = 4
        self.SQ = [AR.alloc([TG - 512], BF16)[0] for _ in range(self.NSQ)]
        self.SQB = [B("sq") for _ in range(self.NSQ)]
        self.sq_i = 0
        self.NTMP = 2
        self.TMP = [AR.alloc([512], F32)[0] for _ in range(self.NTMP)]
        self.TMPB = [B("tmp") for _ in range(self.NTMP)]
        self.tmp_i = 0
        self.NSTG = 2
        self.STG = [AR.alloc([512], F32)[0] for _ in range(self.NSTG)]
        self.STGB = [B("stg") for _ in range(self.NSTG)]
        self.stg_i = 0
        xt = [self.AR.at(self.xtra_off + 2048 * i, [512], F32) for i in range(2)]
        self.FT = self.TMP + self.STG + xt
        self.FTB = self.TMPB + self.STGB + [B("xt0"), B("xt1")]
        self.ft_i = 0
        self.WSIZE = 43008
        _, self.w_off = AR.alloc([self.WSIZE // 2], BF16)
        self.WB_cur = []
        self.PS = [nc.alloc_psum_tensor("ps%d" % i, [128, 512], F32) for i in range(8)]
        self.PSB = [B("ps%d" % i) for i in range(8)]
        self.DRB = B("dram_passthru")

    def W(self, off, shape, dt):
        return self.AR.at(self.w_off + off, shape, dt)

    def new_view(self, bufs):
        if self.dry:
            return
        shared = [self.MEANB[t] for t in range(NT)] + [self.XTRAB, self.FTB[4], self.FTB[5]]
        self.fw.fence(self.WB_cur + shared, list(bufs) + shared)
        self.WB_cur = list(bufs)

    def ps(self, cls):
        lst = self.ps_cls[cls]
        i = lst[self.ps_i[cls] % len(lst)]
        self.ps_i[cls] += 1
        return i

    def sq(self):
        i = self.sq_i % self.NSQ
        self.sq_i += 1
        return i

    def tmp(self):
        i = self.tmp_i % self.NTMP
        self.tmp_i += 1
        return i

    def ft(self):
        i = self.ft_i % len(self.FT)
        self.ft_i += 1
        return i

    def stg(self):
        i = self.stg_i % self.NSTG
        self.stg_i += 1
        return i

    def slab(self, spec, first=True):
        if self.dry:
            self.slab_log.append(spec)
            return 0, None
        idx = self.slab_idx
        self.slab_idx += 1
        if first:
            self.slab_base = idx
        while self.slab_loaded < min(len(self.slab_log), self.slab_base + NSLOT):
            self._load_slab(self.slab_loaded)
            self.slab_loaded += 1
        return idx % NSLOT, self.SLB[idx % NSLOT]

    def _load_slab(self, i):
        spec = self.slab_log[i]
        s = i % NSLOT
        for (boff, shape, dt, src, q) in spec(self.d):
            dst = self.AR.at(self.slot_off[s] + boff, shape, dt)
            self.fw.dma(q, (lambda h, dst=dst, src=src: h.dma_start(out=dst, in_=src)), writes=[self.SLB[s]])

    def slot(self, s, boff, shape, dt):
        return self.AR.at(self.slot_off[s] + boff, shape, dt)

    def prm(self, col):
        return self.PRM[:, col:col + 1]

    def mm(self, cls_or_bank, n, terms, reads, col0=0):
        bank = cls_or_bank if isinstance(cls_or_bank, int) else self.ps(cls_or_bank)
        if self.dry:
            return bank
        out = self.PS[bank][:, col0:col0 + n]
        nt = len(terms)
        self.nmm += nt

        def fn(h):
            first = None
            ins = None
            for i, (l, r) in enumerate(terms):
                ins = h.matmul(out, l, r, start=(i == 0), stop=(i == nt - 1))
                if first is None:
                    first = ins
            return first, ins
        self.fw.op("pe", fn, reads=reads, writes=[self.PSB[bank]])
        return bank

    def act(self, out, in_, func, reads, writes, bias=None, scale=None):
        kw = {}
        if bias is not None:
            kw["bias"] = bias
        if scale is not None:
            kw["scale"] = scale
        self.fw.op("act", lambda h: h.activation(out=out, in_=in_, func=func, **kw), reads=reads, writes=writes)

    def stt(self, out, in0, scalar, in1, op0, op1, reads, writes):
        self.fw.op("dve", lambda h: h.scalar_tensor_tensor(out=out, in0=in0, scalar=scalar, in1=in1, op0=op0, op1=op1),
                   reads=reads, writes=writes)

    def tt(self, out, in0, in1, op, reads, writes):
        self.fw.op("dve", lambda h: h.tensor_tensor(out=out, in0=in0, in1=in1, op=op), reads=reads, writes=writes)

    def ttp(self, out, in0, in1, op, reads, writes):
        self.fw.op("pool", lambda h: h.tensor_tensor(out=out, in0=in0, in1=in1, op=op), reads=reads, writes=writes)

    def ts(self, out, in0, s1, op0, reads, writes):
        self.fw.op("dve", lambda h: h.tensor_scalar(out=out, in0=in0, scalar1=s1, scalar2=None, op0=op0), reads=reads, writes=writes)

    def cp(self, out, in_, reads, writes):
        self.fw.op("dve", lambda h: h.tensor_copy(out=out, in_=in_), reads=reads, writes=writes)

    def rms_stats(self, src3, srcbufs, e):
        e0, en, tl = ETILES[e]
        banks = {t: self.ps("st") for t in tl}
        for c in range(NCH):
            q = self.sq()
            self.act(self.SQ[q][:, :en], src3[:, c, e0:e0 + en], AF.Square, reads=[srcbufs[c][t] for t in tl], writes=[self.SQB[q]])
            for t in tl:
                c0, n = TILES[t]
                out = self.PS[banks[t]][:, :n]
                rhs = self.SQ[q][:, c0 - e0:c0 - e0 + n]
                self.nmm += 1
                self.fw.op("pe", (lambda h, out=out, rhs=rhs, c=c: h.matmul(out, self.ONESM, rhs, start=(c == 0), stop=(c == NCH - 1))),
                           reads=[self.SQB[q], self.CONB], writes=[self.PSB[banks[t]]])
        for t in tl:
            c0, n = TILES[t]
            self.ts(self.MS[:, c0:c0 + n], self.PS[banks[t]][:, :n], RMS_EPS, ALU.add, reads=[self.PSB[banks[t]]], writes=[self.MSB[t]])
        self.ln_exp_rstd(e0, en, tl)

    def ln_exp_rstd(self, e0, en, tl):
        self.act(self.R[:, e0:e0 + en], self.MS[:, e0:e0 + en], AF.Ln, reads=[self.MSB[t] for t in tl], writes=[self.RB[t] for t in tl])
        self.act(self.R[:, e0:e0 + en], self.R[:, e0:e0 + en], AF.Exp, reads=[self.RB[t] for t in tl], writes=[self.RB[t] for t in tl], scale=-0.5)

    def tail_begin(self):
        self._tail = {"pend": None, "banks": {}}

    def tail_chunk(self, t, c, src, reads, **kw):
        self._tail_flush()
        q = self.sq()
        n = TILES[t][1]
        self.act(self.SQ[q][:, :n], src, AF.Square, reads=reads, writes=[self.SQB[q]], **kw)
        self._tail["pend"] = (t, c, q)

    def _tail_flush(self):
        p = self._tail["pend"]
        if p is None:
            return
        t, c, q = p
        c0, n = TILES[t]
        if t not in self._tail["banks"]:
            self._tail["banks"][t] = self.ps("st")
        bank = self._tail["banks"][t]
        out = self.PS[bank][:, :n]
        rhs = self.SQ[q][:, :n]
        self.nmm += 1
        self.fw.op("pe", (lambda h, out=out, rhs=rhs, c=c: h.matmul(out, self.ONESM, rhs, start=(c == 0), stop=(c == NCH - 1))),
                   reads=[self.SQB[q], self.CONB], writes=[self.PSB[bank]])
        if c == NCH - 1:
            self.ts(self.MS[:, c0:c0 + n], self.PS[bank][:, :n], RMS_EPS, ALU.add, reads=[self.PSB[bank]], writes=[self.MSB[t]])
            for (e0, en, tl) in ETILES:
                if tl[-1] == t:
                    self.ln_exp_rstd(e0, en, tl)
        self._tail["pend"] = None

    def tail_end(self):
        self._tail_flush()
        self.post_stats_done = True

    def post_e(self, L, k, e):
        e0, en, tl = ETILES[e]
        if not self.post_stats_done:
            self.rms_stats(self.CO, self.COB, e)
        for c in range(NCH):
            o = self.CO[:, c, e0:e0 + en]
            cb = [self.COB[c][t] for t in tl]
            xb = [self.XB[c][t] for t in tl]
            self.stt(o, o, self.prm(_pc_gain(L, k, c)), self.R[:, e0:e0 + en], ALU.mult, ALU.mult,
                     reads=cb + [self.RB[t] for t in tl] + [self.PRMB], writes=cb)
            x = self.X[:, c, e0:e0 + en]
            self.tt(x, x, o, ALU.add, reads=xb + cb, writes=xb)

    def pre_e_hb(self, L, k, e):
        e0, en, tl = ETILES[e]
        self.rms_stats(self.X, self.XB, e)
        for c in range(NCH):
            self.stt(self.HB[:, c, e0:e0 + en], self.X[:, c, e0:e0 + en], self.prm(_pc_gain(L, k, c)), self.R[:, e0:e0 + en],
                     ALU.mult, ALU.mult, reads=[self.XB[c][t] for t in tl] + [self.RB[t] for t in tl] + [self.PRMB],
                     writes=[self.HBB[c][t] for t in tl])

    def boundary(self, pre_fn):
        if self.dry:
            return
        pend = self.pending_post
        self.pending_post = None
        for e in range(len(ETILES)):
            if pend is not None:
                self.post_e(pend[0], pend[1], e)
            if pre_fn is not None:
                pre_fn(e)
        self.post_stats_done = False

    def postnorm(self, L, k):
        self.pending_post = (L, k)

    def hb(self, c, t):
        c0, n = TILES[t]
        return self.HB[:, c, c0:c0 + n]

    def init_consts(self):
        if self.dry:
            return
        fw = self.fw
        d = self.d
        fw.dma("sp", lambda h: h.dma_start(out=self.PRM, in_=d["prm"][:, :]), writes=[self.PRMB])
        fw.dma("pool", lambda h: h.dma_start(out=self.IDENT, in_=d["ident"][:, :]), writes=[self.CONB])
        fw.op("pool", lambda h: h.memset(self.ONESM, 1.0 / D), writes=[self.CONB])
        fw.op("pool", lambda h: h.memset(self.ONES1, 1.0), writes=[self.CONB])
        for i in range(16):
            fw.op("pool", (lambda h, i=i: h.memset(self.RC[:, i:i + 1], 1.0 / (i + 1))), writes=[self.CONB])
        for j in range(2):
            fw.op("pool", (lambda h, j=j: h.memset(self.CARRY_A[:, j], 0.0)), writes=[self.CAB[j]])
            fw.op("pool", (lambda h, j=j: h.memset(self.CARRY_P[:, j], 0.0)), writes=[self.CPB[j]])
        for i in range(DEPTH):
            fw.op("pool", (lambda h, i=i: h.memset(self.CARRY_F[:, i], 0.0)), writes=[self.CFB[i]])

    def load_x(self, g):
        if self.dry:
            return
        xT = self.d["xT"].rearrange("(c p) t -> p c t", p=128)
        allx = [self.XB[c][t] for c in range(NCH) for t in range(NT)]
        self.fw.dma("sp", lambda h: h.dma_start(out=self.X[:, :, 0:TP], in_=xT[:, :, TP * g:TP * g + TP]),
                    writes=allx, sem_of=self.XB[0][0])
        self.fw.dma("sp", lambda h: h.dma_start(out=self.X[:, :, TP:TG], in_=xT[:, :, SEQ + TS * g:SEQ + TS * g + TS]),
                    writes=allx, sem_of=self.XB[0][0])

    def store_x(self, g):
        if self.dry:
            return
        yT = self.d["yT"].rearrange("(c p) t -> p c t", p=128)
        allx = [self.XB[c][t] for c in range(NCH) for t in range(NT)]
        self.fw.dma("sp", lambda h: h.dma_start(out=yT[:, :, TP * g:TP * g + TP], in_=self.X[:, :, 0:TP]),
                    reads=allx, sem_of=self.XB[0][0])
        self.fw.dma("sp", lambda h: h.dma_start(out=yT[:, :, SEQ + TS * g:SEQ + TS * g + TS], in_=self.X[:, :, TP:TG]),
                    reads=allx, sem_of=self.XB[0][0])

    def a_mixer(self, L, g):
        j = L // 2
        fw = self.fw
        d = self.d
        dry = self.dry
        B = Buf
        if not dry:
            self.boundary(lambda e: self.pre_e_hb(L, 0, e))
            GLUX = self.W(0, [NCH, TP + 30], BF16)
            GLUS = self.W(16896, [NCH, SG_, 38], BF16)
            DG = self.W(21760, [2, CW, 128], BF16)
            GLUF = self.W(37632, [NCH, 30], F32)
            GLUSF = self.W(38592, [NCH, SG_, DSEQ], F32)
            GXH = [B("gxh") for _ in range(NCH)]
            GXB = [[B("gx") for _ in range(2)] for _ in range(NCH)]
            GSB = [B("gs") for _ in range(NCH)]
            DGB = [B("dg") for _ in range(2)]
            GFB = B("gluf")
            GSFB = B("glusf")
            self.new_view([b for l in GXB for b in l] + GXH + GSB + DGB + [GFB, GSFB])
            self.cp(GLUX[:, :, 0:30], self.CARRY_A[:, j], reads=[self.CAB[j]], writes=GXH)
        s, sb = self.slab(lambda d, j=j, g=g: [(0, [NCH * SG_ * 30], BF16, d["sconv"][j, g], "pool")])
        if not dry:
            src = self.slot(s, 0, [NCH, SG_, 30], BF16)
            self.cp(GLUS[:, :, :, 0:30], src, reads=[sb], writes=GSB)
            for c in range(NCH):
                fw.dma("sp", (lambda h, c=c: h.dma_start(out=d["ncs"][j, g, :, c, :, 0:22], in_=d["sconv4"][j, g, :, c, :, 8:30])),
                       sem_of=self.DRB)
        last_g = (g == G - 1)
        for sl in range(4):
            s, sb = self.slab(lambda d, j=j, sl=sl: [
                (0, [NCH, 256], BF16, d["a_w_in"][j].rearrange("(kc p) n -> p kc n", p=128)[:, :, 256 * sl:256 * sl + 256], "pool"),
                (NCH * 256 * 2, [NCH, 256], BF16, d["a_w_in"][j].rearrange("(kc p) n -> p kc n", p=128)[:, :, D + 256 * sl:D + 256 * sl + 256], "pool")])
            if dry:
                continue
            wa = self.slot(s, 0, [NCH, 256], BF16)
            wg = self.slot(s, NCH * 256 * 2, [NCH, 256], BF16)
            for cc in range(2):
                c = 2 * sl + cc
                banks = []
                for t, (c0, n) in enumerate(TILES):
                    hr = [self.HBB[kc][t] for kc in range(NCH)] + [sb]
                    ba = self.mm("mm", n, [(wa[:, kc, 128 * cc:128 * cc + 128], self.hb(kc, t)) for kc in range(NCH)], hr)
                    bg = self.mm("mm", n, [(wg[:, kc, 128 * cc:128 * cc + 128], self.hb(kc, t)) for kc in range(NCH)], hr)
                    if t == 1:
                        self._glu_tile(j, c, 0, banks[0][0], banks[0][1], GLUX, GLUS, GLUF, GLUSF, GXB, GSB, GFB, GSFB, last_g)
                    banks.append((ba, bg))
                self._glu_tile(j, c, 1, banks[1][0], banks[1][1], GLUX, GLUS, GLUF, GLUSF, GXB, GSB, GFB, GSFB, last_g)
                self._glu_tile(j, c, 2, banks[2][0], banks[2][1], GLUX, GLUS, GLUF, GLUSF, GXB, GSB, GFB, GSFB, last_g)
        if not dry:
            if last_g:
                fw.dma("sp", lambda h: h.dma_start(out=d["ncp"][j].rearrange("(c p) k -> p c k", p=128), in_=GLUF), reads=[GFB])
            fw.dma("sp", lambda h: h.dma_start(out=d["ncs"][j, g, :, :, :, 22:30], in_=GLUSF), reads=[GSFB])
            for c in range(NCH):
                db = c % 2
                for k in range(CW):
                    self.ts(DG[:, db, k], self.IDENT, self.prm(PC_AW + j * 248 + c * CW + k), ALU.mult,
                            reads=[self.CONB, self.PRMB], writes=[DGB[db]])
                for t, (c0, n) in enumerate(TILES):
                    if t < ST:
                        terms = [(DG[:, db, k], GLUX[:, c, c0 + k:c0 + k + n]) for k in range(CW)]
                        rd = [DGB[db], GXH[c], GXB[c][0]] + ([GXB[c][1]] if t == 1 else [])
                    else:
                        terms = [(DG[:, db, k], GLUS[:, c, :, k:k + DSEQ]) for k in range(CW)]
                        rd = [DGB[db], GSB[c]]
                    bank = self.mm("aux", n, terms, rd)
                    self.act(self.CO[:, c, c0:c0 + n], self.PS[bank][:, :n], AF.Identity, reads=[self.PSB[bank], self.PRMB],
                             writes=[self.COB[c][t]], bias=self.prm(PC_A + j * 32 + 0 + c))
            if not last_g:
                self.cp(self.CARRY_A[:, j], GLUX[:, :, TP:TP + 30], reads=[GXB[c][1] for c in range(NCH)], writes=[self.CAB[j]])
            for e, (e0, en, tl) in enumerate(ETILES):
                bms = {}
                bqs = {}
                for i_, t in enumerate(tl):
                    cls = "st" if i_ == 0 else "aux"
                    bms[t] = self.ps(cls)
                    bqs[t] = self.ps(cls)
                for c in range(NCH):
                    src = self.CO[:, c, e0:e0 + en]
                    cb = [self.COB[c][t] for t in tl]
                    q1 = self.sq()
                    self.act(self.SQ[q1][:, :en], src, AF.Copy, reads=cb, writes=[self.SQB[q1]])
                    q2 = self.sq()
                    self.act(self.SQ[q2][:, :en], src, AF.Square, reads=cb, writes=[self.SQB[q2]])
                    for t in tl:
                        c0, n = TILES[t]
                        for (bk, q) in ((bms[t], q1), (bqs[t], q2)):
                            out = self.PS[bk][:, :n]
                            rhs = self.SQ[q][:, c0 - e0:c0 - e0 + n]
                            self.nmm += 1
                            fw.op("pe", (lambda h, out=out, rhs=rhs, c=c: h.matmul(out, self.ONESM, rhs, start=(c == 0), stop=(c == NCH - 1))),
                                  reads=[self.SQB[q], self.CONB], writes=[self.PSB[bk]])
                for t in tl:
                    c0, n = TILES[t]
                    mean = self.MEAN[:, c0:c0 + n]
                    ms = self.MS[:, c0:c0 + n]
                    self.cp(mean, self.PS[bms[t]][:, :n], reads=[self.PSB[bms[t]]], writes=[self.MEANB[t]])
                    self.stt(ms, mean, -1.0, mean, ALU.mult, ALU.mult, reads=[self.MEANB[t]], writes=[self.MSB[t]])
                    self.stt(ms, ms, LN_EPS, self.PS[bqs[t]][:, :n], ALU.add, ALU.add, reads=[self.MSB[t], self.PSB[bqs[t]]], writes=[self.MSB[t]])
                self.ln_exp_rstd(e0, en, tl)
                for c in range(NCH):
                    cc_ = self.CO[:, c, e0:e0 + en]
                    cb = [self.COB[c][t] for t in tl]
                    self.tt(cc_, cc_, self.MEAN[:, e0:e0 + en], ALU.subtract, reads=cb + [self.MEANB[t] for t in tl], writes=cb)
                    self.tt(cc_, cc_, self.R[:, e0:e0 + en], ALU.mult, reads=cb + [self.RB[t] for t in tl], writes=cb)
                    self.act(self.HB[:, c, e0:e0 + en], cc_, AF.Silu, reads=cb + [self.PRMB], writes=[self.HBB[c][t] for t in tl],
                             scale=self.prm(PC_A + j * 32 + 8 + c), bias=self.prm(PC_A + j * 32 + 16 + c))
        sl_h = []
        for sl in range(2):
            sl_h.append(self.slab(lambda d, j=j, sl=sl: [
                (0, [NCH, 512], BF16, d["a_w_out"][j].rearrange("(kc p) n -> p kc n", p=128)[:, :, 512 * sl:512 * sl + 512], "pool")], first=(sl == 0)))
        if not dry:
            self.tail_begin()
            for t, (c0, n) in enumerate(TILES):
                for sl in range(2):
                    s, sb = sl_h[sl]
                    w = self.slot(s, 0, [NCH, 512], BF16)
                    for cc in range(4):
                        c = 4 * sl + cc
                        bank = self.mm("mm", n, [(w[:, kc, 128 * cc:128 * cc + 128], self.hb(kc, t)) for kc in range(NCH)],
                                       [self.HBB[kc][t] for kc in range(NCH)] + [sb])
                        self.act(self.CO[:, c, c0:c0 + n], self.PS[bank][:, :n], AF.Identity, reads=[self.PSB[bank], self.PRMB],
                                 writes=[self.COB[c][t]], bias=self.prm(PC_A + j * 32 + 24 + c))
                        self.tail_chunk(t, c, self.PS[bank][:, :n], [self.PSB[bank], self.PRMB], bias=self.prm(PC_A + j * 32 + 24 + c))
            self.tail_end()
        self.postnorm(L, 1)

    def _glu_tile(self, j, c, t, ba, bg, GLUX, GLUS, GLUF, GLUSF, GXB, GSB, GFB, GSFB, last_g):
        c0, n = TILES[t]
        k = self.tmp()
        sg = self.TMP[k][:, :n]
        self.act(sg, self.PS[bg][:, :n], AF.Sigmoid, reads=[self.PSB[bg], self.PRMB], writes=[self.TMPB[k]],
                 bias=self.prm(PC_BIN + j * 16 + 8 + c))
        ba_col = self.prm(PC_BIN + j * 16 + c)
        if t < ST:
            self.stt(GLUX[:, c, 30 + c0:30 + c0 + n], self.PS[ba][:, :n], ba_col, sg, ALU.add, ALU.mult,
                     reads=[self.PSB[ba], self.TMPB[k], self.PRMB], writes=[GXB[c][t]])
            if last_g and t == ST - 1:
                self.stt(GLUF[:, c, :], self.PS[ba][:, n - 30:n], ba_col, sg[:, n - 30:n], ALU.add, ALU.mult,
                         reads=[self.PSB[ba], self.TMPB[k], self.PRMB], writes=[GFB])
        else:
            pa = self.PS[ba][:, :n].rearrange("p (s l) -> p s l", l=DSEQ)
            sg3 = sg.rearrange("p (s l) -> p s l", l=DSEQ)
            self.stt(GLUS[:, c, :, 30:38], pa, ba_col, sg3, ALU.add, ALU.mult,
                     reads=[self.PSB[ba], self.TMPB[k], self.PRMB], writes=[GSB[c]])
            self.stt(GLUSF[:, c], pa, ba_col, sg3, ALU.add, ALU.mult,
                     reads=[self.PSB[ba], self.TMPB[k], self.PRMB], writes=[GSFB])

    def b_mixer(self, L, g):
        j = L // 2
        fw = self.fw
        d = self.d
        dry = self.dry
        B = Buf
        XW = TP + 16
        if not dry:
            HFX = self.W(0, [NCH, XW], F32)
            HFS = self.W(33280, [NCH, SG_, 24], F32)
            T1S = self.W(39424, [SG_, 24], F32)
            T2S = self.W(40192, [SG_, 24], F32)
            T1 = self.AR.at(self.mean_off, [XW], F32)
            T2 = self.AR.at(self.xtra_off, [XW], F32)
            HXH = [B("hxh") for _ in range(NCH)]
            HXB = [[B("hx") for _ in range(2)] for _ in range(NCH)]
            HSB = [B("hs") for _ in range(NCH)]
            TB = B("t12")
            TSB = B("t12s")
            self.new_view([b for l in HXB for b in l] + HXH + HSB + [TB, TSB])
            self.cp(HFX[:, :, 0:16], self.CARRY_P[:, j], reads=[self.CPB[j]], writes=HXH)
        s, sb = self.slab(lambda d, j=j, g=g: [(0, [NCH * SG_ * 15], F32, d["spool"][j, g], "pool")])
        if not dry:
            src = self.slot(s, 0, [NCH, SG_, 15], F32)
            fw.op("dve", lambda h: h.memset(HFS[:, :, :, 0:1], 0.0), writes=HSB)
            self.cp(HFS[:, :, :, 1:16], src, reads=[sb], writes=HSB)

            def dst_of(c, t):
                c0, n = TILES[t]
                if t < ST:
                    return HFX[:, c, 16 + c0:16 + c0 + n]
                return HFS[:, c, :, 16:24]

            def dstb(c, t):
                return [HXB[c][t]] if t < ST else [HSB[c]]
            def pre_b(e):
                e0, en, tl = ETILES[e]
                self.rms_stats(self.X, self.XB, e)
                for t in tl:
                    c0, n = TILES[t]
                    for c in range(NCH):
                        x = self.X[:, c, c0:c0 + n]
                        r = self.R[:, c0:c0 + n]
                        if t == ST:
                            x = x.rearrange("p (s l) -> p s l", l=DSEQ)
                            r = r.rearrange("p (s l) -> p s l", l=DSEQ)
                        self.stt(dst_of(c, t), x, self.prm(_pc_gain(L, 0, c)), r, ALU.mult, ALU.mult,
                                 reads=[self.XB[c][t], self.RB[t], self.PRMB], writes=dstb(c, t))
            self.boundary(pre_b)
            last_g = (g == G - 1)
            if last_g:
                fw.dma("sp", lambda h: h.dma_start(out=d["npp"][j].rearrange("(c p) k -> p c k", p=128), in_=HFX[:, :, XW - 15:XW]),
                       reads=[HXB[c][1] for c in range(NCH)], sem_of=HXB[0][1])
            else:
                self.cp(self.CARRY_P[:, j], HFX[:, :, XW - 16:XW], reads=[HXB[c][1] for c in range(NCH)], writes=[self.CPB[j]])
            fw.dma("sp", lambda h: h.dma_start(out=d["nps"][j, g], in_=HFS[:, :, :, 9:24]), reads=HSB, sem_of=HSB[0])
            for c in range(NCH):
                lw = c // 2 + 1
                w = 1 << lw
                z = HFX[:, c, :]
                zs = HFS[:, c]
                rdz = [HXH[c], HXB[c][0], HXB[c][1]]
                cur, curs = z, zs
                bufs = [T1, T2]
                bufss = [T1S, T2S]
                for st in range(lw):
                    sh = 1 << st
                    o = bufs[st % 2]
                    os_ = bufss[st % 2]
                    lo = 2 * sh - 1
                    self.tt(o[:, lo:XW], cur[:, lo:XW], cur[:, lo - sh:XW - sh], ALU.add, reads=rdz + [TB], writes=[TB])
                    self.tt(os_[:, :, lo:24], curs[:, :, lo:24], curs[:, :, lo - sh:24 - sh], ALU.add, reads=[HSB[c], TSB], writes=[TSB])
                    cur, curs = o, os_
                inv = 1.0 / w
                self.stt(self.HB[:, c, 0:TP], cur[:, 16:XW], inv, z[:, 16:XW], ALU.mult, ALU.subtract,
                         reads=rdz + [TB], writes=[self.HBB[c][0], self.HBB[c][1]])
                if g == 0:
                    k = self.tmp()
                    tm = self.TMP[k][:, 0:w - 1]
                    self.tt(tm, cur[:, 16:16 + w - 1], self.RC[:, 0:w - 1], ALU.mult, reads=[TB, self.CONB], writes=[self.TMPB[k]])
                    self.tt(self.HB[:, c, 0:w - 1], tm, z[:, 16:16 + w - 1], ALU.subtract, reads=[self.TMPB[k]] + rdz, writes=[self.HBB[c][0]])
                self.stt(self.HB[:, c, TP:TG].rearrange("p (s l) -> p s l", l=DSEQ), curs[:, :, 16:24], inv, zs[:, :, 16:24],
                         ALU.mult, ALU.subtract, reads=[HSB[c], TSB], writes=[self.HBB[c][ST]])
        s, sb = self.slab(lambda d, j=j: [(0, [NCH, 256], BF16, d["p_w_group"][j].rearrange("g (kc p) n -> p (g kc) n", p=128), "pool")])
        if not dry:
            w = self.slot(s, 0, [NCH, 256], BF16)
            self.tail_begin()
            for t, (c0, n) in enumerate(TILES):
                for oc in range(NCH):
                    gi, oh = oc // 2, oc % 2
                    bank = self.mm("mm", n, [(w[:, 2 * gi + kc, 128 * oh:128 * oh + 128], self.hb(2 * gi + kc, t)) for kc in range(2)],
                                   [self.HBB[2 * gi][t], self.HBB[2 * gi + 1][t], sb])
                    self.act(self.CO[:, oc, c0:c0 + n], self.PS[bank][:, :n], AF.Copy, reads=[self.PSB[bank], self.PRMB],
                             writes=[self.COB[oc][t]], scale=self.prm(PC_PS + j * 8 + oc))
                    self.tail_chunk(t, oc, self.PS[bank][:, :n], [self.PSB[bank], self.PRMB], scale=self.prm(PC_PS + j * 8 + oc))
            self.tail_end()
        self.postnorm(L, 1)

    def cross(self, L, g):
        fw = self.fw
        d = self.d
        dry = self.dry
        B = Buf
        SC = 1.0 / 16.0
        if not dry:
            Q = self.W(0, [NCH, TG], BF16)
            E = self.W(17408, [NH, 2, 512], BF16)
            RS = self.W(25600, [NH, 512], F32)
            ES = self.W(33792, [2, NH, TS], BF16)
            RSS = self.W(34816, [NH, TS], F32)
            MEMN = self.W(35840, [NCH, NMEM], BF16)
            MSM = self.W(39936, [NMEM], F32)
            QB = [[B("q") for _ in range(NT)] for _ in range(NCH)]
            EB = [B("e") for _ in range(NH)]
            RSB = [B("rs") for _ in range(NH)]
            ESB = B("es")
            RSSB = B("rss")
            MNB = B("memn")
            MSMB = B("msm")
            self.new_view([b for l in QB for b in l] + EB + RSB + [ESB, RSSB, MNB, MSMB])
        s, sb = self.slab(lambda d: [(0, [NCH, NMEM], F32, d["memT"].rearrange("(c p) m -> p c m", p=128), "pool")])
        if not dry:
            MEMT = self.slot(s, 0, [NCH, NMEM], F32)
            bank = self.ps("st")
            for c in range(NCH):
                q = self.sq()
                self.act(self.SQ[q][:, :NMEM], MEMT[:, c], AF.Square, reads=[sb], writes=[self.SQB[q]])
                self.nmm += 1
                fw.op("pe", (lambda h, bank=bank, q=q, c=c: h.matmul(self.PS[bank][:, :NMEM], self.ONESM, self.SQ[q][:, :NMEM], start=(c == 0), stop=(c == NCH - 1))),
                      reads=[self.SQB[q], self.CONB], writes=[self.PSB[bank]])
            self.ts(MSM, self.PS[bank][:, :NMEM], RMS_EPS, ALU.add, reads=[self.PSB[bank]], writes=[MSMB])
            self.act(MSM, MSM, AF.Ln, reads=[MSMB], writes=[MSMB])
            self.act(MSM, MSM, AF.Exp, reads=[MSMB], writes=[MSMB], scale=-0.5)
            for c in range(NCH):
                self.stt(MEMN[:, c], MEMT[:, c], self.prm(_pc_gain(L, 6, c)), MSM, ALU.mult, ALU.mult, reads=[sb, MSMB, self.PRMB], writes=[MNB])
        import os
        CC = int(os.environ.get("CROSS_CUT", "99"))
        for sl in range(4):
            if CC <= 1:
                break
            s, sb = self.slab(lambda d, L=L, sl=sl: [
                (0, [NCH, 512], BF16, d["c_w_kv"][L].rearrange("(kc p) n -> p kc n", p=128)[:, :, 512 * sl:512 * sl + 512], "pool")])
            if dry:
                continue
            w = self.slot(s, 0, [NCH, 512], BF16)
            if sl < 2:
                for cc in range(4):
                    c = 4 * sl + cc
                    bank = self.mm("mm", NMEM, [(w[:, kc, 128 * cc:128 * cc + 128], MEMN[:, kc]) for kc in range(NCH)], [MNB, sb])
                    self.act(self.KT[:, c], self.PS[bank][:, :NMEM], AF.Copy, reads=[self.PSB[bank]], writes=[self.KTB])
            if sl >= 2 or g == 0:
                for mc in range(2):
                    bank = self.mm("mm", 512, [(MEMN[:, kc, 128 * mc:128 * mc + 128], w[:, kc]) for kc in range(NCH)], [MNB, sb])
                    if sl >= 2:
                        self.act(self.VV[:, mc, 512 * (sl - 2):512 * (sl - 2) + 512], self.PS[bank][:, :], AF.Copy, reads=[self.PSB[bank]], writes=[self.VVB])
                    if g == 0 and int(os.environ.get("NO_STG", "0")) == 0:
                        k = self.stg()
                        self.act(self.STG[k], self.PS[bank][:, :], AF.Copy, reads=[self.PSB[bank]], writes=[self.STGB[k]])
                        dst = d["mk"] if sl < 2 else d["mv"]
                        col = 512 * (sl % 2)
                        if int(os.environ.get("NO_STGDMA", "0")):
                            continue
                        fw.dma(os.environ.get("STG_Q", "sp"), (lambda h, dst=dst, mc=mc, col=col, k=k: h.dma_start(out=dst[L, 128 * mc:128 * mc + 128, col:col + 512], in_=self.STG[k])),
                               reads=[self.STGB[k]])
        self.boundary(lambda e: self.pre_e_hb(L, 2, e))
        for sl in range(2):
            if CC <= 2:
                break
            s, sb = self.slab(lambda d, L=L, sl=sl: [
                (0, [NCH, 512], BF16, d["c_w_q"][L].rearrange("(kc p) n -> p kc n", p=128)[:, :, 512 * sl:512 * sl + 512], "pool")])
            if dry:
                continue
            w = self.slot(s, 0, [NCH, 512], BF16)
            for t, (c0, n) in enumerate(TILES):
                for cc in range(4):
                    c = 4 * sl + cc
                    bank = self.mm("mm", n, [(w[:, kc, 128 * cc:128 * cc + 128], self.hb(kc, t)) for kc in range(NCH)],
                                   [self.HBB[kc][t] for kc in range(NCH)] + [sb])
                    self.act(Q[:, c, c0:c0 + n], self.PS[bank][:, :n], AF.Copy, reads=[self.PSB[bank]], writes=[QB[c][t]])
        if not dry and CC > 3:
            for t in range(ST):
                c0, n = TILES[t]
                for hh in range(NH):
                    for mc in range(2):
                        bank = self.mm("mm", n, [(self.KT[:, 2 * hh + dc, 128 * mc:128 * mc + 128], Q[:, 2 * hh + dc, c0:c0 + n]) for dc in range(2)],
                                       [self.KTB, QB[2 * hh][t], QB[2 * hh + 1][t]])
                        self.act(E[:, hh, mc, :n], self.PS[bank][:, :n], AF.Exp, reads=[self.PSB[bank]], writes=[EB[hh]], scale=SC)
                for hh in range(NH):
                    bs = self.mm("st", n, [(self.ONES1, E[:, hh, mc, :n]) for mc in range(2)], [EB[hh], self.CONB])
                    self.act(RS[:, hh, :n], self.PS[bs][:, :n], AF.Ln, reads=[self.PSB[bs]], writes=[RSB[hh]])
                    self.act(RS[:, hh, :n], RS[:, hh, :n], AF.Exp, reads=[RSB[hh]], writes=[RSB[hh]], scale=-1.0)
                    for dc in range(2):
                        c = 2 * hh + dc
                        bo = self.mm("aux", n, [(self.VV[:, mc, 128 * c:128 * c + 128], E[:, hh, mc, :n]) for mc in range(2)], [self.VVB, EB[hh]])
                        self.tt(self.HB[:, c, c0:c0 + n], self.PS[bo][:, :n], RS[:, hh, :n], ALU.mult,
                                reads=[self.PSB[bo], RSB[hh]], writes=[self.HBB[c][t]])
        c0, n = TILES[ST]
        bsc = self.ps("mm")
        if CC <= 4:
            self.postnorm(L, 3)
            return
        for sq_ in range(SG_):
            s, sb = self.slab(lambda d, L=L, g=g, sq_=sq_: [
                (0, [NCH, NMEM], BF16, d["kT"][L, g * SG_ + sq_].rearrange("(c p) m -> p c m", p=128), "pool")])
            if dry:
                continue
            kt = self.slot(s, 0, [NCH, NMEM], BF16)

            def fn(h, kt=kt, sq_=sq_, c0=c0, bsc=bsc):
                first = None
                ins = None
                for hh in range(NH):
                    for mc in range(2):
                        col = mc * NH * TS + hh * TS + DSEQ * sq_
                        for dc in range(2):
                            ins = h.matmul(self.PS[bsc][:, col:col + DSEQ], kt[:, 2 * hh + dc, 128 * mc:128 * mc + 128],
                                           Q[:, 2 * hh + dc, c0 + DSEQ * sq_:c0 + DSEQ * sq_ + DSEQ], start=(dc == 0), stop=(dc == 1))
                            if first is None:
                                first = ins
                return first, ins
            self.nmm += 16
            fw.op("pe", fn, reads=[sb] + [QB[c][ST] for c in range(NCH)], writes=[self.PSB[bsc]])
        if not dry:
            self.act(ES, self.PS[bsc][:, :].rearrange("p (a b c) -> p a b c", a=2, b=NH), AF.Exp, reads=[self.PSB[bsc]], writes=[ESB], scale=SC)
            bs = self.mm("st", NH * TS, [(self.ONES1, ES[:, mc].rearrange("p a b -> p (a b)")) for mc in range(2)], [ESB, self.CONB])
            self.act(RSS, self.PS[bs][:, :NH * TS].rearrange("p (a b) -> p a b", a=NH), AF.Ln, reads=[self.PSB[bs]], writes=[RSSB])
            self.act(RSS, RSS, AF.Exp, reads=[RSSB], writes=[RSSB], scale=-1.0)
        bo = self.ps("aux")
        for sq_ in range(SG_):
            s, sb = self.slab(lambda d, L=L, g=g, sq_=sq_: [
                (0, [2, D], BF16, d["v"][L, g * SG_ + sq_].rearrange("(mc p) f -> p mc f", p=128), "pool")])
            if dry:
                continue
            vs = self.slot(s, 0, [2, D], BF16)

            def fn2(h, vs=vs, sq_=sq_, bo=bo):
                first = None
                ins = None
                for c in range(NCH):
                    hh = c // 2
                    col = c * TS + DSEQ * sq_
                    for mc in range(2):
                        ins = h.matmul(self.PS[bo][:, col:col + DSEQ], vs[:, mc, 128 * c:128 * c + 128],
                                       ES[:, mc, hh, DSEQ * sq_:DSEQ * sq_ + DSEQ], start=(mc == 0), stop=(mc == 1))
                        if first is None:
                            first = ins
                return first, ins
            self.nmm += 16
            fw.op("pe", fn2, reads=[sb, ESB], writes=[self.PSB[bo]])
        if not dry:
            for c in range(NCH):
                self.tt(self.HB[:, c, c0:c0 + n], self.PS[bo][:, c * TS:c * TS + TS], RSS[:, c // 2], ALU.mult,
                        reads=[self.PSB[bo], RSSB], writes=[self.HBB[c][ST]])
        sl_h = []
        for sl in range(2):
            sl_h.append(self.slab(lambda d, L=L, sl=sl: [
                (0, [NCH, 512], BF16, d["c_w_o"][L].rearrange("(kc p) n -> p kc n", p=128)[:, :, 512 * sl:512 * sl + 512], "pool")], first=(sl == 0)))
        if not dry:
            self.tail_begin()
            for t, (c0, n) in enumerate(TILES):
                for sl in range(2):
                    s, sb = sl_h[sl]
                    w = self.slot(s, 0, [NCH, 512], BF16)
                    for cc in range(4):
                        c = 4 * sl + cc
                        bank = self.mm("mm", n, [(w[:, kc, 128 * cc:128 * cc + 128], self.hb(kc, t)) for kc in range(NCH)],
                                       [self.HBB[kc][t] for kc in range(NCH)] + [sb])
                        self.act(self.CO[:, c, c0:c0 + n], self.PS[bank][:, :n], AF.Copy, reads=[self.PSB[bank]], writes=[self.COB[c][t]])
                        self.tail_chunk(t, c, self.PS[bank][:, :n], [self.PSB[bank]])
            self.tail_end()
        self.postnorm(L, 3)

    def ffn(self, L, g):
        fw = self.fw
        d = self.d
        dry = self.dry
        B = Buf
        NJ = NFC // 2
        last_g = (g == G - 1)
        if not dry:
            self.boundary(lambda e: self.pre_e_hb(L, 4, e))
            A = self.W(0, [NJ, TG], BF16)
            UX = self.W(23936, [2, 2, TP + 2], BF16)
            UXS = self.W(32192, [44, SG_, 10], BF16)
            DG3 = self.W(39232, [2, 6, 128], BF16)
            NF = self.W(42304, [44, 2], F32)
            NFS = self.AR.at(self.mean_off, [44, SG_, 2], F32)
            AB = [[B("a") for _ in range(NT)] for _ in range(NJ)]
            UXB = [[B("ux") for _ in range(3)] for _ in range(2)]
            UXSB = B("uxs")
            DG3B = [B("dg3") for _ in range(2)]
            NFB = B("nf")
            NFSB = B("nfs")
            self.new_view([b for l in AB for b in l] + [b for l in UXB for b in l] + [UXSB, NFB, NFSB] + DG3B)
        s, sb = self.slab(lambda d, L=L, g=g: [(0, [44 * SG_ * 2], BF16, d["sffn"][L, g], "pool")])
        if not dry:
            src = self.slot(s, 0, [44, SG_, 2], BF16)
            self.cp(UXS[:, :, :, 0:2], src, reads=[sb], writes=[UXSB])
        import os
        CUT = int(os.environ.get("FFN_CUT", "99"))
        for hf in range(2):
            if CUT <= 1:
                break
            for jj in range(NJ):
                if CUT <= 2 and jj >= 1:
                    break
                jf = NJ * hf + jj
                s, sb = self.slab(lambda d, L=L, jf=jf: [
                    (0, [NCH, 128], BF16, d["f_w_up"][L].rearrange("(kc p) n -> p kc n", p=128)[:, :, 128 * jf:128 * jf + 128], "pool"),
                    (NCH * 128 * 2, [NCH, 128], BF16, d["f_w_up"][L].rearrange("(kc p) n -> p kc n", p=128)[:, :, DFF + 128 * jf:DFF + 128 * jf + 128], "pool")])
                if dry:
                    continue
                wg = self.slot(s, 0, [NCH, 128], BF16)
                wv = self.slot(s, NCH * 128 * 2, [NCH, 128], BF16)
                ub = jf % 2
                chs = (jf, NFC + jf)
                for gv in range(2):
                    self.cp(UX[:, ub, gv, 0:2], self.CARRY_F[:, L, chs[gv]], reads=[self.CFB[L]], writes=[UXB[ub][0]])
                for t, (c0, n) in enumerate(TILES):
                    hr = [self.HBB[kc][t] for kc in range(NCH)] + [sb]
                    t0 = []
                    for gv, wsl in enumerate((wg, wv)):
                        ch = chs[gv]
                        bank = self.mm("mm6", n, [(wsl[:, kc], self.hb(kc, t)) for kc in range(NCH)], hr)
                        p = self.PS[bank][:, :n]
                        ti = self.ft()
                        T0 = self.FT[ti][:, :n]
                        w0c = self.prm(PC_FW + (L * 3 + 0) * 44 + ch)
                        w1c = self.prm(PC_FW + (L * 3 + 1) * 44 + ch)
                        w2c = self.prm(PC_FW + (L * 3 + 2) * 44 + ch)
                        self.act(T0, p, AF.Identity, reads=[self.PSB[bank], self.PRMB], writes=[self.FTB[ti]],
                                 scale=w2c, bias=self.prm(PC_FB + L * 44 + ch))
                        if t < ST:
                            self.act(UX[:, ub, gv, 2 + c0:2 + c0 + n], p, AF.Copy, reads=[self.PSB[bank]], writes=[UXB[ub][1 + t]])
                            if last_g and t == ST - 1:
                                self.act(NF[:, ch], p[:, n - 2:n], AF.Copy, reads=[self.PSB[bank]], writes=[NFB])
                            rd = [UXB[ub][0], UXB[ub][1]] + ([UXB[ub][2]] if t == 1 else [])
                            self.stt(T0, UX[:, ub, gv, c0 + 1:c0 + 1 + n], w1c, T0, ALU.mult, ALU.add, reads=rd + [self.FTB[ti], self.PRMB], writes=[self.FTB[ti]])
                            self.stt(T0, UX[:, ub, gv, c0:c0 + n], w0c, T0, ALU.mult, ALU.add, reads=rd + [self.FTB[ti], self.PRMB], writes=[self.FTB[ti]])
                        else:
                            p3 = p.rearrange("p (s l) -> p s l", l=DSEQ)
                            T3 = T0.rearrange("p (s l) -> p s l", l=DSEQ)
                            self.act(UXS[:, ch, :, 2:10], p3, AF.Copy, reads=[self.PSB[bank]], writes=[UXSB])
                            self.act(NFS[:, ch], p3[:, :, DSEQ - 2:DSEQ], AF.Copy, reads=[self.PSB[bank]], writes=[NFSB])
                            self.stt(T3, UXS[:, ch, :, 1:9], w1c, T3, ALU.mult, ALU.add, reads=[UXSB, self.FTB[ti], self.PRMB], writes=[self.FTB[ti]])
                            self.stt(T3, UXS[:, ch, :, 0:8], w0c, T3, ALU.mult, ALU.add, reads=[UXSB, self.FTB[ti], self.PRMB], writes=[self.FTB[ti]])
                        t0.append(ti)
                    tg, tv = t0
                    G_ = self.FT[tg][:, :n]
                    self.act(G_, G_, AF.Silu, reads=[self.FTB[tg]], writes=[self.FTB[tg]])
                    self.tt(A[:, jj, c0:c0 + n], self.FT[tv][:, :n], G_, ALU.mult, reads=[self.FTB[tv], self.FTB[tg]], writes=[AB[jj][t]])
                if not last_g:
                    for gv in range(2):
                        self.cp(self.CARRY_F[:, L, chs[gv]], UX[:, ub, gv, TP:TP + 2], reads=[UXB[ub][2]], writes=[self.CFB[L]])
            dn = []
            for oc2 in range(4):
                dn.append(self.slab(lambda d, L=L, hf=hf, oc2=oc2: [
                    (0, [NJ, 256], BF16, d["f_w_down"][L][NJ * 128 * hf:NJ * 128 * hf + NJ * 128, :].rearrange("(jj p) n -> p jj n", p=128)[:, :, 256 * oc2:256 * oc2 + 256], "pool")],
                    first=(hf == 0 or oc2 == 0)))
                if dry or hf == 1:
                    continue
                self._down(dn[oc2], oc2, list(range(NT)), hf, A, AB, NJ)
            if not dry and hf == 1:
                self.tail_begin()
                for t in range(NT):
                    for oc2 in range(4):
                        self._down(dn[oc2], oc2, [t], hf, A, AB, NJ, tail=True)
                self.tail_end()
        if not dry:
            if last_g:
                fw.dma("sp", lambda h: h.dma_start(out=d["nfp"][L].rearrange("(c p) k -> p c k", p=128), in_=NF), reads=[NFB])
            fw.dma("sp", lambda h: h.dma_start(out=d["nfs"][L, g], in_=NFS), reads=[NFSB])
        self.postnorm(L, 5)

    def _down(self, sh, oc2, tiles, hf, A, AB, NJ, tail=False):
        s, sb = sh
        w = self.slot(s, 0, [NJ, 256], BF16)
        for t in tiles:
            c0, n = TILES[t]
            for cc in range(2):
                c = 2 * oc2 + cc
                bank = self.mm("mm", n, [(w[:, jj, 128 * cc:128 * cc + 128], A[:, jj, c0:c0 + n]) for jj in range(NJ)],
                               [AB[jj][t] for jj in range(NJ)] + [sb])
                o = self.CO[:, c, c0:c0 + n]
                if hf == 0:
                    self.act(o, self.PS[bank][:, :n], AF.Copy, reads=[self.PSB[bank]], writes=[self.COB[c][t]])
                else:
                    self.tt(o, o, self.PS[bank][:, :n], ALU.add, reads=[self.PSB[bank], self.COB[c][t]], writes=[self.COB[c][t]])
                if tail:
                    self.tail_chunk(t, c, o, [self.COB[c][t]])

    def build(self, nstage=None, ngroups=G):
        self.init_consts()
        k = 0
        for g in range(ngroups):
            self.load_x(g)
            for L in range(DEPTH):
                for st in range(3):
                    if nstage is not None and k >= nstage:
                        continue
                    k += 1
                    self.marks.append(("g%d L%d %s" % (g, L, ("mix", "cross", "ffn")[st]), self.nmm))
                    if st == 0:
                        if L % 2 == 0:
                            self.a_mixer(L, g)
                        else:
                            self.b_mixer(L, g)
                    elif st == 1:
                        self.cross(L, g)
                    else:
                        self.ffn(L, g)
            self.boundary(None)
            self.store_x(g)
        self.marks.append(("end", self.nmm))
        if not self.dry:
            import os as _os2, json as _json
            if _os2.environ.get("KMARKS"):
                _json.dump(self.marks, open(_os2.environ["KMARKS"], "w"))
            self.fw.run()


def _declare(nc):
    d = {}

    def inp(name, shape):
        d[name] = nc.dram_tensor(name, list(shape), F32, kind="ExternalInput").ap()

    def outp(name, shape):
        d[name] = nc.dram_tensor(name, list(shape), F32, kind="ExternalOutput").ap()

    inp("xT", [D, SEQ + NSEQ * DSEQ])
    inp("memT", [D, NMEM])
    inp("sconv", [2, G, 128, NCH * SG_ * 30])
    inp("spool", [2, G, 128, NCH * SG_ * 15])
    inp("sffn", [DEPTH, G, 128, 44 * SG_ * 2])
    inp("kT", [DEPTH, NSEQ, D, NMEM])
    inp("v", [DEPTH, NSEQ, NMEM, D])
    inp("prm", [128, NPRM])
    inp("ident", [128, 128])
    inp("a_w_in", [2, D, 2 * D])
    inp("a_w_out", [2, D, D])
    inp("p_w_group", [2, 4, 256, 256])
    inp("c_w_q", [DEPTH, D, D])
    inp("c_w_kv", [DEPTH, D, 2 * D])
    inp("c_w_o", [DEPTH, D, D])
    inp("f_w_up", [DEPTH, D, F2])
    inp("f_w_down", [DEPTH, DFF, D])
    d["sconv4"] = d["sconv"].rearrange("j g p (c s k) -> j g p c s k", c=NCH, s=SG_)
    outp("yT", [D, SEQ + NSEQ * DSEQ])
    outp("ncp", [2, D, 30])
    outp("npp", [2, D, 15])
    outp("nfp", [DEPTH, F2, 2])
    outp("mk", [DEPTH, NMEM, D])
    outp("mv", [DEPTH, NMEM, D])
    outp("ncs_", [2, G, 128, NCH * SG_ * 30])
    outp("nps_", [2, G, 128, NCH * SG_ * 15])
    outp("nfs_", [DEPTH, G, 128, 44 * SG_ * 2])
    d["ncs"] = d["ncs_"].rearrange("j g p (c s k) -> j g p c s k", c=NCH, s=SG_)
    d["nps"] = d["nps_"].rearrange("j g p (c s k) -> j g p c s k", c=NCH, s=SG_)
    d["nfs"] = d["nfs_"].rearrange("l g p (c s k) -> l g p c s k", c=44, s=SG_)
    return d


_BF = None


def _bf16_dtype():
    import ml_dtypes
    return ml_dtypes.bfloat16


def build_program(nstage=None, ngroups=G):
    nc = bass.Bass("TRN2", target_bir_lowering=False)
    d = _declare(nc)
    log = []
    Builder(nc, d, True, log).build(nstage, ngroups)
    Builder(nc, d, False, log).build(nstage, ngroups)
    return nc


def _fm(vec):
    v = np.asarray(vec, np.float32)
    sh = v.shape
    v = v.reshape(sh[:-1] + (sh[-1] // 128, 128))
    return np.moveaxis(v, -1, 0)


def _pack_params(inp):
    P = np.zeros((128, NPRM), np.float32)
    P[:, 0:224] = _fm(inp["norm_gains"]).reshape(128, 224)
    P[:, PC_BIN:PC_BIN + 32] = _fm(inp["a_b_in"]).reshape(128, 32)
    for j in range(2):
        for o, nm in ((0, "a_b_dw"), (8, "a_ln_g"), (16, "a_ln_b"), (24, "a_b_out")):
            P[:, PC_A + j * 32 + o:PC_A + j * 32 + o + 8] = _fm(inp[nm][j])
    P[:, PC_PS:PC_PS + 16] = _fm(inp["p_scale"]).reshape(128, 16)
    P[:, PC_FB:PC_FB + 176] = _fm(inp["f_b_dw"]).reshape(128, 176)
    P[:, PC_FW:PC_FW + 528] = _fm(inp["f_w_dw"]).reshape(128, 528)
    aw = _fm(inp["a_w_dw"])
    P[:, PC_AW:PC_AW + 496] = np.transpose(aw, (0, 1, 3, 2)).reshape(128, 496)
    return P


def _state_fm(st, nch):
    J, S, K, F = st.shape
    v = st.reshape(J, G, SG_, K, nch, 128)
    v = np.transpose(v, (0, 1, 5, 4, 2, 3))
    return np.ascontiguousarray(v).reshape(J, G, 128, nch * SG_ * K)


def _state_back(dev, nch, K):
    J = dev.shape[0]
    v = dev.reshape(J, G, 128, nch, SG_, K)
    v = np.transpose(v, (0, 1, 4, 5, 3, 2))
    return v.reshape(J, G * SG_, K, nch * 128)


_PROG = None


def kernel(_ncore=8, _nstage=None, _ngroups=G, **inputs):
    global _PROG
    inp = {k: np.asarray(v) for k, v in inputs.items()}
    ncore = _ncore
    if _PROG is None:
        _PROG = build_program(_nstage, _ngroups)
    nc = _PROG
    prm = _pack_params(inp)
    ident = np.eye(128, dtype=np.float32)
    shared = {k: np.ascontiguousarray(inp[k], dtype=np.float32) for k in
              ("a_w_in", "a_w_out", "p_w_group", "c_w_q", "c_w_kv", "c_w_o", "f_w_up", "f_w_down")}
    in_maps = []
    import os as _os
    _same = int(_os.environ.get("SAME_DATA", "0"))
    for i_ in range(ncore):
        i = 0 if _same else i_
        sl = slice(NSEQ * i, NSEQ * i + NSEQ)
        xs = inp["x_sample"][sl].reshape(NSEQ * DSEQ, D)
        xT = np.ascontiguousarray(np.concatenate([inp["x_prompt"][i], xs], axis=0).T)
        m = {
            "xT": xT,
            "memT": np.ascontiguousarray(inp["mem_prompt"][i].T),
            "sconv": _state_fm(inp["state_conv"][:, sl], NCH),
            "spool": _state_fm(inp["state_pool"][:, sl], NCH),
            "sffn": _state_fm(inp["state_ffn"][:, sl], 44),
            "kT": np.ascontiguousarray(np.transpose(inp["cache_mem_k"][:, sl].reshape(DEPTH, NSEQ, NMEM, D), (0, 1, 3, 2))),
            "v": np.ascontiguousarray(inp["cache_mem_v"][:, sl].reshape(DEPTH, NSEQ, NMEM, D)),
            "prm": prm,
            "ident": ident,
        }
        m.update(shared)
        in_maps.append(m)
    res = run_bass_kernel_spmd(nc, in_maps, core_ids=list(range(ncore)))
    R = res.results
    y_prompt = np.stack([R[i]["yT"][:, :SEQ].T for i in range(ncore)])
    y_sample = np.concatenate([R[i]["yT"][:, SEQ:].T.reshape(NSEQ, DSEQ, D) for i in range(ncore)])
    ncp = np.stack([np.transpose(R[i]["ncp"], (0, 2, 1)) for i in range(ncore)], axis=1)
    npp = np.stack([np.transpose(R[i]["npp"], (0, 2, 1)) for i in range(ncore)], axis=1)
    nfp = np.stack([np.transpose(R[i]["nfp"], (0, 2, 1)) for i in range(ncore)], axis=1)
    mk = np.stack([R[i]["mk"] for i in range(ncore)], axis=1).reshape(DEPTH, ncore, NMEM, NH, D // NH)
    mv = np.stack([R[i]["mv"] for i in range(ncore)], axis=1).reshape(DEPTH, ncore, NMEM, NH, D // NH)
    ncs = np.concatenate([_state_back(R[i]["ncs_"], NCH, 30) for i in range(ncore)], axis=1)
    nps = np.concatenate([_state_back(R[i]["nps_"], NCH, 15) for i in range(ncore)], axis=1)
    nfs = np.concatenate([_state_back(R[i]["nfs_"], 44, 2) for i in range(ncore)], axis=1)
    f = lambda a: np.ascontiguousarray(a, dtype=np.float32)
    return (f(y_prompt), f(y_sample), f(ncp), f(npp), f(nfp), f(mk), f(mv), f(ncs), f(nps), f(nfs))
```

```python
import contextlib
import numpy as np
import concourse.bass as bass
import concourse.mybir as mybir
from concourse.bass_utils import run_bass_kernel_spmd

F32 = mybir.dt.float32
BF16 = mybir.dt.bfloat16
AF = mybir.ActivationFunctionType
ALU = mybir.AluOpType

D = 1024
NCH = 8
SEQ = 2048
DEPTH = 4
NSEQ = 16
DSEQ = 8
NMEM = 256
NH = 4
DFF = 2816
F2 = 5632
NFC = 22
CW = 31
G = 2
TP = SEQ // G
SG_ = NSEQ // G
TS = SG_ * DSEQ
TG = TP + TS
TILES = [(0, 512), (512, 512), (1024, TS)]
NT = len(TILES)
ETILES = [(0, 512, (0,)), (512, TG - 512, (1, 2))]
ST = 2
RMS_EPS = 1e-6
LN_EPS = 1e-5
NSLOT = 4
SLOT_BYTES = 8192

def _pc_gain(i, k, c): return (i * 7 + k) * 8 + c
PC_BIN = 224
PC_A = 256
PC_PS = 320
PC_FB = 336
PC_FW = 512
PC_AW = 1040
NPRM = 1536


class Buf:
    __slots__ = ("name", "lw", "rd", "dsem", "dcnt")

    def __init__(self, name):
        self.name = name
        self.lw = None
        self.rd = {}
        self.dsem = None
        self.dcnt = 0


class Eng:
    def __init__(self, name):
        self.name = name
        self.ops = []
        self.count = 0
        self.seen = {}
        self.semkey = "E_" + name


class FW:
    def __init__(self, nc, dry=False):
        self.nc = nc
        self.dry = dry
        self.eng = {n: Eng(n) for n in ("pe", "act", "dve", "pool", "sp")}
        self.semkeys = [e.semkey for e in self.eng.values()]
        self.ndsem = 0
        self.all_dma = {}

    def _deps(self, e, reads, writes):
        need = {}
        for b in reads:
            if b.lw is not None:
                k, v = b.lw
                if need.get(k, 0) < v:
                    need[k] = v
        for b in writes:
            if b.lw is not None:
                k, v = b.lw
                if need.get(k, 0) < v:
                    need[k] = v
            for k, v in b.rd.items():
                if need.get(k, 0) < v:
                    need[k] = v
        waits = []
        for k, v in need.items():
            if k == e.semkey and e.name == "pe":
                continue
            if e.seen.get(k, 0) < v:
                e.seen[k] = v
                waits.append((k, v))
        return waits

    def op(self, engname, fn, reads=(), writes=()):
        if self.dry:
            return
        e = self.eng[engname]
        waits = self._deps(e, reads, writes)
        e.count += 1
        t = (e.semkey, e.count)
        e.ops.append((fn, waits, e.semkey, 1, True))
        for b in writes:
            b.lw = t
            b.rd = {}
        for b in reads:
            if b.rd.get(t[0], 0) < t[1]:
                b.rd[t[0]] = t[1]

    def dma(self, qname, fn, reads=(), writes=(), sem_of=None):
        if self.dry:
            return
        e = self.eng[qname]
        owner = sem_of if sem_of is not None else (writes[0] if writes else reads[0])
        if owner.dsem is None:
            owner.dsem = "D_%d" % self.ndsem
            self.ndsem += 1
            self.semkeys.append(owner.dsem)
        waits = self._deps(e, reads, writes)
        owner.dcnt += 16
        t = (owner.dsem, owner.dcnt)
        e.ops.append((fn, waits, owner.dsem, 16, False))
        self.all_dma[owner.dsem] = owner.dcnt
        for b in writes:
            b.lw = t
            b.rd = {}
        for b in reads:
            if b.rd.get(t[0], 0) < t[1]:
                b.rd[t[0]] = t[1]

    def fence(self, bufs_old, bufs_new):
        need = {}
        for b in bufs_old:
            if b.lw is not None:
                k, v = b.lw
                need[k] = max(need.get(k, 0), v)
            for k, v in b.rd.items():
                need[k] = max(need.get(k, 0), v)
        for b in bufs_new:
            b.lw = None
            b.rd = dict(need)

    def run(self):
        nc = self.nc
        with contextlib.ExitStack() as st:
            sems = {}
            for k in self.semkeys:
                sems[k] = st.enter_context(nc.semaphore(k))
            block = st.enter_context(nc.Block())
            fin = self.eng["sp"]
            handles = {"pe": "tensor", "act": "scalar", "dve": "vector", "pool": "gpsimd", "sp": "sync"}

            marked = {}
            for e in self.eng.values():
                for (fn, waits, isem, iamt, attach) in e.ops:
                    for (k, v) in waits:
                        if k.startswith("E_"):
                            marked.setdefault(k, set()).add(v)
            rank = {k: {v: i + 1 for i, v in enumerate(sorted(vs))} for k, vs in marked.items()}

            def wv(k, v):
                return rank[k][v] if k.startswith("E_") else v

            def make(e):
                def body(h):
                    ordinal = 0
                    for (fn, waits, isem, iamt, attach) in e.ops:
                        if attach and waits:
                            for (k, v) in waits[:-1]:
                                h.wait_ge(sems[k], wv(k, v))
                            r = fn(h)
                            first, last = r if isinstance(r, tuple) else (r, r)
                            k, v = waits[-1]
                            first._wait_ge(sems[k], wv(k, v))
                        else:
                            for (k, v) in waits:
                                h.wait_ge(sems[k], wv(k, v))
                            r = fn(h)
                            first, last = r if isinstance(r, tuple) else (r, r)
                        if isem.startswith("E_"):
                            ordinal += 1
                            if ordinal in rank.get(isem, ()):
                                last.then_inc(sems[isem], 1)
                        else:
                            last.then_inc(sems[isem], iamt)
                    if e is fin:
                        for k, v in self.all_dma.items():
                            h.wait_ge(sems[k], v)
                return body

            for name, e in self.eng.items():
                if not e.ops and e is not fin:
                    continue
                getattr(block, handles[name])(make(e))


class Arena:
    def __init__(self, nc, nbytes):
        self.nbytes = nbytes
        self.t = nc.alloc_sbuf_tensor("arena", [128, nbytes // 2], BF16)
        self.top = 0

    def at(self, off, shape, dt):
        n = int(np.prod(shape))
        esz = 4 if dt == F32 else 2
        assert off % 4 == 0 and off + n * esz <= self.nbytes, (off, shape, self.nbytes)
        v = self.t[:, off // 2: off // 2 + n * esz // 2]
        if dt == F32:
            v = v.bitcast(F32)
        if len(shape) == 2:
            v = v.rearrange("p (a b) -> p a b", a=shape[0])
        elif len(shape) == 3:
            v = v.rearrange("p (a b c) -> p a b c", a=shape[0], b=shape[1])
        elif len(shape) == 4:
            v = v.rearrange("p (a b c d) -> p a b c d", a=shape[0], b=shape[1], c=shape[2])
        return v

    def alloc(self, shape, dt):
        n = int(np.prod(shape)) * (4 if dt == F32 else 2)
        off = self.top
        self.top = (off + n + 63) // 64 * 64
        assert self.top <= self.nbytes, ("arena overflow", self.top, self.nbytes)
        return self.at(off, shape, dt), off


class Builder:
    def __init__(self, nc, dram, dry, slab_log):
        self.nc = nc
        self.d = dram
        self.fw = FW(nc, dry=dry)
        self.dry = dry
        self.slab_log = slab_log
        self.slab_idx = 0
        self.slab_loaded = 0
        self.pending_post = None
        self.post_stats_done = False
        self.nmm = 0
        self.marks = []
        self._alloc()

    def _alloc(self):
        nc = self.nc
        self.ps_cls = {"st": [0, 1], "mm": [2, 3, 4, 5], "aux": [6, 7], "mm6": [2, 3, 4, 5, 6, 7]}
        self.ps_i = {"st": 0, "mm": 0, "aux": 0, "mm6": 0}
        if self.dry:
            return
        AR = Arena(nc, 211200)
        self.AR = AR
        B = Buf
        self.X, _ = AR.alloc([NCH, TG], F32)
        self.XB = [[B("x") for _ in range(NT)] for _ in range(NCH)]
        self.PRM, _ = AR.alloc([NPRM], F32)
        self.PRMB = B("prm")
        self.IDENT, _ = AR.alloc([128], BF16)
        self.ONESM, _ = AR.alloc([128], BF16)
        self.ONES1, _ = AR.alloc([128], BF16)
        self.CONB = B("const")
        self.RC, _ = AR.alloc([16], F32)
        self.CARRY_A, _ = AR.alloc([2, NCH, 30], BF16)
        self.CARRY_P, _ = AR.alloc([2, NCH, 16], F32)
        self.CARRY_F, _ = AR.alloc([DEPTH, 44, 2], BF16)
        self.CAB = [B("ca") for _ in range(2)]
        self.CPB = [B("cp") for _ in range(2)]
        self.CFB = [B("cf") for _ in range(DEPTH)]
        self.KT, _ = AR.alloc([NCH, NMEM], BF16)
        self.VV, _ = AR.alloc([2, D], BF16)
        self.KTB = B("kt")
        self.VVB = B("vv")
        self.slot_off = []
        for s in range(NSLOT):
            _, off = AR.alloc([SLOT_BYTES // 2], BF16)
            self.slot_off.append(off)
        self.SLB = [B("slot%d" % s) for s in range(NSLOT)]
        self.HB, _ = AR.alloc([NCH, TG], BF16)
        self.HBB = [[B("h") for _ in range(NT)] for _ in range(NCH)]
        self.CO, _ = AR.alloc([NCH, TG], F32)
        self.COB = [[B("co") for _ in range(NT)] for _ in range(NCH)]
        self.MS, _ = AR.alloc([TG], F32)
        self.R, _ = AR.alloc([TG], F32)
        self.MEAN, self.mean_off = AR.alloc([TG], F32)
        self.XTRA, self.xtra_off = AR.alloc([TG], F32)
        self.MSB = [B("ms") for _ in range(NT)]
        self.RB = [B("r") for _ in range(NT)]
        self.MEANB = [B("mean") for _ in range(NT)]
        self.XTRAB = B("xtra")
        self.NSQ = 4
        self.SQ = [AR.alloc([TG - 512], BF16)[0] for _ in range(self.NSQ)]
        self.SQB = [B("sq") for _ in range(self.NSQ)]
        self.sq_i = 0
        self.NTMP = 2
        self.TMP = [AR.alloc([512], F32)[0] for _ in range(self.NTMP)]
        self.TMPB = [B("tmp") for _ in range(self.NTMP)]
        self.tmp_i = 0
        self.NSTG = 2
        self.STG = [AR.alloc([512], F32)[0] for _ in range(self.NSTG)]
        self.STGB = [B("stg") for _ in range(self.NSTG)]
        self.stg_i = 0
        xt = [self.AR.at(self.xtra_off + 2048 * i, [512], F32) for i in range(2)]
        self.FT = self.TMP + self.STG + xt
        self.FTB = self.TMPB + self.STGB + [B("xt0"), B("xt1")]
        self.ft_i = 0
        self.WSIZE = 43008
        _, self.w_off = AR.alloc([self.WSIZE // 2], BF16)
        self.WB_cur = []
        self.PS = [nc.alloc_psum_tensor("ps%d" % i, [128, 512], F32) for i in range(8)]
        self.PSB = [B("ps%d" % i) for i in range(8)]
        self.DRB = B("dram_passthru")

    def W(self, off, shape, dt):
        return self.AR.at(self.w_off + off, shape, dt)

    def new_view(self, bufs):
        if self.dry:
            return
        shared = [self.MEANB[t] for t in range(NT)] + [self.XTRAB, self.FTB[4], self.FTB[5]]
        self.fw.fence(self.WB_cur + shared, list(bufs) + shared)
        self.WB_cur = list(bufs)

    def ps(self, cls):
        lst = self.ps_cls[cls]
        i = lst[self.ps_i[cls] % len(lst)]
        self.ps_i[cls] += 1
        return i

    def sq(self):
        i = self.sq_i % self.NSQ
        self.sq_i += 1
        return i

    def tmp(self):
        i = self.tmp_i % self.NTMP
        self.tmp_i += 1
        return i

    def ft(self):
        i = self.ft_i % len(self.FT)
        self.ft_i += 1
        return i

    def stg(self):
        i = self.stg_i % self.NSTG
        self.stg_i += 1
        return i

    def slab(self, spec, first=True):
        if self.dry:
            self.slab_log.append(spec)
            return 0, None
        idx = self.slab_idx
        self.slab_idx += 1
        if first:
            self.slab_base = idx
        while self.slab_loaded < min(len(self.slab_log), self.slab_base + NSLOT):
            self._load_slab(self.slab_loaded)
            self.slab_loaded += 1
        return idx % NSLOT, self.SLB[idx % NSLOT]

    def _load_slab(self, i):
        spec = self.slab_log[i]
        s = i % NSLOT
        for (boff, shape, dt, src, q) in spec(self.d):
            dst = self.AR.at(self.slot_off[s] + boff, shape, dt)
            self.fw.dma(q, (lambda h, dst=dst, src=src: h.dma_start(out=dst, in_=src)), writes=[self.SLB[s]])

    def slot(self, s, boff, shape, dt):
        return self.AR.at(self.slot_off[s] + boff, shape, dt)

    def prm(self, col):
        return self.PRM[:, col:col + 1]

    def mm(self, cls_or_bank, n, terms, reads, col0=0):
        bank = cls_or_bank if isinstance(cls_or_bank, int) else self.ps(cls_or_bank)
        if self.dry:
            return bank
        out = self.PS[bank][:, col0:col0 + n]
        nt = len(terms)
        self.nmm += nt

        def fn(h):
            first = None
            ins = None
            for i, (l, r) in enumerate(terms):
                ins = h.matmul(out, l, r, start=(i == 0), stop=(i == nt - 1))
                if first is None:
                    first = ins
            return first, ins
        self.fw.op("pe", fn, reads=reads, writes=[self.PSB[bank]])
        return bank

    def act(self, out, in_, func, reads, writes, bias=None, scale=None):
        kw = {}
        if bias is not None:
            kw["bias"] = bias
        if scale is not None:
            kw["scale"] = scale
        self.fw.op("act", lambda h: h.activation(out=out, in_=in_, func=func, **kw), reads=reads, writes=writes)

    def stt(self, out, in0, scalar, in1, op0, op1, reads, writes):
        self.fw.op("dve", lambda h: h.scalar_tensor_tensor(out=out, in0=in0, scalar=scalar, in1=in1, op0=op0, op1=op1),
                   reads=reads, writes=writes)

    def tt(self, out, in0, in1, op, reads, writes):
        self.fw.op("dve", lambda h: h.tensor_tensor(out=out, in0=in0, in1=in1, op=op), reads=reads, writes=writes)

    def ttp(self, out, in0, in1, op, reads, writes):
        self.fw.op("pool", lambda h: h.tensor_tensor(out=out, in0=in0, in1=in1, op=op), reads=reads, writes=writes)

    def ts(self, out, in0, s1, op0, reads, writes):
        self.fw.op("dve", lambda h: h.tensor_scalar(out=out, in0=in0, scalar1=s1, scalar2=None, op0=op0), reads=reads, writes=writes)

    def cp(self, out, in_, reads, writes):
        self.fw.op("dve", lambda h: h.tensor_copy(out=out, in_=in_), reads=reads, writes=writes)

    def rms_stats(self, src3, srcbufs, e):
        e0, en, tl = ETILES[e]
        banks = {t: self.ps("st") for t in tl}
        for c in range(NCH):
            q = self.sq()
            self.act(self.SQ[q][:, :en], src3[:, c, e0:e0 + en], AF.Square, reads=[srcbufs[c][t] for t in tl], writes=[self.SQB[q]])
            for t in tl:
                c0, n = TILES[t]
                out = self.PS[banks[t]][:, :n]
                rhs = self.SQ[q][:, c0 - e0:c0 - e0 + n]
                self.nmm += 1
                self.fw.op("pe", (lambda h, out=out, rhs=rhs, c=c: h.matmul(out, self.ONESM, rhs, start=(c == 0), stop=(c == NCH - 1))),
                           reads=[self.SQB[q], self.CONB], writes=[self.PSB[banks[t]]])
        for t in tl:
            c0, n = TILES[t]
            self.ts(self.MS[:, c0:c0 + n], self.PS[banks[t]][:, :n], RMS_EPS, ALU.add, reads=[self.PSB[banks[t]]], writes=[self.MSB[t]])
        self.ln_exp_rstd(e0, en, tl)

    def ln_exp_rstd(self, e0, en, tl):
        self.act(self.R[:, e0:e0 + en], self.MS[:, e0:e0 + en], AF.Ln, reads=[self.MSB[t] for t in tl], writes=[self.RB[t] for t in tl])
        self.act(self.R[:, e0:e0 + en], self.R[:, e0:e0 + en], AF.Exp, reads=[self.RB[t] for t in tl], writes=[self.RB[t] for t in tl], scale=-0.5)

    def tail_begin(self):
        self._tail = {"pend": None, "banks": {}}

    def tail_chunk(self, t, c, src, reads, **kw):
        self._tail_flush()
        q = self.sq()
        n = TILES[t][1]
        self.act(self.SQ[q][:, :n], src, AF.Square, reads=reads, writes=[self.SQB[q]], **kw)
        self._tail["pend"] = (t, c, q)

    def _tail_flush(self):
        p = self._tail["pend"]
        if p is None:
            return
        t, c, q = p
        c0, n = TILES[t]
        if t not in self._tail["banks"]:
            self._tail["banks"][t] = self.ps("st")
        bank = self._tail["banks"][t]
        out = self.PS[bank][:, :n]
        rhs = self.SQ[q][:, :n]
        self.nmm += 1
        self.fw.op("pe", (lambda h, out=out, rhs=rhs, c=c: h.matmul(out, self.ONESM, rhs, start=(c == 0), stop=(c == NCH - 1))),
                   reads=[self.SQB[q], self.CONB], writes=[self.PSB[bank]])
        if c == NCH - 1:
            self.ts(self.MS[:, c0:c0 + n], self.PS[bank][:, :n], RMS_EPS, ALU.add, reads=[self.PSB[bank]], writes=[self.MSB[t]])
            for (e0, en, tl) in ETILES:
                if tl[-1] == t:
                    self.ln_exp_rstd(e0, en, tl)
        self._tail["pend"] = None

    def tail_end(self):
        self._tail_flush()
        self.post_stats_done = True

    def post_e(self, L, k, e):
        e0, en, tl = ETILES[e]
        if not self.post_stats_done:
            self.rms_stats(self.CO, self.COB, e)
        for c in range(NCH):
            o = self.CO[:, c, e0:e0 + en]
            cb = [self.COB[c][t] for t in tl]
            xb = [self.XB[c][t] for t in tl]
            self.stt(o, o, self.prm(_pc_gain(L, k, c)), self.R[:, e0:e0 + en], ALU.mult, ALU.mult,
                     reads=cb + [self.RB[t] for t in tl] + [self.PRMB], writes=cb)
            x = self.X[:, c, e0:e0 + en]
            self.tt(x, x, o, ALU.add, reads=xb + cb, writes=xb)

    def pre_e_hb(self, L, k, e):
        e0, en, tl = ETILES[e]
        self.rms_stats(self.X, self.XB, e)
        for c in range(NCH):
            self.stt(self.HB[:, c, e0:e0 + en], self.X[:, c, e0:e0 + en], self.prm(_pc_gain(L, k, c)), self.R[:, e0:e0 + en],
                     ALU.mult, ALU.mult, reads=[self.XB[c][t] for t in tl] + [self.RB[t] for t in tl] + [self.PRMB],
                     writes=[self.HBB[c][t] for t in tl])

    def boundary(self, pre_fn):
        if self.dry:
            return
        pend = self.pending_post
        self.pending_post = None
        for e in range(len(ETILES)):
            if pend is not None:
                self.post_e(pend[0], pend[1], e)
            if pre_fn is not None:
                pre_fn(e)
        self.post_stats_done = False

    def postnorm(self, L, k):
        self.pending_post = (L, k)

    def hb(self, c, t):
        c0, n = TILES[t]
        return self.HB[:, c, c0:c0 + n]

    def init_consts(self):
        if self.dry:
            return
        fw = self.fw
        d = self.d
        fw.dma("sp", lambda h: h.dma_start(out=self.PRM, in_=d["prm"][:, :]), writes=[self.PRMB])
        fw.dma("pool", lambda h: h.dma_start(out=self.IDENT, in_=d["ident"][:, :]), writes=[self.CONB])
        fw.op("pool", lambda h: h.memset(self.ONESM, 1.0 / D), writes=[self.CONB])
        fw.op("pool", lambda h: h.memset(self.ONES1, 1.0), writes=[self.CONB])
        for i in range(16):
            fw.op("pool", (lambda h, i=i: h.memset(self.RC[:, i:i + 1], 1.0 / (i + 1))), writes=[self.CONB])
        for j in range(2):
            fw.op("pool", (lambda h, j=j: h.memset(self.CARRY_A[:, j], 0.0)), writes=[self.CAB[j]])
            fw.op("pool", (lambda h, j=j: h.memset(self.CARRY_P[:, j], 0.0)), writes=[self.CPB[j]])
        for i in range(DEPTH):
            fw.op("pool", (lambda h, i=i: h.memset(self.CARRY_F[:, i], 0.0)), writes=[self.CFB[i]])

    def load_x(self, g):
        if self.dry:
            return
        xT = self.d["xT"].rearrange("(c p) t -> p c t", p=128)
        allx = [self.XB[c][t] for c in range(NCH) for t in range(NT)]
        self.fw.dma("sp", lambda h: h.dma_start(out=self.X[:, :, 0:TP], in_=xT[:, :, TP * g:TP * g + TP]),
                    writes=allx, sem_of=self.XB[0][0])
        self.fw.dma("sp", lambda h: h.dma_start(out=self.X[:, :, TP:TG], in_=xT[:, :, SEQ + TS * g:SEQ + TS * g + TS]),
                    writes=allx, sem_of=self.XB[0][0])

    def store_x(self, g):
        if self.dry:
            return
        yT = self.d["yT"].rearrange("(c p) t -> p c t", p=128)
        allx = [self.XB[c][t] for c in range(NCH) for t in range(NT)]
        self.fw.dma("sp", lambda h: h.dma_start(out=yT[:, :, TP * g:TP * g + TP], in_=self.X[:, :, 0:TP]),
                    reads=allx, sem_of=self.XB[0][0])
        self.fw.dma("sp", lambda h: h.dma_start(out=yT[:, :, SEQ + TS * g:SEQ + TS * g + TS], in_=self.X[:, :, TP:TG]),
                    reads=allx, sem_of=self.XB[0][0])

    def a_mixer(self, L, g):
        j = L // 2
        fw = self.fw
        d = self.d
        dry = self.dry
        B = Buf
        if not dry:
            self.boundary(lambda e: self.pre_e_hb(L, 0, e))
            GLUX = self.W(0, [NCH, TP + 30], BF16)
            GLUS = self.W(16896, [NCH, SG_, 38], BF16)
            DG = self.W(21760, [2, CW, 128], BF16)
            GLUF = self.W(37632, [NCH, 30], F32)
            GLUSF = self.W(38592, [NCH, SG_, DSEQ], F32)
            GXH = [B("gxh") for _ in range(NCH)]
            GXB = [[B("gx") for _ in range(2)] for _ in range(NCH)]
            GSB = [B("gs") for _ in range(NCH)]
            DGB = [B("dg") for _ in range(2)]
            GFB = B("gluf")
            GSFB = B("glusf")
            self.new_view([b for l in GXB for b in l] + GXH + GSB + DGB + [GFB, GSFB])
            self.cp(GLUX[:, :, 0:30], self.CARRY_A[:, j], reads=[self.CAB[j]], writes=GXH)
        s, sb = self.slab(lambda d, j=j, g=g: [(0, [NCH * SG_ * 30], BF16, d["sconv"][j, g], "pool")])
        if not dry:
            src = self.slot(s, 0, [NCH, SG_, 30], BF16)
            self.cp(GLUS[:, :, :, 0:30], src, reads=[sb], writes=GSB)
            for c in range(NCH):
                fw.dma("sp", (lambda h, c=c: h.dma_start(out=d["ncs"][j, g, :, c, :, 0:22], in_=d["sconv4"][j, g, :, c, :, 8:30])),
                       sem_of=self.DRB)
        last_g = (g == G - 1)
        for sl in range(4):
            s, sb = self.slab(lambda d, j=j, sl=sl: [
                (0, [NCH, 256], BF16, d["a_w_in"][j].rearrange("(kc p) n -> p kc n", p=128)[:, :, 256 * sl:256 * sl + 256], "pool"),
                (NCH * 256 * 2, [NCH, 256], BF16, d["a_w_in"][j].rearrange("(kc p) n -> p kc n", p=128)[:, :, D + 256 * sl:D + 256 * sl + 256], "pool")])
            if dry:
                continue
            wa = self.slot(s, 0, [NCH, 256], BF16)
            wg = self.slot(s, NCH * 256 * 2, [NCH, 256], BF16)
            for cc in range(2):
                c = 2 * sl + cc
                banks = []
                for t, (c0, n) in enumerate(TILES):
                    hr = [self.HBB[kc][t] for kc in range(NCH)] + [sb]
                    ba = self.mm("mm", n, [(wa[:, kc, 128 * cc:128 * cc + 128], self.hb(kc, t)) for kc in range(NCH)], hr)
                    bg = self.mm("mm", n, [(wg[:, kc, 128 * cc:128 * cc + 128], self.hb(kc, t)) for kc in range(NCH)], hr)
                    if t == 1:
                        self._glu_tile(j, c, 0, banks[0][0], banks[0][1], GLUX, GLUS, GLUF, GLUSF, GXB, GSB, GFB, GSFB, last_g)
                    banks.append((ba, bg))
                self._glu_tile(j, c, 1, banks[1][0], banks[1][1], GLUX, GLUS, GLUF, GLUSF, GXB, GSB, GFB, GSFB, last_g)
                self._glu_tile(j, c, 2, banks[2][0], banks[2][1], GLUX, GLUS, GLUF, GLUSF, GXB, GSB, GFB, GSFB, last_g)
        if not dry:
            if last_g:
                fw.dma("sp", lambda h: h.dma_start(out=d["ncp"][j].rearrange("(c p) k -> p c k", p=128), in_=GLUF), reads=[GFB])
            fw.dma("sp", lambda h: h.dma_start(out=d["ncs"][j, g, :, :, :, 22:30], in_=GLUSF), reads=[GSFB])
            for c in range(NCH):
                db = c % 2
                for k in range(CW):
                    self.ts(DG[:, db, k], self.IDENT, self.prm(PC_AW + j * 248 + c * CW + k), ALU.mult,
                            reads=[self.CONB, self.PRMB], writes=[DGB[db]])
                for t, (c0, n) in enumerate(TILES):
                    if t < ST:
                        terms = [(DG[:, db, k], GLUX[:, c, c0 + k:c0 + k + n]) for k in range(CW)]
                        rd = [DGB[db], GXH[c], GXB[c][0]] + ([GXB[c][1]] if t == 1 else [])
                    else:
                        terms = [(DG[:, db, k], GLUS[:, c, :, k:k + DSEQ]) for k in range(CW)]
                        rd = [DGB[db], GSB[c]]
                    bank = self.mm("aux", n, terms, rd)
                    self.act(self.CO[:, c, c0:c0 + n], self.PS[bank][:, :n], AF.Identity, reads=[self.PSB[bank], self.PRMB],
                             writes=[self.COB[c][t]], bias=self.prm(PC_A + j * 32 + 0 + c))
            if not last_g:
                self.cp(self.CARRY_A[:, j], GLUX[:, :, TP:TP + 30], reads=[GXB[c][1] for c in range(NCH)], writes=[self.CAB[j]])
            for e, (e0, en, tl) in enumerate(ETILES):
                bms = {}
                bqs = {}
                for i_, t in enumerate(tl):
                    cls = "st" if i_ == 0 else "aux"
                    bms[t] = self.ps(cls)
                    bqs[t] = self.ps(cls)
                for c in range(NCH):
                    src = self.CO[:, c, e0:e0 + en]
                    cb = [self.COB[c][t] for t in tl]
                    q1 = self.sq()
                    self.act(self.SQ[q1][:, :en], src, AF.Copy, reads=cb, writes=[self.SQB[q1]])
                    q2 = self.sq()
                    self.act(self.SQ[q2][:, :en], src, AF.Square, reads=cb, writes=[self.SQB[q2]])
                    for t in tl:
                        c0, n = TILES[t]
                        for (bk, q) in ((bms[t], q1), (bqs[t], q2)):
                            out = self.PS[bk][:, :n]
                            rhs = self.SQ[q][:, c0 - e0:c0 - e0 + n]
                            self.nmm += 1
                            fw.op("pe", (lambda h, out=out, rhs=rhs, c=c: h.matmul(out, self.ONESM, rhs, start=(c == 0), stop=(c == NCH - 1))),
                                  reads=[self.SQB[q], self.CONB], writes=[self.PSB[bk]])
                for t in tl:
                    c0, n = TILES[t]
                    mean = self.MEAN[:, c0:c0 + n]
                    ms = self.MS[:, c0:c0 + n]
                    self.ts(mean, self.PS[bms[t]][:, :n], 0.0, ALU.add, reads=[self.PSB[bms[t]]], writes=[self.MEANB[t]])
                    self.stt(ms, mean, -1.0, mean, ALU.mult, ALU.mult, reads=[self.MEANB[t]], writes=[self.MSB[t]])
                    self.stt(ms, ms, LN_EPS, self.PS[bqs[t]][:, :n], ALU.add, ALU.add, reads=[self.MSB[t], self.PSB[bqs[t]]], writes=[self.MSB[t]])
                self.ln_exp_rstd(e0, en, tl)
                for c in range(NCH):
                    cc_ = self.CO[:, c, e0:e0 + en]
                    cb = [self.COB[c][t] for t in tl]
                    self.tt(cc_, cc_, self.MEAN[:, e0:e0 + en], ALU.subtract, reads=cb + [self.MEANB[t] for t in tl], writes=cb)
                    self.tt(cc_, cc_, self.R[:, e0:e0 + en], ALU.mult, reads=cb + [self.RB[t] for t in tl], writes=cb)
                    self.act(self.HB[:, c, e0:e0 + en], cc_, AF.Silu, reads=cb + [self.PRMB], writes=[self.HBB[c][t] for t in tl],
                             scale=self.prm(PC_A + j * 32 + 8 + c), bias=self.prm(PC_A + j * 32 + 16 + c))
        sl_h = []
        for sl in range(2):
            sl_h.append(self.slab(lambda d, j=j, sl=sl: [
                (0, [NCH, 512], BF16, d["a_w_out"][j].rearrange("(kc p) n -> p kc n", p=128)[:, :, 512 * sl:512 * sl + 512], "pool")], first=(sl == 0)))
        if not dry:
            self.tail_begin()
            for t, (c0, n) in enumerate(TILES):
                for sl in range(2):
                    s, sb = sl_h[sl]
                    w = self.slot(s, 0, [NCH, 512], BF16)
                    for cc in range(4):
                        c = 4 * sl + cc
                        bank = self.mm("mm", n, [(w[:, kc, 128 * cc:128 * cc + 128], self.hb(kc, t)) for kc in range(NCH)],
                                       [self.HBB[kc][t] for kc in range(NCH)] + [sb])
                        self.act(self.CO[:, c, c0:c0 + n], self.PS[bank][:, :n], AF.Identity, reads=[self.PSB[bank], self.PRMB],
                                 writes=[self.COB[c][t]], bias=self.prm(PC_A + j * 32 + 24 + c))
                        self.tail_chunk(t, c, self.PS[bank][:, :n], [self.PSB[bank], self.PRMB], bias=self.prm(PC_A + j * 32 + 24 + c))
            self.tail_end()
        self.postnorm(L, 1)

    def _glu_tile(self, j, c, t, ba, bg, GLUX, GLUS, GLUF, GLUSF, GXB, GSB, GFB, GSFB, last_g):
        c0, n = TILES[t]
        k = self.tmp()
        sg = self.TMP[k][:, :n]
        self.act(sg, self.PS[bg][:, :n], AF.Sigmoid, reads=[self.PSB[bg], self.PRMB], writes=[self.TMPB[k]],
                 bias=self.prm(PC_BIN + j * 16 + 8 + c))
        ba_col = self.prm(PC_BIN + j * 16 + c)
        if t < ST:
            self.stt(GLUX[:, c, 30 + c0:30 + c0 + n], self.PS[ba][:, :n], ba_col, sg, ALU.add, ALU.mult,
                     reads=[self.PSB[ba], self.TMPB[k], self.PRMB], writes=[GXB[c][t]])
            if last_g and t == ST - 1:
                self.stt(GLUF[:, c, :], self.PS[ba][:, n - 30:n], ba_col, sg[:, n - 30:n], ALU.add, ALU.mult,
                         reads=[self.PSB[ba], self.TMPB[k], self.PRMB], writes=[GFB])
        else:
            pa = self.PS[ba][:, :n].rearrange("p (s l) -> p s l", l=DSEQ)
            sg3 = sg.rearrange("p (s l) -> p s l", l=DSEQ)
            self.stt(GLUS[:, c, :, 30:38], pa, ba_col, sg3, ALU.add, ALU.mult,
                     reads=[self.PSB[ba], self.TMPB[k], self.PRMB], writes=[GSB[c]])
            self.stt(GLUSF[:, c], pa, ba_col, sg3, ALU.add, ALU.mult,
                     reads=[self.PSB[ba], self.TMPB[k], self.PRMB], writes=[GSFB])

    def b_mixer(self, L, g):
        j = L // 2
        fw = self.fw
        d = self.d
        dry = self.dry
        B = Buf
        XW = TP + 16
        if not dry:
            HFX = self.W(0, [NCH, XW], F32)
            HFS = self.W(33280, [NCH, SG_, 24], F32)
            T1S = self.W(39424, [SG_, 24], F32)
            T2S = self.W(40192, [SG_, 24], F32)
            T1 = self.AR.at(self.mean_off, [XW], F32)
            T2 = self.AR.at(self.xtra_off, [XW], F32)
            HXH = [B("hxh") for _ in range(NCH)]
            HXB = [[B("hx") for _ in range(2)] for _ in range(NCH)]
            HSB = [B("hs") for _ in range(NCH)]
            TB = B("t12")
            TSB = B("t12s")
            self.new_view([b for l in HXB for b in l] + HXH + HSB + [TB, TSB])
            self.cp(HFX[:, :, 0:16], self.CARRY_P[:, j], reads=[self.CPB[j]], writes=HXH)
        s, sb = self.slab(lambda d, j=j, g=g: [(0, [NCH * SG_ * 15], F32, d["spool"][j, g], "pool")])
        if not dry:
            src = self.slot(s, 0, [NCH, SG_, 15], F32)
            fw.op("dve", lambda h: h.memset(HFS[:, :, :, 0:1], 0.0), writes=HSB)
            self.cp(HFS[:, :, :, 1:16], src, reads=[sb], writes=HSB)

            def dst_of(c, t):
                c0, n = TILES[t]
                if t < ST:
                    return HFX[:, c, 16 + c0:16 + c0 + n]
                return HFS[:, c, :, 16:24]

            def dstb(c, t):
                return [HXB[c][t]] if t < ST else [HSB[c]]
            def pre_b(e):
                e0, en, tl = ETILES[e]
                self.rms_stats(self.X, self.XB, e)
                for t in tl:
                    c0, n = TILES[t]
                    for c in range(NCH):
                        x = self.X[:, c, c0:c0 + n]
                        r = self.R[:, c0:c0 + n]
                        if t == ST:
                            x = x.rearrange("p (s l) -> p s l", l=DSEQ)
                            r = r.rearrange("p (s l) -> p s l", l=DSEQ)
                        self.stt(dst_of(c, t), x, self.prm(_pc_gain(L, 0, c)), r, ALU.mult, ALU.mult,
                                 reads=[self.XB[c][t], self.RB[t], self.PRMB], writes=dstb(c, t))
            self.boundary(pre_b)
            last_g = (g == G - 1)
            if last_g:
                fw.dma("sp", lambda h: h.dma_start(out=d["npp"][j].rearrange("(c p) k -> p c k", p=128), in_=HFX[:, :, XW - 15:XW]),
                       reads=[HXB[c][1] for c in range(NCH)], sem_of=HXB[0][1])
            else:
                self.cp(self.CARRY_P[:, j], HFX[:, :, XW - 16:XW], reads=[HXB[c][1] for c in range(NCH)], writes=[self.CPB[j]])
            fw.dma("sp", lambda h: h.dma_start(out=d["nps"][j, g], in_=HFS[:, :, :, 9:24]), reads=HSB, sem_of=HSB[0])
            for c in range(NCH):
                lw = c // 2 + 1
                w = 1 << lw
                z = HFX[:, c, :]
                zs = HFS[:, c]
                rdz = [HXH[c], HXB[c][0], HXB[c][1]]
                cur, curs = z, zs
                bufs = [T1, T2]
                bufss = [T1S, T2S]
                for st in range(lw):
                    sh = 1 << st
                    o = bufs[st % 2]
                    os_ = bufss[st % 2]
                    lo = 2 * sh - 1
                    self.tt(o[:, lo:XW], cur[:, lo:XW], cur[:, lo - sh:XW - sh], ALU.add, reads=rdz + [TB], writes=[TB])
                    self.tt(os_[:, :, lo:24], curs[:, :, lo:24], curs[:, :, lo - sh:24 - sh], ALU.add, reads=[HSB[c], TSB], writes=[TSB])
                    cur, curs = o, os_
                inv = 1.0 / w
                self.stt(self.HB[:, c, 0:TP], cur[:, 16:XW], inv, z[:, 16:XW], ALU.mult, ALU.subtract,
                         reads=rdz + [TB], writes=[self.HBB[c][0], self.HBB[c][1]])
                if g == 0:
                    k = self.tmp()
                    tm = self.TMP[k][:, 0:w - 1]
                    self.tt(tm, cur[:, 16:16 + w - 1], self.RC[:, 0:w - 1], ALU.mult, reads=[TB, self.CONB], writes=[self.TMPB[k]])
                    self.tt(self.HB[:, c, 0:w - 1], tm, z[:, 16:16 + w - 1], ALU.subtract, reads=[self.TMPB[k]] + rdz, writes=[self.HBB[c][0]])
                self.stt(self.HB[:, c, TP:TG].rearrange("p (s l) -> p s l", l=DSEQ), curs[:, :, 16:24], inv, zs[:, :, 16:24],
                         ALU.mult, ALU.subtract, reads=[HSB[c], TSB], writes=[self.HBB[c][ST]])
        s, sb = self.slab(lambda d, j=j: [(0, [NCH, 256], BF16, d["p_w_group"][j].rearrange("g (kc p) n -> p (g kc) n", p=128), "pool")])
        if not dry:
            w = self.slot(s, 0, [NCH, 256], BF16)
            self.tail_begin()
            for t, (c0, n) in enumerate(TILES):
                for oc in range(NCH):
                    gi, oh = oc // 2, oc % 2
                    bank = self.mm("mm", n, [(w[:, 2 * gi + kc, 128 * oh:128 * oh + 128], self.hb(2 * gi + kc, t)) for kc in range(2)],
                                   [self.HBB[2 * gi][t], self.HBB[2 * gi + 1][t], sb])
                    self.act(self.CO[:, oc, c0:c0 + n], self.PS[bank][:, :n], AF.Copy, reads=[self.PSB[bank], self.PRMB],
                             writes=[self.COB[oc][t]], scale=self.prm(PC_PS + j * 8 + oc))
                    self.tail_chunk(t, oc, self.PS[bank][:, :n], [self.PSB[bank], self.PRMB], scale=self.prm(PC_PS + j * 8 + oc))
            self.tail_end()
        self.postnorm(L, 1)

    def cross(self, L, g):
        fw = self.fw
        d = self.d
        dry = self.dry
        B = Buf
        SC = 1.0 / 16.0
        if not dry:
            Q = self.W(0, [NCH, TG], BF16)
            E = self.W(17408, [NH, 2, 512], BF16)
            RS = self.W(25600, [NH, 512], F32)
            ES = self.W(33792, [2, NH, TS], BF16)
            RSS = self.W(34816, [NH, TS], F32)
            MEMN = self.W(35840, [NCH, NMEM], BF16)
            MSM = self.W(39936, [NMEM], F32)
            QB = [[B("q") for _ in range(NT)] for _ in range(NCH)]
            EB = [B("e") for _ in range(NH)]
            RSB = [B("rs") for _ in range(NH)]
            ESB = B("es")
            RSSB = B("rss")
            MNB = B("memn")
            MSMB = B("msm")
            self.new_view([b for l in QB for b in l] + EB + RSB + [ESB, RSSB, MNB, MSMB])
        s, sb = self.slab(lambda d: [(0, [NCH, NMEM], F32, d["memT"].rearrange("(c p) m -> p c m", p=128), "pool")])
        if not dry:
            MEMT = self.slot(s, 0, [NCH, NMEM], F32)
            bank = self.ps("st")
            for c in range(NCH):
                q = self.sq()
                self.act(self.SQ[q][:, :NMEM], MEMT[:, c], AF.Square, reads=[sb], writes=[self.SQB[q]])
                self.nmm += 1
                fw.op("pe", (lambda h, bank=bank, q=q, c=c: h.matmul(self.PS[bank][:, :NMEM], self.ONESM, self.SQ[q][:, :NMEM], start=(c == 0), stop=(c == NCH - 1))),
                      reads=[self.SQB[q], self.CONB], writes=[self.PSB[bank]])
            self.ts(MSM, self.PS[bank][:, :NMEM], RMS_EPS, ALU.add, reads=[self.PSB[bank]], writes=[MSMB])
            self.act(MSM, MSM, AF.Ln, reads=[MSMB], writes=[MSMB])
            self.act(MSM, MSM, AF.Exp, reads=[MSMB], writes=[MSMB], scale=-0.5)
            for c in range(NCH):
                self.stt(MEMN[:, c], MEMT[:, c], self.prm(_pc_gain(L, 6, c)), MSM, ALU.mult, ALU.mult, reads=[sb, MSMB, self.PRMB], writes=[MNB])
        import os
        CC = int(os.environ.get("CROSS_CUT", "99"))
        for sl in range(4):
            if CC <= 1:
                break
            s, sb = self.slab(lambda d, L=L, sl=sl: [
                (0, [NCH, 512], BF16, d["c_w_kv"][L].rearrange("(kc p) n -> p kc n", p=128)[:, :, 512 * sl:512 * sl + 512], "pool")])
            if dry:
                continue
            w = self.slot(s, 0, [NCH, 512], BF16)
            if sl < 2:
                for cc in range(4):
                    c = 4 * sl + cc
                    bank = self.mm("mm", NMEM, [(w[:, kc, 128 * cc:128 * cc + 128], MEMN[:, kc]) for kc in range(NCH)], [MNB, sb])
                    self.act(self.KT[:, c], self.PS[bank][:, :NMEM], AF.Copy, reads=[self.PSB[bank]], writes=[self.KTB])
            if sl >= 2 or g == 0:
                for mc in range(2):
                    bank = self.mm("mm", 512, [(MEMN[:, kc, 128 * mc:128 * mc + 128], w[:, kc]) for kc in range(NCH)], [MNB, sb])
                    if sl >= 2:
                        self.act(self.VV[:, mc, 512 * (sl - 2):512 * (sl - 2) + 512], self.PS[bank][:, :], AF.Copy, reads=[self.PSB[bank]], writes=[self.VVB])
                    if g == 0 and int(os.environ.get("NO_STG", "0")) == 0:
                        k = self.stg()
                        self.act(self.STG[k], self.PS[bank][:, :], AF.Copy, reads=[self.PSB[bank]], writes=[self.STGB[k]])
                        dst = d["mk"] if sl < 2 else d["mv"]
                        col = 512 * (sl % 2)
                        if int(os.environ.get("NO_STGDMA", "0")):
                            continue
                        fw.dma(os.environ.get("STG_Q", "sp"), (lambda h, dst=dst, mc=mc, col=col, k=k: h.dma_start(out=dst[L, 128 * mc:128 * mc + 128, col:col + 512], in_=self.STG[k])),
                               reads=[self.STGB[k]])
        self.boundary(lambda e: self.pre_e_hb(L, 2, e))
        for sl in range(2):
            if CC <= 2:
                break
            s, sb = self.slab(lambda d, L=L, sl=sl: [
                (0, [NCH, 512], BF16, d["c_w_q"][L].rearrange("(kc p) n -> p kc n", p=128)[:, :, 512 * sl:512 * sl + 512], "pool")])
            if dry:
                continue
            w = self.slot(s, 0, [NCH, 512], BF16)
            for t, (c0, n) in enumerate(TILES):
                for cc in range(4):
                    c = 4 * sl + cc
                    bank = self.mm("mm", n, [(w[:, kc, 128 * cc:128 * cc + 128], self.hb(kc, t)) for kc in range(NCH)],
                                   [self.HBB[kc][t] for kc in range(NCH)] + [sb])
                    self.act(Q[:, c, c0:c0 + n], self.PS[bank][:, :n], AF.Copy, reads=[self.PSB[bank]], writes=[QB[c][t]])
        if not dry and CC > 3:
            for t in range(ST):
                c0, n = TILES[t]
                for hh in range(NH):
                    for mc in range(2):
                        bank = self.mm("mm", n, [(self.KT[:, 2 * hh + dc, 128 * mc:128 * mc + 128], Q[:, 2 * hh + dc, c0:c0 + n]) for dc in range(2)],
                                       [self.KTB, QB[2 * hh][t], QB[2 * hh + 1][t]])
                        self.act(E[:, hh, mc, :n], self.PS[bank][:, :n], AF.Exp, reads=[self.PSB[bank]], writes=[EB[hh]], scale=SC)
                for hh in range(NH):
                    bs = self.mm("st", n, [(self.ONES1, E[:, hh, mc, :n]) for mc in range(2)], [EB[hh], self.CONB])
                    self.act(RS[:, hh, :n], self.PS[bs][:, :n], AF.Ln, reads=[self.PSB[bs]], writes=[RSB[hh]])
                    self.act(RS[:, hh, :n], RS[:, hh, :n], AF.Exp, reads=[RSB[hh]], writes=[RSB[hh]], scale=-1.0)
                    for dc in range(2):
                        c = 2 * hh + dc
                        bo = self.mm("aux", n, [(self.VV[:, mc, 128 * c:128 * c + 128], E[:, hh, mc, :n]) for mc in range(2)], [self.VVB, EB[hh]])
                        self.tt(self.HB[:, c, c0:c0 + n], self.PS[bo][:, :n], RS[:, hh, :n], ALU.mult,
                                reads=[self.PSB[bo], RSB[hh]], writes=[self.HBB[c][t]])
        c0, n = TILES[ST]
        bsc = self.ps("mm")
        if CC <= 4:
            self.postnorm(L, 3)
            return
        for sq_ in range(SG_):
            if sq_ % 2 == 0:
                s, sb = self.slab(lambda d, L=L, g=g, sq_=sq_: [
                    (0, [NCH, NMEM], BF16, d["kT"][L, g * SG_ + sq_].rearrange("(c p) m -> p c m", p=128), "pool"),
                    (4096, [NCH, NMEM], BF16, d["kT"][L, g * SG_ + sq_ + 1].rearrange("(c p) m -> p c m", p=128), "pool")])
            if dry:
                continue
            kt = self.slot(s, 4096 * (sq_ % 2), [NCH, NMEM], BF16)

            def fn(h, kt=kt, sq_=sq_, c0=c0, bsc=bsc):
                first = None
                ins = None
                for hh in range(NH):
                    for mc in range(2):
                        col = mc * NH * TS + hh * TS + DSEQ * sq_
                        for dc in range(2):
                            ins = h.matmul(self.PS[bsc][:, col:col + DSEQ], kt[:, 2 * hh + dc, 128 * mc:128 * mc + 128],
                                           Q[:, 2 * hh + dc, c0 + DSEQ * sq_:c0 + DSEQ * sq_ + DSEQ], start=(dc == 0), stop=(dc == 1))
                            if first is None:
                                first = ins
                return first, ins
            self.nmm += 16
            fw.op("pe", fn, reads=[sb] + [QB[c][ST] for c in range(NCH)], writes=[self.PSB[bsc]])
        if not dry:
            self.act(ES, self.PS[bsc][:, :].rearrange("p (a b c) -> p a b c", a=2, b=NH), AF.Exp, reads=[self.PSB[bsc]], writes=[ESB], scale=SC)
            bs = self.mm("st", NH * TS, [(self.ONES1, ES[:, mc].rearrange("p a b -> p (a b)")) for mc in range(2)], [ESB, self.CONB])
            self.act(RSS, self.PS[bs][:, :NH * TS].rearrange("p (a b) -> p a b", a=NH), AF.Ln, reads=[self.PSB[bs]], writes=[RSSB])
            self.act(RSS, RSS, AF.Exp, reads=[RSSB], writes=[RSSB], scale=-1.0)
        bo = self.ps("aux")
        for sq_ in range(SG_):
            if sq_ % 2 == 0:
                s, sb = self.slab(lambda d, L=L, g=g, sq_=sq_: [
                    (0, [2, D], BF16, d["v"][L, g * SG_ + sq_].rearrange("(mc p) f -> p mc f", p=128), "pool"),
                    (4096, [2, D], BF16, d["v"][L, g * SG_ + sq_ + 1].rearrange("(mc p) f -> p mc f", p=128), "pool")])
            if dry:
                continue
            vs = self.slot(s, 4096 * (sq_ % 2), [2, D], BF16)

            def fn2(h, vs=vs, sq_=sq_, bo=bo):
                first = None
                ins = None
                for c in range(NCH):
                    hh = c // 2
                    col = c * TS + DSEQ * sq_
                    for mc in range(2):
                        ins = h.matmul(self.PS[bo][:, col:col + DSEQ], vs[:, mc, 128 * c:128 * c + 128],
                                       ES[:, mc, hh, DSEQ * sq_:DSEQ * sq_ + DSEQ], start=(mc == 0), stop=(mc == 1))
                        if first is None:
                            first = ins
                return first, ins
            self.nmm += 16
            fw.op("pe", fn2, reads=[sb, ESB], writes=[self.PSB[bo]])
        if not dry:
            for c in range(NCH):
                self.tt(self.HB[:, c, c0:c0 + n], self.PS[bo][:, c * TS:c * TS + TS], RSS[:, c // 2], ALU.mult,
                        reads=[self.PSB[bo], RSSB], writes=[self.HBB[c][ST]])
        sl_h = []
        for sl in range(2):
            sl_h.append(self.slab(lambda d, L=L, sl=sl: [
                (0, [NCH, 512], BF16, d["c_w_o"][L].rearrange("(kc p) n -> p kc n", p=128)[:, :, 512 * sl:512 * sl + 512], "pool")], first=(sl == 0)))
        if not dry:
            self.tail_begin()
            for t, (c0, n) in enumerate(TILES):
                for sl in range(2):
                    s, sb = sl_h[sl]
                    w = self.slot(s, 0, [NCH, 512], BF16)
                    for cc in range(4):
                        c = 4 * sl + cc
                        bank = self.mm("mm", n, [(w[:, kc, 128 * cc:128 * cc + 128], self.hb(kc, t)) for kc in range(NCH)],
                                       [self.HBB[kc][t] for kc in range(NCH)] + [sb])
                        self.act(self.CO[:, c, c0:c0 + n], self.PS[bank][:, :n], AF.Copy, reads=[self.PSB[bank]], writes=[self.COB[c][t]])
                        self.tail_chunk(t, c, self.PS[bank][:, :n], [self.PSB[bank]])
            self.tail_end()
        self.postnorm(L, 3)

    def ffn(self, L, g):
        fw = self.fw
        d = self.d
        dry = self.dry
        B = Buf
        NJ = NFC // 2
        last_g = (g == G - 1)
        if not dry:
            self.boundary(lambda e: self.pre_e_hb(L, 4, e))
            A = self.W(0, [NJ, TG], BF16)
            UX = self.W(23936, [2, 2, TP + 2], BF16)
            UXS = self.W(32192, [44, SG_, 10], BF16)
            DG3 = self.W(39232, [2, 6, 128], BF16)
            NF = self.W(42304, [44, 2], F32)
            NFS = self.AR.at(self.mean_off, [44, SG_, 2], F32)
            AB = [[B("a") for _ in range(NT)] for _ in range(NJ)]
            UXB = [[B("ux") for _ in range(3)] for _ in range(2)]
            UXSB = B("uxs")
            DG3B = [B("dg3") for _ in range(2)]
            NFB = B("nf")
            NFSB = B("nfs")
            self.new_view([b for l in AB for b in l] + [b for l in UXB for b in l] + [UXSB, NFB, NFSB] + DG3B)
        s, sb = self.slab(lambda d, L=L, g=g: [(0, [44 * SG_ * 2], BF16, d["sffn"][L, g], "pool")])
        if not dry:
            src = self.slot(s, 0, [44, SG_, 2], BF16)
            self.cp(UXS[:, :, :, 0:2], src, reads=[sb], writes=[UXSB])
        import os
        CUT = int(os.environ.get("FFN_CUT", "99"))
        for hf in range(2):
            if CUT <= 1:
                break
            for jj in range(NJ):
                if CUT <= 2 and jj >= 1:
                    break
                jf = NJ * hf + jj
                s, sb = self.slab(lambda d, L=L, jf=jf: [
                    (0, [NCH, 128], BF16, d["f_w_up"][L].rearrange("(kc p) n -> p kc n", p=128)[:, :, 128 * jf:128 * jf + 128], "pool"),
                    (NCH * 128 * 2, [NCH, 128], BF16, d["f_w_up"][L].rearrange("(kc p) n -> p kc n", p=128)[:, :, DFF + 128 * jf:DFF + 128 * jf + 128], "pool")])
                if dry:
                    continue
                wg = self.slot(s, 0, [NCH, 128], BF16)
                wv = self.slot(s, NCH * 128 * 2, [NCH, 128], BF16)
                ub = jf % 2
                chs = (jf, NFC + jf)
                for gv in range(2):
                    self.cp(UX[:, ub, gv, 0:2], self.CARRY_F[:, L, chs[gv]], reads=[self.CFB[L]], writes=[UXB[ub][0]])
                for t, (c0, n) in enumerate(TILES):
                    hr = [self.HBB[kc][t] for kc in range(NCH)] + [sb]
                    t0 = []
                    for gv, wsl in enumerate((wg, wv)):
                        ch = chs[gv]
                        bank = self.mm("mm6", n, [(wsl[:, kc], self.hb(kc, t)) for kc in range(NCH)], hr)
                        p = self.PS[bank][:, :n]
                        ti = self.ft()
                        T0 = self.FT[ti][:, :n]
                        w0c = self.prm(PC_FW + (L * 3 + 0) * 44 + ch)
                        w1c = self.prm(PC_FW + (L * 3 + 1) * 44 + ch)
                        w2c = self.prm(PC_FW + (L * 3 + 2) * 44 + ch)
                        self.act(T0, p, AF.Identity, reads=[self.PSB[bank], self.PRMB], writes=[self.FTB[ti]],
                                 scale=w2c, bias=self.prm(PC_FB + L * 44 + ch))
                        if t < ST:
                            self.act(UX[:, ub, gv, 2 + c0:2 + c0 + n], p, AF.Copy, reads=[self.PSB[bank]], writes=[UXB[ub][1 + t]])
                            if last_g and t == ST - 1:
                                self.act(NF[:, ch], p[:, n - 2:n], AF.Copy, reads=[self.PSB[bank]], writes=[NFB])
                            rd = [UXB[ub][0], UXB[ub][1]] + ([UXB[ub][2]] if t == 1 else [])
                            self.stt(T0, UX[:, ub, gv, c0 + 1:c0 + 1 + n], w1c, T0, ALU.mult, ALU.add, reads=rd + [self.FTB[ti], self.PRMB], writes=[self.FTB[ti]])
                            self.stt(T0, UX[:, ub, gv, c0:c0 + n], w0c, T0, ALU.mult, ALU.add, reads=rd + [self.FTB[ti], self.PRMB], writes=[self.FTB[ti]])
                        else:
                            p3 = p.rearrange("p (s l) -> p s l", l=DSEQ)
                            T3 = T0.rearrange("p (s l) -> p s l", l=DSEQ)
                            self.act(UXS[:, ch, :, 2:10], p3, AF.Copy, reads=[self.PSB[bank]], writes=[UXSB])
                            self.act(NFS[:, ch], p3[:, :, DSEQ - 2:DSEQ], AF.Copy, reads=[self.PSB[bank]], writes=[NFSB])
                            self.stt(T3, UXS[:, ch, :, 1:9], w1c, T3, ALU.mult, ALU.add, reads=[UXSB, self.FTB[ti], self.PRMB], writes=[self.FTB[ti]])
                            self.stt(T3, UXS[:, ch, :, 0:8], w0c, T3, ALU.mult, ALU.add, reads=[UXSB, self.FTB[ti], self.PRMB], writes=[self.FTB[ti]])
                        t0.append(ti)
                    tg, tv = t0
                    G_ = self.FT[tg][:, :n]
                    self.act(G_, G_, AF.Silu, reads=[self.FTB[tg]], writes=[self.FTB[tg]])
                    self.tt(A[:, jj, c0:c0 + n], self.FT[tv][:, :n], G_, ALU.mult, reads=[self.FTB[tv], self.FTB[tg]], writes=[AB[jj][t]])
                if not last_g:
                    for gv in range(2):
                        self.cp(self.CARRY_F[:, L, chs[gv]], UX[:, ub, gv, TP:TP + 2], reads=[UXB[ub][2]], writes=[self.CFB[L]])
            dn = []
            for oc2 in range(4):
                dn.append(self.slab(lambda d, L=L, hf=hf, oc2=oc2: [
                    (0, [NJ, 256], BF16, d["f_w_down"][L][NJ * 128 * hf:NJ * 128 * hf + NJ * 128, :].rearrange("(jj p) n -> p jj n", p=128)[:, :, 256 * oc2:256 * oc2 + 256], "pool")],
                    first=(hf == 0 or oc2 == 0)))
                if dry or hf == 1:
                    continue
                self._down(dn[oc2], oc2, list(range(NT)), hf, A, AB, NJ)
            if not dry and hf == 1:
                self.tail_begin()
                for t in range(NT):
                    for oc2 in range(4):
                        self._down(dn[oc2], oc2, [t], hf, A, AB, NJ, tail=True)
                self.tail_end()
        if not dry:
            if last_g:
                fw.dma("sp", lambda h: h.dma_start(out=d["nfp"][L].rearrange("(c p) k -> p c k", p=128), in_=NF), reads=[NFB])
            fw.dma("sp", lambda h: h.dma_start(out=d["nfs"][L, g], in_=NFS), reads=[NFSB])
        self.postnorm(L, 5)

    def _down(self, sh, oc2, tiles, hf, A, AB, NJ, tail=False):
        s, sb = sh
        w = self.slot(s, 0, [NJ, 256], BF16)
        for t in tiles:
            c0, n = TILES[t]
            for cc in range(2):
                c = 2 * oc2 + cc
                bank = self.mm("mm", n, [(w[:, jj, 128 * cc:128 * cc + 128], A[:, jj, c0:c0 + n]) for jj in range(NJ)],
                               [AB[jj][t] for jj in range(NJ)] + [sb])
                o = self.CO[:, c, c0:c0 + n]
                if hf == 0:
                    self.act(o, self.PS[bank][:, :n], AF.Copy, reads=[self.PSB[bank]], writes=[self.COB[c][t]])
                else:
                    self.tt(o, o, self.PS[bank][:, :n], ALU.add, reads=[self.PSB[bank], self.COB[c][t]], writes=[self.COB[c][t]])
                if tail:
                    self.tail_chunk(t, c, o, [self.COB[c][t]])

    def build(self, nstage=None, ngroups=G):
        self.init_consts()
        k = 0
        for g in range(ngroups):
            self.load_x(g)
            for L in range(DEPTH):
                for st in range(3):
                    if nstage is not None and k >= nstage:
                        continue
                    k += 1
                    self.marks.append(("g%d L%d %s" % (g, L, ("mix", "cross", "ffn")[st]), self.nmm))
                    if st == 0:
                        if L % 2 == 0:
                            self.a_mixer(L, g)
                        else:
                            self.b_mixer(L, g)
                    elif st == 1:
                        self.cross(L, g)
                    else:
                        self.ffn(L, g)
            self.boundary(None)
            self.store_x(g)
        self.marks.append(("end", self.nmm))
        if not self.dry:
            import os as _os2, json as _json
            if _os2.environ.get("KMARKS"):
                _json.dump(self.marks, open(_os2.environ["KMARKS"], "w"))
            self.fw.run()


def _declare(nc):
    d = {}

    def inp(name, shape):
        d[name] = nc.dram_tensor(name, list(shape), F32, kind="ExternalInput").ap()

    def outp(name, shape):
        d[name] = nc.dram_tensor(name, list(shape), F32, kind="ExternalOutput").ap()

    inp("xT", [D, SEQ + NSEQ * DSEQ])
    inp("memT", [D, NMEM])
    inp("sconv", [2, G, 128, NCH * SG_ * 30])
    inp("spool", [2, G, 128, NCH * SG_ * 15])
    inp("sffn", [DEPTH, G, 128, 44 * SG_ * 2])
    inp("kT", [DEPTH, NSEQ, D, NMEM])
    inp("v", [DEPTH, NSEQ, NMEM, D])
    inp("prm", [128, NPRM])
    inp("ident", [128, 128])
    inp("a_w_in", [2, D, 2 * D])
    inp("a_w_out", [2, D, D])
    inp("p_w_group", [2, 4, 256, 256])
    inp("c_w_q", [DEPTH, D, D])
    inp("c_w_kv", [DEPTH, D, 2 * D])
    inp("c_w_o", [DEPTH, D, D])
    inp("f_w_up", [DEPTH, D, F2])
    inp("f_w_down", [DEPTH, DFF, D])
    d["sconv4"] = d["sconv"].rearrange("j g p (c s k) -> j g p c s k", c=NCH, s=SG_)
    outp("yT", [D, SEQ + NSEQ * DSEQ])
    outp("ncp", [2, D, 30])
    outp("npp", [2, D, 15])
    outp("nfp", [DEPTH, F2, 2])
    outp("mk", [DEPTH, NMEM, D])
    outp("mv", [DEPTH, NMEM, D])
    outp("ncs_", [2, G, 128, NCH * SG_ * 30])
    outp("nps_", [2, G, 128, NCH * SG_ * 15])
    outp("nfs_", [DEPTH, G, 128, 44 * SG_ * 2])
    d["ncs"] = d["ncs_"].rearrange("j g p (c s k) -> j g p c s k", c=NCH, s=SG_)
    d["nps"] = d["nps_"].rearrange("j g p (c s k) -> j g p c s k", c=NCH, s=SG_)
    d["nfs"] = d["nfs_"].rearrange("l g p (c s k) -> l g p c s k", c=44, s=SG_)
    return d


_BF = None


def _bf16_dtype():
    import ml_dtypes
    return ml_dtypes.bfloat16


def build_program(nstage=None, ngroups=G):
    nc = bass.Bass("TRN2", target_bir_lowering=False)
    d = _declare(nc)
    log = []
    Builder(nc, d, True, log).build(nstage, ngroups)
    Builder(nc, d, False, log).build(nstage, ngroups)
    return nc


def _fm(vec):
    v = np.asarray(vec, np.float32)
    sh = v.shape
    v = v.reshape(sh[:-1] + (sh[-1] // 128, 128))
    return np.moveaxis(v, -1, 0)


def _pack_params(inp):
    P = np.zeros((128, NPRM), np.float32)
    P[:, 0:224] = _fm(inp["norm_gains"]).reshape(128, 224)
    P[:, PC_BIN:PC_BIN + 32] = _fm(inp["a_b_in"]).reshape(128, 32)
    for j in range(2):
        for o, nm in ((0, "a_b_dw"), (8, "a_ln_g"), (16, "a_ln_b"), (24, "a_b_out")):
            P[:, PC_A + j * 32 + o:PC_A + j * 32 + o + 8] = _fm(inp[nm][j])
    P[:, PC_PS:PC_PS + 16] = _fm(inp["p_scale"]).reshape(128, 16)
    P[:, PC_FB:PC_FB + 176] = _fm(inp["f_b_dw"]).reshape(128, 176)
    P[:, PC_FW:PC_FW + 528] = _fm(inp["f_w_dw"]).reshape(128, 528)
    aw = _fm(inp["a_w_dw"])
    P[:, PC_AW:PC_AW + 496] = np.transpose(aw, (0, 1, 3, 2)).reshape(128, 496)
    return P


def _state_fm(st, nch):
    J, S, K, F = st.shape
    v = st.reshape(J, G, SG_, K, nch, 128)
    v = np.transpose(v, (0, 1, 5, 4, 2, 3))
    return np.ascontiguousarray(v).reshape(J, G, 128, nch * SG_ * K)


def _state_back(dev, nch, K):
    J = dev.shape[0]
    v = dev.reshape(J, G, 128, nch, SG_, K)
    v = np.transpose(v, (0, 1, 4, 5, 3, 2))
    return v.reshape(J, G * SG_, K, nch * 128)


_PROG = None


def kernel(_ncore=8, _nstage=None, _ngroups=G, **inputs):
    global _PROG
    inp = {k: np.asarray(v) for k, v in inputs.items()}
    ncore = _ncore
    if _PROG is None:
        _PROG = build_program(_nstage, _ngroups)
    nc = _PROG
    prm = _pack_params(inp)
    ident = np.eye(128, dtype=np.float32)
    shared = {k: np.ascontiguousarray(inp[k], dtype=np.float32) for k in
              ("a_w_in", "a_w_out", "p_w_group", "c_w_q", "c_w_kv", "c_w_o", "f_w_up", "f_w_down")}
    in_maps = []
    import os as _os
    _same = int(_os.environ.get("SAME_DATA", "0"))
    for i_ in range(ncore):
        i = 0 if _same else i_
        sl = slice(NSEQ * i, NSEQ * i + NSEQ)
        xs = inp["x_sample"][sl].reshape(NSEQ * DSEQ, D)
        xT = np.ascontiguousarray(np.concatenate([inp["x_prompt"][i], xs], axis=0).T)
        m = {
            "xT": xT,
            "memT": np.ascontiguousarray(inp["mem_prompt"][i].T),
            "sconv": _state_fm(inp["state_conv"][:, sl], NCH),
            "spool": _state_fm(inp["state_pool"][:, sl], NCH),
            "sffn": _state_fm(inp["state_ffn"][:, sl], 44),
            "kT": np.ascontiguousarray(np.transpose(inp["cache_mem_k"][:, sl].reshape(DEPTH, NSEQ, NMEM, D), (0, 1, 3, 2))),
            "v": np.ascontiguousarray(inp["cache_mem_v"][:, sl].reshape(DEPTH, NSEQ, NMEM, D)),
            "prm": prm,
            "ident": ident,
        }
        m.update(shared)
        in_maps.append(m)
    res = run_bass_kernel_spmd(nc, in_maps, core_ids=list(range(ncore)))
    R = res.results
    y_prompt = np.stack([R[i]["yT"][:, :SEQ].T for i in range(ncore)])
    y_sample = np.concatenate([R[i]["yT"][:, SEQ:].T.reshape(NSEQ, DSEQ, D) for i in range(ncore)])
    ncp = np.stack([np.transpose(R[i]["ncp"], (0, 2, 1)) for i in range(ncore)], axis=1)
    npp = np.stack([np.transpose(R[i]["npp"], (0, 2, 1)) for i in range(ncore)], axis=1)
    nfp = np.stack([np.transpose(R[i]["nfp"], (0, 2, 1)) for i in range(ncore)], axis=1)
    mk = np.stack([R[i]["mk"] for i in range(ncore)], axis=1).reshape(DEPTH, ncore, NMEM, NH, D // NH)
    mv = np.stack([R[i]["mv"] for i in range(ncore)], axis=1).reshape(DEPTH, ncore, NMEM, NH, D // NH)
    ncs = np.concatenate([_state_back(R[i]["ncs_"], NCH, 30) for i in range(ncore)], axis=1)
    nps = np.concatenate([_state_back(R[i]["nps_"], NCH, 15) for i in range(ncore)], axis=1)
    nfs = np.concatenate([_state_back(R[i]["nfs_"], 44, 2) for i in range(ncore)], axis=1)
    f = lambda a: np.ascontiguousarray(a, dtype=np.float32)
    return (f(y_prompt), f(y_sample), f(ncp), f(npp), f(nfp), f(mk), f(mv), f(ncs), f(nps), f(nfs))
```

```python
import contextlib
import numpy as np
import concourse.bass as bass
import concourse.mybir as mybir
from concourse.bass_utils import run_bass_kernel_spmd

F32 = mybir.dt.float32
BF16 = mybir.dt.bfloat16
AF = mybir.ActivationFunctionType
ALU = mybir.AluOpType

D = 1024
NCH = 8
SEQ = 2048
DEPTH = 4
NSEQ = 16
DSEQ = 8
NMEM = 256
NH = 4
DFF = 2816
F2 = 5632
NFC = 22
CW = 31
G = 2
TP = SEQ // G
SG_ = NSEQ // G
TS = SG_ * DSEQ
TG = TP + TS
TILES = [(0, 512), (512, 512), (1024, TS)]
NT = len(TILES)
ETILES = [(0, 512, (0,)), (512, TG - 512, (1, 2))]
ST = 2
RMS_EPS = 1e-6
LN_EPS = 1e-5
NSLOT = 4
SLOT_BYTES = 8192

def _pc_gain(i, k, c): return (i * 7 + k) * 8 + c
PC_BIN = 224
PC_A = 256
PC_PS = 320
PC_FB = 336
PC_FW = 512
PC_AW = 1040
NPRM = 1536


class Buf:
    __slots__ = ("name", "lw", "rd", "dsem", "dcnt")

    def __init__(self, name):
        self.name = name
        self.lw = None
        self.rd = {}
        self.dsem = None
        self.dcnt = 0


class Eng:
    def __init__(self, name):
        self.name = name
        self.ops = []
        self.count = 0
        self.seen = {}
        self.semkey = "E_" + name


class FW:
    def __init__(self, nc, dry=False):
        self.nc = nc
        self.dry = dry
        self.eng = {n: Eng(n) for n in ("pe", "act", "dve", "pool", "sp")}
        self.semkeys = [e.semkey for e in self.eng.values()]
        self.ndsem = 0
        self.all_dma = {}

    def _deps(self, e, reads, writes):
        need = {}
        for b in reads:
            if b.lw is not None:
                k, v = b.lw
                if need.get(k, 0) < v:
                    need[k] = v
        for b in writes:
            if b.lw is not None:
                k, v = b.lw
                if need.get(k, 0) < v:
                    need[k] = v
            for k, v in b.rd.items():
                if need.get(k, 0) < v:
                    need[k] = v
        waits = []
        for k, v in need.items():
            if k == e.semkey and e.name == "pe":
                continue
            if e.seen.get(k, 0) < v:
                e.seen[k] = v
                waits.append((k, v))
        return waits

    def op(self, engname, fn, reads=(), writes=()):
        if self.dry:
            return
        e = self.eng[engname]
        waits = self._deps(e, reads, writes)
        e.count += 1
        t = (e.semkey, e.count)
        e.ops.append((fn, waits, e.semkey, 1, True))
        for b in writes:
            b.lw = t
            b.rd = {}
        for b in reads:
            if b.rd.get(t[0], 0) < t[1]:
                b.rd[t[0]] = t[1]

    def dma(self, qname, fn, reads=(), writes=(), sem_of=None):
        if self.dry:
            return
        e = self.eng[qname]
        owner = sem_of if sem_of is not None else (writes[0] if writes else reads[0])
        if owner.dsem is None:
            owner.dsem = "D_%d" % self.ndsem
            self.ndsem += 1
            self.semkeys.append(owner.dsem)
        waits = self._deps(e, reads, writes)
        owner.dcnt += 16
        t = (owner.dsem, owner.dcnt)
        e.ops.append((fn, waits, owner.dsem, 16, False))
        self.all_dma[owner.dsem] = owner.dcnt
        for b in writes:
            b.lw = t
            b.rd = {}
        for b in reads:
            if b.rd.get(t[0], 0) < t[1]:
                b.rd[t[0]] = t[1]

    def fence(self, bufs_old, bufs_new):
        need = {}
        for b in bufs_old:
            if b.lw is not None:
                k, v = b.lw
                need[k] = max(need.get(k, 0), v)
            for k, v in b.rd.items():
                need[k] = max(need.get(k, 0), v)
        for b in bufs_new:
            b.lw = None
            b.rd = dict(need)

    def run(self):
        nc = self.nc
        with contextlib.ExitStack() as st:
            sems = {}
            for k in self.semkeys:
                sems[k] = st.enter_context(nc.semaphore(k))
            block = st.enter_context(nc.Block())
            fin = self.eng["sp"]
            handles = {"pe": "tensor", "act": "scalar", "dve": "vector", "pool": "gpsimd", "sp": "sync"}

            marked = {}
            for e in self.eng.values():
                for (fn, waits, isem, iamt, attach) in e.ops:
                    for (k, v) in waits:
                        if k.startswith("E_"):
                            marked.setdefault(k, set()).add(v)
            rank = {k: {v: i + 1 for i, v in enumerate(sorted(vs))} for k, vs in marked.items()}

            def wv(k, v):
                return rank[k][v] if k.startswith("E_") else v

            def make(e):
                def body(h):
                    ordinal = 0
                    for (fn, waits, isem, iamt, attach) in e.ops:
                        if attach and waits:
                            for (k, v) in waits[:-1]:
                                h.wait_ge(sems[k], wv(k, v))
                            r = fn(h)
                            first, last = r if isinstance(r, tuple) else (r, r)
                            k, v = waits[-1]
                            first._wait_ge(sems[k], wv(k, v))
                        else:
                            for (k, v) in waits:
                                h.wait_ge(sems[k], wv(k, v))
                            r = fn(h)
                            first, last = r if isinstance(r, tuple) else (r, r)
                        if isem.startswith("E_"):
                            ordinal += 1
                            if ordinal in rank.get(isem, ()):
                                last.then_inc(sems[isem], 1)
                        else:
                            last.then_inc(sems[isem], iamt)
                    if e is fin:
                        for k, v in self.all_dma.items():
                            h.wait_ge(sems[k], v)
                return body

            for name, e in self.eng.items():
                if not e.ops and e is not fin:
                    continue
                getattr(block, handles[name])(make(e))


class Arena:
    def __init__(self, nc, nbytes):
        self.nbytes = nbytes
        self.t = nc.alloc_sbuf_tensor("arena", [128, nbytes // 2], BF16)
        self.top = 0

    def at(self, off, shape, dt):
        n = int(np.prod(shape))
        esz = 4 if dt == F32 else 2
        assert off % 4 == 0 and off + n * esz <= self.nbytes, (off, shape, self.nbytes)
        v = self.t[:, off // 2: off // 2 + n * esz // 2]
        if dt == F32:
            v = v.bitcast(F32)
        if len(shape) == 2:
            v = v.rearrange("p (a b) -> p a b", a=shape[0])
        elif len(shape) == 3:
            v = v.rearrange("p (a b c) -> p a b c", a=shape[0], b=shape[1])
        elif len(shape) == 4:
            v = v.rearrange("p (a b c d) -> p a b c d", a=shape[0], b=shape[1], c=shape[2])
        return v

    def alloc(self, shape, dt):
        n = int(np.prod(shape)) * (4 if dt == F32 else 2)
        off = self.top
        self.top = (off + n + 63) // 64 * 64
        assert self.top <= self.nbytes, ("arena overflow", self.top, self.nbytes)
        return self.at(off, shape, dt), off


class Builder:
    def __init__(self, nc, dram, dry, slab_log):
        self.nc = nc
        self.d = dram
        self.fw = FW(nc, dry=dry)
        self.dry = dry
        self.slab_log = slab_log
        self.slab_idx = 0
        self.slab_loaded = 0
        self.pending_post = None
        self.post_stats_done = False
        self.nmm = 0
        self.marks = []
        self._alloc()

    def _alloc(self):
        nc = self.nc
        self.ps_cls = {"st": [0, 1], "mm": [2, 3, 4, 5, 6, 7], "aux": [6, 7], "mm6": [2, 3, 4, 5, 6, 7]}
        self.ps_i = {"st": 0, "mm": 0, "aux": 0, "mm6": 0}
        if self.dry:
            return
        AR = Arena(nc, 211200)
        self.AR = AR
        B = Buf
        self.X, _ = AR.alloc([NCH, TG], F32)
        self.XB = [[B("x") for _ in range(NT)] for _ in range(NCH)]
        self.PRM, _ = AR.alloc([NPRM], F32)
        self.PRMB = B("prm")
        self.IDENT, _ = AR.alloc([128], BF16)
        self.ONESM, _ = AR.alloc([128], BF16)
        self.ONES1, _ = AR.alloc([128], BF16)
        self.CONB = B("const")
        self.RC, _ = AR.alloc([16], F32)
        self.CARRY_A, _ = AR.alloc([2, NCH, 30], BF16)
        self.CARRY_P, _ = AR.alloc([2, NCH, 16], F32)
        self.CARRY_F, _ = AR.alloc([DEPTH, 44, 2], BF16)
        self.CAB = [B("ca") for _ in range(2)]
        self.CPB = [B("cp") for _ in range(2)]
        self.CFB = [B("cf") for _ in range(DEPTH)]
        self.KT, _ = AR.alloc([NCH, NMEM], BF16)
        self.VV, _ = AR.alloc([2, D], BF16)
        self.KTB = B("kt")
        self.VVB = B("vv")
        self.slot_off = []
        for s in range(NSLOT):
            _, off = AR.alloc([SLOT_BYTES // 2], BF16)
            self.slot_off.append(off)
        self.SLB = [B("slot%d" % s) for s in range(NSLOT)]
        self.HB, _ = AR.alloc([NCH, TG], BF16)
        self.HBB = [[B("h") for _ in range(NT)] for _ in range(NCH)]
        self.CO, _ = AR.alloc([NCH, TG], F32)
        self.COB = [[B("co") for _ in range(NT)] for _ in range(NCH)]
        self.MS, _ = AR.alloc([TG], F32)
        self.R, _ = AR.alloc([TG], F32)
        self.MEAN, self.mean_off = AR.alloc([TG], F32)
        self.XTRA, self.xtra_off = AR.alloc([TG], F32)
        self.MSB = [B("ms") for _ in range(NT)]
        self.RB = [B("r") for _ in range(NT)]
        self.MEANB = [B("mean") for _ in range(NT)]
        self.XTRAB = B("xtra")
        self.NSQ = 4
        self.SQ = [AR.alloc([TG - 512], BF16)[0] for _ in range(self.NSQ)]
        self.SQB = [B("sq") for _ in range(self.NSQ)]
        self.sq_i = 0
        self.NTMP = 2
        self.TMP = [AR.alloc([512], F32)[0] for _ in range(self.NTMP)]
        self.TMPB = [B("tmp") for _ in range(self.NTMP)]
        self.tmp_i = 0
        self.NSTG = 2
        self.STG = [AR.alloc([512], F32)[0] for _ in range(self.NSTG)]
        self.STGB = [B("stg") for _ in range(self.NSTG)]
        self.stg_i = 0
        xt = [self.AR.at(self.xtra_off + 2048 * i, [512], F32) for i in range(2)]
        self.FT = self.TMP + self.STG + xt
        self.FTB = self.TMPB + self.STGB + [B("xt0"), B("xt1")]
        self.ft_i = 0
        self.WSIZE = 43008
        _, self.w_off = AR.alloc([self.WSIZE // 2], BF16)
        self.WB_cur = []
        self.PS = [nc.alloc_psum_tensor("ps%d" % i, [128, 512], F32) for i in range(8)]
        self.PSB = [B("ps%d" % i) for i in range(8)]
        self.DRB = B("dram_passthru")

    def W(self, off, shape, dt):
        return self.AR.at(self.w_off + off, shape, dt)

    def new_view(self, bufs):
        if self.dry:
            return
        shared = [self.MEANB[t] for t in range(NT)] + [self.XTRAB, self.FTB[4], self.FTB[5]]
        self.fw.fence(self.WB_cur + shared, list(bufs) + shared)
        self.WB_cur = list(bufs)

    def ps(self, cls):
        lst = self.ps_cls[cls]
        i = lst[self.ps_i[cls] % len(lst)]
        self.ps_i[cls] += 1
        return i

    def sq(self):
        i = self.sq_i % self.NSQ
        self.sq_i += 1
        return i

    def tmp(self):
        i = self.tmp_i % self.NTMP
        self.tmp_i += 1
        return i

    def ft(self):
        i = self.ft_i % len(self.FT)
        self.ft_i += 1
        return i

    def stg(self):
        i = self.stg_i % self.NSTG
        self.stg_i += 1
        return i

    def slab(self, spec, first=True):
        if self.dry:
            self.slab_log.append(spec)
            return 0, None
        idx = self.slab_idx
        self.slab_idx += 1
        if first:
            self.slab_base = idx
        while self.slab_loaded < min(len(self.slab_log), self.slab_base + NSLOT):
            self._load_slab(self.slab_loaded)
            self.slab_loaded += 1
        return idx % NSLOT, self.SLB[idx % NSLOT]

    def _load_slab(self, i):
        spec = self.slab_log[i]
        s = i % NSLOT
        for (boff, shape, dt, src, q) in spec(self.d):
            dst = self.AR.at(self.slot_off[s] + boff, shape, dt)
            self.fw.dma(q, (lambda h, dst=dst, src=src: h.dma_start(out=dst, in_=src)), writes=[self.SLB[s]])

    def slot(self, s, boff, shape, dt):
        return self.AR.at(self.slot_off[s] + boff, shape, dt)

    def prm(self, col):
        return self.PRM[:, col:col + 1]

    def mm(self, cls_or_bank, n, terms, reads, col0=0):
        bank = cls_or_bank if isinstance(cls_or_bank, int) else self.ps(cls_or_bank)
        if self.dry:
            return bank
        out = self.PS[bank][:, col0:col0 + n]
        nt = len(terms)
        self.nmm += nt

        def fn(h):
            first = None
            ins = None
            for i, (l, r) in enumerate(terms):
                ins = h.matmul(out, l, r, start=(i == 0), stop=(i == nt - 1))
                if first is None:
                    first = ins
            return first, ins
        self.fw.op("pe", fn, reads=reads, writes=[self.PSB[bank]])
        return bank

    def act(self, out, in_, func, reads, writes, bias=None, scale=None):
        kw = {}
        if bias is not None:
            kw["bias"] = bias
        if scale is not None:
            kw["scale"] = scale
        self.fw.op("act", lambda h: h.activation(out=out, in_=in_, func=func, **kw), reads=reads, writes=writes)

    def stt(self, out, in0, scalar, in1, op0, op1, reads, writes):
        self.fw.op("dve", lambda h: h.scalar_tensor_tensor(out=out, in0=in0, scalar=scalar, in1=in1, op0=op0, op1=op1),
                   reads=reads, writes=writes)

    def tt(self, out, in0, in1, op, reads, writes):
        self.fw.op("dve", lambda h: h.tensor_tensor(out=out, in0=in0, in1=in1, op=op), reads=reads, writes=writes)

    def ttp(self, out, in0, in1, op, reads, writes):
        self.fw.op("pool", lambda h: h.tensor_tensor(out=out, in0=in0, in1=in1, op=op), reads=reads, writes=writes)

    def ts(self, out, in0, s1, op0, reads, writes):
        self.fw.op("dve", lambda h: h.tensor_scalar(out=out, in0=in0, scalar1=s1, scalar2=None, op0=op0), reads=reads, writes=writes)

    def cp(self, out, in_, reads, writes):
        self.fw.op("dve", lambda h: h.tensor_copy(out=out, in_=in_), reads=reads, writes=writes)

    def rms_stats(self, src3, srcbufs, e):
        e0, en, tl = ETILES[e]
        banks = {t: self.ps("st") for t in tl}
        for c in range(NCH):
            q = self.sq()
            self.act(self.SQ[q][:, :en], src3[:, c, e0:e0 + en], AF.Square, reads=[srcbufs[c][t] for t in tl], writes=[self.SQB[q]])
            for t in tl:
                c0, n = TILES[t]
                out = self.PS[banks[t]][:, :n]
                rhs = self.SQ[q][:, c0 - e0:c0 - e0 + n]
                self.nmm += 1
                self.fw.op("pe", (lambda h, out=out, rhs=rhs, c=c: h.matmul(out, self.ONESM, rhs, start=(c == 0), stop=(c == NCH - 1))),
                           reads=[self.SQB[q], self.CONB], writes=[self.PSB[banks[t]]])
        for t in tl:
            c0, n = TILES[t]
            self.ts(self.MS[:, c0:c0 + n], self.PS[banks[t]][:, :n], RMS_EPS, ALU.add, reads=[self.PSB[banks[t]]], writes=[self.MSB[t]])
        self.ln_exp_rstd(e0, en, tl)

    def ln_exp_rstd(self, e0, en, tl):
        self.act(self.R[:, e0:e0 + en], self.MS[:, e0:e0 + en], AF.Ln, reads=[self.MSB[t] for t in tl], writes=[self.RB[t] for t in tl])
        self.act(self.R[:, e0:e0 + en], self.R[:, e0:e0 + en], AF.Exp, reads=[self.RB[t] for t in tl], writes=[self.RB[t] for t in tl], scale=-0.5)

    def tail_begin(self):
        self._tail = {"pend": None, "banks": {}}

    def tail_chunk(self, t, c, src, reads, **kw):
        self._tail_flush()
        q = self.sq()
        n = TILES[t][1]
        self.act(self.SQ[q][:, :n], src, AF.Square, reads=reads, writes=[self.SQB[q]], **kw)
        self._tail["pend"] = (t, c, q)

    def _tail_flush(self):
        p = self._tail["pend"]
        if p is None:
            return
        t, c, q = p
        c0, n = TILES[t]
        if t not in self._tail["banks"]:
            self._tail["banks"][t] = self.ps("st")
        bank = self._tail["banks"][t]
        out = self.PS[bank][:, :n]
        rhs = self.SQ[q][:, :n]
        self.nmm += 1
        self.fw.op("pe", (lambda h, out=out, rhs=rhs, c=c: h.matmul(out, self.ONESM, rhs, start=(c == 0), stop=(c == NCH - 1))),
                   reads=[self.SQB[q], self.CONB], writes=[self.PSB[bank]])
        if c == NCH - 1:
            self.ts(self.MS[:, c0:c0 + n], self.PS[bank][:, :n], RMS_EPS, ALU.add, reads=[self.PSB[bank]], writes=[self.MSB[t]])
            for (e0, en, tl) in ETILES:
                if tl[-1] == t:
                    self.ln_exp_rstd(e0, en, tl)
        self._tail["pend"] = None

    def tail_end(self):
        self._tail_flush()
        self.post_stats_done = True

    def post_e(self, L, k, e):
        e0, en, tl = ETILES[e]
        if not self.post_stats_done:
            self.rms_stats(self.CO, self.COB, e)
        for c in range(NCH):
            o = self.CO[:, c, e0:e0 + en]
            cb = [self.COB[c][t] for t in tl]
            xb = [self.XB[c][t] for t in tl]
            self.stt(o, o, self.prm(_pc_gain(L, k, c)), self.R[:, e0:e0 + en], ALU.mult, ALU.mult,
                     reads=cb + [self.RB[t] for t in tl] + [self.PRMB], writes=cb)
            x = self.X[:, c, e0:e0 + en]
            self.tt(x, x, o, ALU.add, reads=xb + cb, writes=xb)

    def pre_e_hb(self, L, k, e):
        e0, en, tl = ETILES[e]
        self.rms_stats(self.X, self.XB, e)
        for c in range(NCH):
            self.stt(self.HB[:, c, e0:e0 + en], self.X[:, c, e0:e0 + en], self.prm(_pc_gain(L, k, c)), self.R[:, e0:e0 + en],
                     ALU.mult, ALU.mult, reads=[self.XB[c][t] for t in tl] + [self.RB[t] for t in tl] + [self.PRMB],
                     writes=[self.HBB[c][t] for t in tl])

    def boundary(self, pre_fn):
        if self.dry:
            return
        pend = self.pending_post
        self.pending_post = None
        for e in range(len(ETILES)):
            if pend is not None:
                self.post_e(pend[0], pend[1], e)
            if pre_fn is not None:
                pre_fn(e)
        self.post_stats_done = False

    def postnorm(self, L, k):
        self.pending_post = (L, k)

    def hb(self, c, t):
        c0, n = TILES[t]
        return self.HB[:, c, c0:c0 + n]

    def init_consts(self):
        if self.dry:
            return
        fw = self.fw
        d = self.d
        fw.dma("sp", lambda h: h.dma_start(out=self.PRM, in_=d["prm"][:, :]), writes=[self.PRMB])
        fw.dma("pool", lambda h: h.dma_start(out=self.IDENT, in_=d["ident"][:, :]), writes=[self.CONB])
        fw.op("pool", lambda h: h.memset(self.ONESM, 1.0 / D), writes=[self.CONB])
        fw.op("pool", lambda h: h.memset(self.ONES1, 1.0), writes=[self.CONB])
        for i in range(16):
            fw.op("pool", (lambda h, i=i: h.memset(self.RC[:, i:i + 1], 1.0 / (i + 1))), writes=[self.CONB])
        for j in range(2):
            fw.op("pool", (lambda h, j=j: h.memset(self.CARRY_A[:, j], 0.0)), writes=[self.CAB[j]])
            fw.op("pool", (lambda h, j=j: h.memset(self.CARRY_P[:, j], 0.0)), writes=[self.CPB[j]])
        for i in range(DEPTH):
            fw.op("pool", (lambda h, i=i: h.memset(self.CARRY_F[:, i], 0.0)), writes=[self.CFB[i]])

    def load_x(self, g):
        if self.dry:
            return
        xT = self.d["xT"].rearrange("(c p) t -> p c t", p=128)
        allx = [self.XB[c][t] for c in range(NCH) for t in range(NT)]
        self.fw.dma("sp", lambda h: h.dma_start(out=self.X[:, :, 0:TP], in_=xT[:, :, TP * g:TP * g + TP]),
                    writes=allx, sem_of=self.XB[0][0])
        self.fw.dma("sp", lambda h: h.dma_start(out=self.X[:, :, TP:TG], in_=xT[:, :, SEQ + TS * g:SEQ + TS * g + TS]),
                    writes=allx, sem_of=self.XB[0][0])

    def store_x(self, g):
        if self.dry:
            return
        yT = self.d["yT"].rearrange("(c p) t -> p c t", p=128)
        allx = [self.XB[c][t] for c in range(NCH) for t in range(NT)]
        self.fw.dma("sp", lambda h: h.dma_start(out=yT[:, :, TP * g:TP * g + TP], in_=self.X[:, :, 0:TP]),
                    reads=allx, sem_of=self.XB[0][0])
        self.fw.dma("sp", lambda h: h.dma_start(out=yT[:, :, SEQ + TS * g:SEQ + TS * g + TS], in_=self.X[:, :, TP:TG]),
                    reads=allx, sem_of=self.XB[0][0])

    def a_mixer(self, L, g):
        j = L // 2
        fw = self.fw
        d = self.d
        dry = self.dry
        B = Buf
        if not dry:
            self.boundary(lambda e: self.pre_e_hb(L, 0, e))
            GLUX = self.W(0, [NCH, TP + 30], BF16)
            GLUS = self.W(16896, [NCH, SG_, 38], BF16)
            DG = self.W(21760, [2, CW, 128], BF16)
            GLUF = self.W(37632, [NCH, 30], F32)
            GLUSF = self.W(38592, [NCH, SG_, DSEQ], F32)
            GXH = [B("gxh") for _ in range(NCH)]
            GXB = [[B("gx") for _ in range(2)] for _ in range(NCH)]
            GSB = [B("gs") for _ in range(NCH)]
            DGB = [B("dg") for _ in range(2)]
            GFB = B("gluf")
            GSFB = B("glusf")
            self.new_view([b for l in GXB for b in l] + GXH + GSB + DGB + [GFB, GSFB])
            self.cp(GLUX[:, :, 0:30], self.CARRY_A[:, j], reads=[self.CAB[j]], writes=GXH)
        s, sb = self.slab(lambda d, j=j, g=g: [(0, [NCH * SG_ * 30], BF16, d["sconv"][j, g], "pool")])
        if not dry:
            src = self.slot(s, 0, [NCH, SG_, 30], BF16)
            self.cp(GLUS[:, :, :, 0:30], src, reads=[sb], writes=GSB)
            for c in range(NCH):
                fw.dma("sp", (lambda h, c=c: h.dma_start(out=d["ncs"][j, g, :, c, :, 0:22], in_=d["sconv4"][j, g, :, c, :, 8:30])),
                       sem_of=self.DRB)
        last_g = (g == G - 1)
        for sl in range(4):
            s, sb = self.slab(lambda d, j=j, sl=sl: [
                (0, [NCH, 256], BF16, d["a_w_in"][j].rearrange("(kc p) n -> p kc n", p=128)[:, :, 256 * sl:256 * sl + 256], "pool"),
                (NCH * 256 * 2, [NCH, 256], BF16, d["a_w_in"][j].rearrange("(kc p) n -> p kc n", p=128)[:, :, D + 256 * sl:D + 256 * sl + 256], "pool")])
            if dry:
                continue
            wa = self.slot(s, 0, [NCH, 256], BF16)
            wg = self.slot(s, NCH * 256 * 2, [NCH, 256], BF16)
            for cc in range(2):
                c = 2 * sl + cc
                banks = []
                for t, (c0, n) in enumerate(TILES):
                    hr = [self.HBB[kc][t] for kc in range(NCH)] + [sb]
                    ba = self.mm("mm", n, [(wa[:, kc, 128 * cc:128 * cc + 128], self.hb(kc, t)) for kc in range(NCH)], hr)
                    bg = self.mm("mm", n, [(wg[:, kc, 128 * cc:128 * cc + 128], self.hb(kc, t)) for kc in range(NCH)], hr)
                    if t == 1:
                        self._glu_tile(j, c, 0, banks[0][0], banks[0][1], GLUX, GLUS, GLUF, GLUSF, GXB, GSB, GFB, GSFB, last_g)
                    banks.append((ba, bg))
                self._glu_tile(j, c, 1, banks[1][0], banks[1][1], GLUX, GLUS, GLUF, GLUSF, GXB, GSB, GFB, GSFB, last_g)
                self._glu_tile(j, c, 2, banks[2][0], banks[2][1], GLUX, GLUS, GLUF, GLUSF, GXB, GSB, GFB, GSFB, last_g)
        if not dry:
            if last_g:
                fw.dma("sp", lambda h: h.dma_start(out=d["ncp"][j].rearrange("(c p) k -> p c k", p=128), in_=GLUF), reads=[GFB])
            fw.dma("sp", lambda h: h.dma_start(out=d["ncs"][j, g, :, :, :, 22:30], in_=GLUSF), reads=[GSFB])
            for c in range(NCH):
                db = c % 2
                for k in range(CW):
                    self.ts(DG[:, db, k], self.IDENT, self.prm(PC_AW + j * 248 + c * CW + k), ALU.mult,
                            reads=[self.CONB, self.PRMB], writes=[DGB[db]])
                for t, (c0, n) in enumerate(TILES):
                    if t < ST:
                        terms = [(DG[:, db, k], GLUX[:, c, c0 + k:c0 + k + n]) for k in range(CW)]
                        rd = [DGB[db], GXH[c], GXB[c][0]] + ([GXB[c][1]] if t == 1 else [])
                    else:
                        terms = [(DG[:, db, k], GLUS[:, c, :, k:k + DSEQ]) for k in range(CW)]
                        rd = [DGB[db], GSB[c]]
                    bank = self.mm("aux", n, terms, rd)
                    self.act(self.CO[:, c, c0:c0 + n], self.PS[bank][:, :n], AF.Identity, reads=[self.PSB[bank], self.PRMB],
                             writes=[self.COB[c][t]], bias=self.prm(PC_A + j * 32 + 0 + c))
            if not last_g:
                self.cp(self.CARRY_A[:, j], GLUX[:, :, TP:TP + 30], reads=[GXB[c][1] for c in range(NCH)], writes=[self.CAB[j]])
            for e, (e0, en, tl) in enumerate(ETILES):
                bms = {}
                bqs = {}
                for i_, t in enumerate(tl):
                    cls = "st" if i_ == 0 else "aux"
                    bms[t] = self.ps(cls)
                    bqs[t] = self.ps(cls)
                for c in range(NCH):
                    src = self.CO[:, c, e0:e0 + en]
                    cb = [self.COB[c][t] for t in tl]
                    q1 = self.sq()
                    self.act(self.SQ[q1][:, :en], src, AF.Copy, reads=cb, writes=[self.SQB[q1]])
                    q2 = self.sq()
                    self.act(self.SQ[q2][:, :en], src, AF.Square, reads=cb, writes=[self.SQB[q2]])
                    for t in tl:
                        c0, n = TILES[t]
                        for (bk, q) in ((bms[t], q1), (bqs[t], q2)):
                            out = self.PS[bk][:, :n]
                            rhs = self.SQ[q][:, c0 - e0:c0 - e0 + n]
                            self.nmm += 1
                            fw.op("pe", (lambda h, out=out, rhs=rhs, c=c: h.matmul(out, self.ONESM, rhs, start=(c == 0), stop=(c == NCH - 1))),
                                  reads=[self.SQB[q], self.CONB], writes=[self.PSB[bk]])
                for t in tl:
                    c0, n = TILES[t]
                    mean = self.MEAN[:, c0:c0 + n]
                    ms = self.MS[:, c0:c0 + n]
                    self.ts(mean, self.PS[bms[t]][:, :n], 0.0, ALU.add, reads=[self.PSB[bms[t]]], writes=[self.MEANB[t]])
                    self.stt(ms, mean, -1.0, mean, ALU.mult, ALU.mult, reads=[self.MEANB[t]], writes=[self.MSB[t]])
                    self.stt(ms, ms, LN_EPS, self.PS[bqs[t]][:, :n], ALU.add, ALU.add, reads=[self.MSB[t], self.PSB[bqs[t]]], writes=[self.MSB[t]])
                self.ln_exp_rstd(e0, en, tl)
                for c in range(NCH):
                    cc_ = self.CO[:, c, e0:e0 + en]
                    cb = [self.COB[c][t] for t in tl]
                    self.tt(cc_, cc_, self.MEAN[:, e0:e0 + en], ALU.subtract, reads=cb + [self.MEANB[t] for t in tl], writes=cb)
                    self.tt(cc_, cc_, self.R[:, e0:e0 + en], ALU.mult, reads=cb + [self.RB[t] for t in tl], writes=cb)
                    self.act(self.HB[:, c, e0:e0 + en], cc_, AF.Silu, reads=cb + [self.PRMB], writes=[self.HBB[c][t] for t in tl],
                             scale=self.prm(PC_A + j * 32 + 8 + c), bias=self.prm(PC_A + j * 32 + 16 + c))
        sl_h = []
        for sl in range(2):
            sl_h.append(self.slab(lambda d, j=j, sl=sl: [
                (0, [NCH, 512], BF16, d["a_w_out"][j].rearrange("(kc p) n -> p kc n", p=128)[:, :, 512 * sl:512 * sl + 512], "pool")], first=(sl == 0)))
        if not dry:
            self.tail_begin()
            for t, (c0, n) in enumerate(TILES):
                for sl in range(2):
                    s, sb = sl_h[sl]
                    w = self.slot(s, 0, [NCH, 512], BF16)
                    for cc in range(4):
                        c = 4 * sl + cc
                        bank = self.mm("mm", n, [(w[:, kc, 128 * cc:128 * cc + 128], self.hb(kc, t)) for kc in range(NCH)],
                                       [self.HBB[kc][t] for kc in range(NCH)] + [sb])
                        self.act(self.CO[:, c, c0:c0 + n], self.PS[bank][:, :n], AF.Identity, reads=[self.PSB[bank], self.PRMB],
                                 writes=[self.COB[c][t]], bias=self.prm(PC_A + j * 32 + 24 + c))
                        self.tail_chunk(t, c, self.PS[bank][:, :n], [self.PSB[bank], self.PRMB], bias=self.prm(PC_A + j * 32 + 24 + c))
            self.tail_end()
        self.postnorm(L, 1)

    def _glu_tile(self, j, c, t, ba, bg, GLUX, GLUS, GLUF, GLUSF, GXB, GSB, GFB, GSFB, last_g):
        c0, n = TILES[t]
        k = self.tmp()
        sg = self.TMP[k][:, :n]
        self.act(sg, self.PS[bg][:, :n], AF.Sigmoid, reads=[self.PSB[bg], self.PRMB], writes=[self.TMPB[k]],
                 bias=self.prm(PC_BIN + j * 16 + 8 + c))
        ba_col = self.prm(PC_BIN + j * 16 + c)
        if t < ST:
            self.stt(GLUX[:, c, 30 + c0:30 + c0 + n], self.PS[ba][:, :n], ba_col, sg, ALU.add, ALU.mult,
                     reads=[self.PSB[ba], self.TMPB[k], self.PRMB], writes=[GXB[c][t]])
            if last_g and t == ST - 1:
                self.stt(GLUF[:, c, :], self.PS[ba][:, n - 30:n], ba_col, sg[:, n - 30:n], ALU.add, ALU.mult,
                         reads=[self.PSB[ba], self.TMPB[k], self.PRMB], writes=[GFB])
        else:
            pa = self.PS[ba][:, :n].rearrange("p (s l) -> p s l", l=DSEQ)
            sg3 = sg.rearrange("p (s l) -> p s l", l=DSEQ)
            self.stt(GLUS[:, c, :, 30:38], pa, ba_col, sg3, ALU.add, ALU.mult,
                     reads=[self.PSB[ba], self.TMPB[k], self.PRMB], writes=[GSB[c]])
            self.stt(GLUSF[:, c], pa, ba_col, sg3, ALU.add, ALU.mult,
                     reads=[self.PSB[ba], self.TMPB[k], self.PRMB], writes=[GSFB])

    def b_mixer(self, L, g):
        j = L // 2
        fw = self.fw
        d = self.d
        dry = self.dry
        B = Buf
        XW = TP + 16
        if not dry:
            HFX = self.W(0, [NCH, XW], F32)
            HFS = self.W(33280, [NCH, SG_, 24], F32)
            T1S = self.W(39424, [SG_, 24], F32)
            T2S = self.W(40192, [SG_, 24], F32)
            T1 = self.AR.at(self.mean_off, [XW], F32)
            T2 = self.AR.at(self.xtra_off, [XW], F32)
            HXH = [B("hxh") for _ in range(NCH)]
            HXB = [[B("hx") for _ in range(2)] for _ in range(NCH)]
            HSB = [B("hs") for _ in range(NCH)]
            TB = B("t12")
            TSB = B("t12s")
            self.new_view([b for l in HXB for b in l] + HXH + HSB + [TB, TSB])
            self.cp(HFX[:, :, 0:16], self.CARRY_P[:, j], reads=[self.CPB[j]], writes=HXH)
        s, sb = self.slab(lambda d, j=j, g=g: [(0, [NCH * SG_ * 15], F32, d["spool"][j, g], "pool")])
        if not dry:
            src = self.slot(s, 0, [NCH, SG_, 15], F32)
            fw.op("dve", lambda h: h.memset(HFS[:, :, :, 0:1], 0.0), writes=HSB)
            self.cp(HFS[:, :, :, 1:16], src, reads=[sb], writes=HSB)

            def dst_of(c, t):
                c0, n = TILES[t]
                if t < ST:
                    return HFX[:, c, 16 + c0:16 + c0 + n]
                return HFS[:, c, :, 16:24]

            def dstb(c, t):
                return [HXB[c][t]] if t < ST else [HSB[c]]
            def pre_b(e):
                e0, en, tl = ETILES[e]
                self.rms_stats(self.X, self.XB, e)
                for t in tl:
                    c0, n = TILES[t]
                    for c in range(NCH):
                        x = self.X[:, c, c0:c0 + n]
                        r = self.R[:, c0:c0 + n]
                        if t == ST:
                            x = x.rearrange("p (s l) -> p s l", l=DSEQ)
                            r = r.rearrange("p (s l) -> p s l", l=DSEQ)
                        self.stt(dst_of(c, t), x, self.prm(_pc_gain(L, 0, c)), r, ALU.mult, ALU.mult,
                                 reads=[self.XB[c][t], self.RB[t], self.PRMB], writes=dstb(c, t))
            self.boundary(pre_b)
            last_g = (g == G - 1)
            if last_g:
                fw.dma("sp", lambda h: h.dma_start(out=d["npp"][j].rearrange("(c p) k -> p c k", p=128), in_=HFX[:, :, XW - 15:XW]),
                       reads=[HXB[c][1] for c in range(NCH)], sem_of=HXB[0][1])
            else:
                self.cp(self.CARRY_P[:, j], HFX[:, :, XW - 16:XW], reads=[HXB[c][1] for c in range(NCH)], writes=[self.CPB[j]])
            fw.dma("sp", lambda h: h.dma_start(out=d["nps"][j, g], in_=HFS[:, :, :, 9:24]), reads=HSB, sem_of=HSB[0])
            for c in range(NCH):
                lw = c // 2 + 1
                w = 1 << lw
                z = HFX[:, c, :]
                zs = HFS[:, c]
                rdz = [HXH[c], HXB[c][0], HXB[c][1]]
                cur, curs = z, zs
                bufs = [T1, T2]
                bufss = [T1S, T2S]
                for st in range(lw):
                    sh = 1 << st
                    o = bufs[st % 2]
                    os_ = bufss[st % 2]
                    lo = 2 * sh - 1
                    self.tt(o[:, lo:XW], cur[:, lo:XW], cur[:, lo - sh:XW - sh], ALU.add, reads=rdz + [TB], writes=[TB])
                    self.tt(os_[:, :, lo:24], curs[:, :, lo:24], curs[:, :, lo - sh:24 - sh], ALU.add, reads=[HSB[c], TSB], writes=[TSB])
                    cur, curs = o, os_
                inv = 1.0 / w
                self.stt(self.HB[:, c, 0:TP], cur[:, 16:XW], inv, z[:, 16:XW], ALU.mult, ALU.subtract,
                         reads=rdz + [TB], writes=[self.HBB[c][0], self.HBB[c][1]])
                if g == 0:
                    k = self.tmp()
                    tm = self.TMP[k][:, 0:w - 1]
                    self.tt(tm, cur[:, 16:16 + w - 1], self.RC[:, 0:w - 1], ALU.mult, reads=[TB, self.CONB], writes=[self.TMPB[k]])
                    self.tt(self.HB[:, c, 0:w - 1], tm, z[:, 16:16 + w - 1], ALU.subtract, reads=[self.TMPB[k]] + rdz, writes=[self.HBB[c][0]])
                self.stt(self.HB[:, c, TP:TG].rearrange("p (s l) -> p s l", l=DSEQ), curs[:, :, 16:24], inv, zs[:, :, 16:24],
                         ALU.mult, ALU.subtract, reads=[HSB[c], TSB], writes=[self.HBB[c][ST]])
        s, sb = self.slab(lambda d, j=j: [(0, [NCH, 256], BF16, d["p_w_group"][j].rearrange("g (kc p) n -> p (g kc) n", p=128), "pool")])
        if not dry:
            w = self.slot(s, 0, [NCH, 256], BF16)
            self.tail_begin()
            for t, (c0, n) in enumerate(TILES):
                for oc in range(NCH):
                    gi, oh = oc // 2, oc % 2
                    bank = self.mm("mm", n, [(w[:, 2 * gi + kc, 128 * oh:128 * oh + 128], self.hb(2 * gi + kc, t)) for kc in range(2)],
                                   [self.HBB[2 * gi][t], self.HBB[2 * gi + 1][t], sb])
                    self.act(self.CO[:, oc, c0:c0 + n], self.PS[bank][:, :n], AF.Copy, reads=[self.PSB[bank], self.PRMB],
                             writes=[self.COB[oc][t]], scale=self.prm(PC_PS + j * 8 + oc))
                    self.tail_chunk(t, oc, self.PS[bank][:, :n], [self.PSB[bank], self.PRMB], scale=self.prm(PC_PS + j * 8 + oc))
            self.tail_end()
        self.postnorm(L, 1)

    def cross(self, L, g):
        fw = self.fw
        d = self.d
        dry = self.dry
        B = Buf
        SC = 1.0 / 16.0
        if not dry:
            Q = self.W(0, [NCH, TG], BF16)
            E = self.W(17408, [NH, 2, 512], BF16)
            RS = self.W(25600, [NH, 512], F32)
            ES = self.W(33792, [2, NH, TS], BF16)
            RSS = self.W(34816, [NH, TS], F32)
            MEMN = self.W(35840, [NCH, NMEM], BF16)
            MSM = self.W(39936, [NMEM], F32)
            QB = [[B("q") for _ in range(NT)] for _ in range(NCH)]
            EB = [B("e") for _ in range(NH)]
            RSB = [B("rs") for _ in range(NH)]
            ESB = B("es")
            RSSB = B("rss")
            MNB = B("memn")
            MSMB = B("msm")
            self.new_view([b for l in QB for b in l] + EB + RSB + [ESB, RSSB, MNB, MSMB])
        s, sb = self.slab(lambda d: [(0, [NCH, NMEM], F32, d["memT"].rearrange("(c p) m -> p c m", p=128), "pool")])
        if not dry:
            MEMT = self.slot(s, 0, [NCH, NMEM], F32)
            bank = self.ps("st")
            for c in range(NCH):
                q = self.sq()
                self.act(self.SQ[q][:, :NMEM], MEMT[:, c], AF.Square, reads=[sb], writes=[self.SQB[q]])
                self.nmm += 1
                fw.op("pe", (lambda h, bank=bank, q=q, c=c: h.matmul(self.PS[bank][:, :NMEM], self.ONESM, self.SQ[q][:, :NMEM], start=(c == 0), stop=(c == NCH - 1))),
                      reads=[self.SQB[q], self.CONB], writes=[self.PSB[bank]])
            self.ts(MSM, self.PS[bank][:, :NMEM], RMS_EPS, ALU.add, reads=[self.PSB[bank]], writes=[MSMB])
            self.act(MSM, MSM, AF.Ln, reads=[MSMB], writes=[MSMB])
            self.act(MSM, MSM, AF.Exp, reads=[MSMB], writes=[MSMB], scale=-0.5)
            for c in range(NCH):
                self.stt(MEMN[:, c], MEMT[:, c], self.prm(_pc_gain(L, 6, c)), MSM, ALU.mult, ALU.mult, reads=[sb, MSMB, self.PRMB], writes=[MNB])
        import os
        CC = int(os.environ.get("CROSS_CUT", "99"))
        for sl in range(4):
            if CC <= 1:
                break
            s, sb = self.slab(lambda d, L=L, sl=sl: [
                (0, [NCH, 512], BF16, d["c_w_kv"][L].rearrange("(kc p) n -> p kc n", p=128)[:, :, 512 * sl:512 * sl + 512], "pool")])
            if dry:
                continue
            w = self.slot(s, 0, [NCH, 512], BF16)
            if sl < 2:
                for cc in range(4):
                    c = 4 * sl + cc
                    bank = self.mm("mm", NMEM, [(w[:, kc, 128 * cc:128 * cc + 128], MEMN[:, kc]) for kc in range(NCH)], [MNB, sb])
                    self.act(self.KT[:, c], self.PS[bank][:, :NMEM], AF.Copy, reads=[self.PSB[bank]], writes=[self.KTB])
            if sl >= 2 or g == 0:
                for mc in range(2):
                    bank = self.mm("mm", 512, [(MEMN[:, kc, 128 * mc:128 * mc + 128], w[:, kc]) for kc in range(NCH)], [MNB, sb])
                    if sl >= 2:
                        self.act(self.VV[:, mc, 512 * (sl - 2):512 * (sl - 2) + 512], self.PS[bank][:, :], AF.Copy, reads=[self.PSB[bank]], writes=[self.VVB])
                    if g == 0 and int(os.environ.get("NO_STG", "0")) == 0:
                        k = self.stg()
                        self.act(self.STG[k], self.PS[bank][:, :], AF.Copy, reads=[self.PSB[bank]], writes=[self.STGB[k]])
                        dst = d["mk"] if sl < 2 else d["mv"]
                        col = 512 * (sl % 2)
                        if int(os.environ.get("NO_STGDMA", "0")):
                            continue
                        fw.dma(os.environ.get("STG_Q", "sp"), (lambda h, dst=dst, mc=mc, col=col, k=k: h.dma_start(out=dst[L, 128 * mc:128 * mc + 128, col:col + 512], in_=self.STG[k])),
                               reads=[self.STGB[k]])
        self.boundary(lambda e: self.pre_e_hb(L, 2, e))
        for sl in range(2):
            if CC <= 2:
                break
            s, sb = self.slab(lambda d, L=L, sl=sl: [
                (0, [NCH, 512], BF16, d["c_w_q"][L].rearrange("(kc p) n -> p kc n", p=128)[:, :, 512 * sl:512 * sl + 512], "pool")])
            if dry:
                continue
            w = self.slot(s, 0, [NCH, 512], BF16)
            for t, (c0, n) in enumerate(TILES):
                for cc in range(4):
                    c = 4 * sl + cc
                    bank = self.mm("mm", n, [(w[:, kc, 128 * cc:128 * cc + 128], self.hb(kc, t)) for kc in range(NCH)],
                                   [self.HBB[kc][t] for kc in range(NCH)] + [sb])
                    self.act(Q[:, c, c0:c0 + n], self.PS[bank][:, :n], AF.Copy, reads=[self.PSB[bank]], writes=[QB[c][t]])
        if not dry and CC > 3:
            for t in range(ST):
                c0, n = TILES[t]
                for hh in range(NH):
                    for mc in range(2):
                        bank = self.mm("mm", n, [(self.KT[:, 2 * hh + dc, 128 * mc:128 * mc + 128], Q[:, 2 * hh + dc, c0:c0 + n]) for dc in range(2)],
                                       [self.KTB, QB[2 * hh][t], QB[2 * hh + 1][t]])
                        self.act(E[:, hh, mc, :n], self.PS[bank][:, :n], AF.Exp, reads=[self.PSB[bank]], writes=[EB[hh]], scale=SC)
                for hh in range(NH):
                    bs = self.mm("st", n, [(self.ONES1, E[:, hh, mc, :n]) for mc in range(2)], [EB[hh], self.CONB])
                    self.act(RS[:, hh, :n], self.PS[bs][:, :n], AF.Ln, reads=[self.PSB[bs]], writes=[RSB[hh]])
                    self.act(RS[:, hh, :n], RS[:, hh, :n], AF.Exp, reads=[RSB[hh]], writes=[RSB[hh]], scale=-1.0)
                    for dc in range(2):
                        c = 2 * hh + dc
                        bo = self.mm("aux", n, [(self.VV[:, mc, 128 * c:128 * c + 128], E[:, hh, mc, :n]) for mc in range(2)], [self.VVB, EB[hh]])
                        self.tt(self.HB[:, c, c0:c0 + n], self.PS[bo][:, :n], RS[:, hh, :n], ALU.mult,
                                reads=[self.PSB[bo], RSB[hh]], writes=[self.HBB[c][t]])
        c0, n = TILES[ST]
        bsc = self.ps("mm")
        if CC <= 4:
            self.postnorm(L, 3)
            return
        for sq_ in range(SG_):
            if sq_ % 2 == 0:
                s, sb = self.slab(lambda d, L=L, g=g, sq_=sq_: [
                    (0, [NCH, NMEM], BF16, d["kT"][L, g * SG_ + sq_].rearrange("(c p) m -> p c m", p=128), "pool"),
                    (4096, [NCH, NMEM], BF16, d["kT"][L, g * SG_ + sq_ + 1].rearrange("(c p) m -> p c m", p=128), "pool")])
            if dry:
                continue
            kt = self.slot(s, 4096 * (sq_ % 2), [NCH, NMEM], BF16)

            def fn(h, kt=kt, sq_=sq_, c0=c0, bsc=bsc):
                first = None
                ins = None
                for hh in range(NH):
                    for mc in range(2):
                        col = mc * NH * TS + hh * TS + DSEQ * sq_
                        for dc in range(2):
                            ins = h.matmul(self.PS[bsc][:, col:col + DSEQ], kt[:, 2 * hh + dc, 128 * mc:128 * mc + 128],
                                           Q[:, 2 * hh + dc, c0 + DSEQ * sq_:c0 + DSEQ * sq_ + DSEQ], start=(dc == 0), stop=(dc == 1))
                            if first is None:
                                first = ins
                return first, ins
            self.nmm += 16
            fw.op("pe", fn, reads=[sb] + [QB[c][ST] for c in range(NCH)], writes=[self.PSB[bsc]])
        if not dry:
            self.act(ES, self.PS[bsc][:, :].rearrange("p (a b c) -> p a b c", a=2, b=NH), AF.Exp, reads=[self.PSB[bsc]], writes=[ESB], scale=SC)
            bs = self.mm("st", NH * TS, [(self.ONES1, ES[:, mc].rearrange("p a b -> p (a b)")) for mc in range(2)], [ESB, self.CONB])
            self.act(RSS, self.PS[bs][:, :NH * TS].rearrange("p (a b) -> p a b", a=NH), AF.Ln, reads=[self.PSB[bs]], writes=[RSSB])
            self.act(RSS, RSS, AF.Exp, reads=[RSSB], writes=[RSSB], scale=-1.0)
        bo = self.ps("aux")
        for sq_ in range(SG_):
            if sq_ % 2 == 0:
                s, sb = self.slab(lambda d, L=L, g=g, sq_=sq_: [
                    (0, [2, D], BF16, d["v"][L, g * SG_ + sq_].rearrange("(mc p) f -> p mc f", p=128), "pool"),
                    (4096, [2, D], BF16, d["v"][L, g * SG_ + sq_ + 1].rearrange("(mc p) f -> p mc f", p=128), "pool")])
            if dry:
                continue
            vs = self.slot(s, 4096 * (sq_ % 2), [2, D], BF16)

            def fn2(h, vs=vs, sq_=sq_, bo=bo):
                first = None
                ins = None
                for c in range(NCH):
                    hh = c // 2
                    col = c * TS + DSEQ * sq_
                    for mc in range(2):
                        ins = h.matmul(self.PS[bo][:, col:col + DSEQ], vs[:, mc, 128 * c:128 * c + 128],
                                       ES[:, mc, hh, DSEQ * sq_:DSEQ * sq_ + DSEQ], start=(mc == 0), stop=(mc == 1))
                        if first is None:
                            first = ins
                return first, ins
            self.nmm += 16
            fw.op("pe", fn2, reads=[sb, ESB], writes=[self.PSB[bo]])
        if not dry:
            for c in range(NCH):
                self.tt(self.HB[:, c, c0:c0 + n], self.PS[bo][:, c * TS:c * TS + TS], RSS[:, c // 2], ALU.mult,
                        reads=[self.PSB[bo], RSSB], writes=[self.HBB[c][ST]])
        sl_h = []
        for sl in range(2):
            sl_h.append(self.slab(lambda d, L=L, sl=sl: [
                (0, [NCH, 512], BF16, d["c_w_o"][L].rearrange("(kc p) n -> p kc n", p=128)[:, :, 512 * sl:512 * sl + 512], "pool")], first=(sl == 0)))
        if not dry:
            self.tail_begin()
            for t, (c0, n) in enumerate(TILES):
                for sl in range(2):
                    s, sb = sl_h[sl]
                    w = self.slot(s, 0, [NCH, 512], BF16)
                    for cc in range(4):
                        c = 4 * sl + cc
                        bank = self.mm("mm", n, [(w[:, kc, 128 * cc:128 * cc + 128], self.hb(kc, t)) for kc in range(NCH)],
                                       [self.HBB[kc][t] for kc in range(NCH)] + [sb])
                        self.act(self.CO[:, c, c0:c0 + n], self.PS[bank][:, :n], AF.Copy, reads=[self.PSB[bank]], writes=[self.COB[c][t]])
                        self.tail_chunk(t, c, self.PS[bank][:, :n], [self.PSB[bank]])
            self.tail_end()
        self.postnorm(L, 3)

    def ffn(self, L, g):
        fw = self.fw
        d = self.d
        dry = self.dry
        B = Buf
        NJ = NFC // 2
        last_g = (g == G - 1)
        if not dry:
            self.boundary(lambda e: self.pre_e_hb(L, 4, e))
            A = self.W(0, [NJ, TG], BF16)
            UX = self.W(23936, [2, 2, TP + 2], BF16)
            UXS = self.W(32192, [44, SG_, 10], BF16)
            DG3 = self.W(39232, [2, 6, 128], BF16)
            NF = self.W(42304, [44, 2], F32)
            NFS = self.AR.at(self.mean_off, [44, SG_, 2], F32)
            AB = [[B("a") for _ in range(NT)] for _ in range(NJ)]
            UXB = [[B("ux") for _ in range(3)] for _ in range(2)]
            UXSB = B("uxs")
            DG3B = [B("dg3") for _ in range(2)]
            NFB = B("nf")
            NFSB = B("nfs")
            self.new_view([b for l in AB for b in l] + [b for l in UXB for b in l] + [UXSB, NFB, NFSB] + DG3B)
        s, sb = self.slab(lambda d, L=L, g=g: [(0, [44 * SG_ * 2], BF16, d["sffn"][L, g], "pool")])
        if not dry:
            src = self.slot(s, 0, [44, SG_, 2], BF16)
            self.cp(UXS[:, :, :, 0:2], src, reads=[sb], writes=[UXSB])
        import os
        CUT = int(os.environ.get("FFN_CUT", "99"))
        for hf in range(2):
            if CUT <= 1:
                break
            for jj in range(NJ):
                if CUT <= 2 and jj >= 1:
                    break
                jf = NJ * hf + jj
                s, sb = self.slab(lambda d, L=L, jf=jf: [
                    (0, [NCH, 128], BF16, d["f_w_up"][L].rearrange("(kc p) n -> p kc n", p=128)[:, :, 128 * jf:128 * jf + 128], "pool"),
                    (NCH * 128 * 2, [NCH, 128], BF16, d["f_w_up"][L].rearrange("(kc p) n -> p kc n", p=128)[:, :, DFF + 128 * jf:DFF + 128 * jf + 128], "pool")])
                if dry:
                    continue
                wg = self.slot(s, 0, [NCH, 128], BF16)
                wv = self.slot(s, NCH * 128 * 2, [NCH, 128], BF16)
                ub = jf % 2
                chs = (jf, NFC + jf)
                for gv in range(2):
                    self.cp(UX[:, ub, gv, 0:2], self.CARRY_F[:, L, chs[gv]], reads=[self.CFB[L]], writes=[UXB[ub][0]])
                for t, (c0, n) in enumerate(TILES):
                    hr = [self.HBB[kc][t] for kc in range(NCH)] + [sb]
                    t0 = []
                    for gv, wsl in enumerate((wg, wv)):
                        ch = chs[gv]
                        bank = self.mm("mm6", n, [(wsl[:, kc], self.hb(kc, t)) for kc in range(NCH)], hr)
                        p = self.PS[bank][:, :n]
                        ti = self.ft()
                        T0 = self.FT[ti][:, :n]
                        w0c = self.prm(PC_FW + (L * 3 + 0) * 44 + ch)
                        w1c = self.prm(PC_FW + (L * 3 + 1) * 44 + ch)
                        w2c = self.prm(PC_FW + (L * 3 + 2) * 44 + ch)
                        self.act(T0, p, AF.Identity, reads=[self.PSB[bank], self.PRMB], writes=[self.FTB[ti]],
                                 scale=w2c, bias=self.prm(PC_FB + L * 44 + ch))
                        if t < ST:
                            self.act(UX[:, ub, gv, 2 + c0:2 + c0 + n], p, AF.Copy, reads=[self.PSB[bank]], writes=[UXB[ub][1 + t]])
                            if last_g and t == ST - 1:
                                self.act(NF[:, ch], p[:, n - 2:n], AF.Copy, reads=[self.PSB[bank]], writes=[NFB])
                            rd = [UXB[ub][0], UXB[ub][1]] + ([UXB[ub][2]] if t == 1 else [])
                            self.stt(T0, UX[:, ub, gv, c0 + 1:c0 + 1 + n], w1c, T0, ALU.mult, ALU.add, reads=rd + [self.FTB[ti], self.PRMB], writes=[self.FTB[ti]])
                            self.stt(T0, UX[:, ub, gv, c0:c0 + n], w0c, T0, ALU.mult, ALU.add, reads=rd + [self.FTB[ti], self.PRMB], writes=[self.FTB[ti]])
                        else:
                            p3 = p.rearrange("p (s l) -> p s l", l=DSEQ)
                            T3 = T0.rearrange("p (s l) -> p s l", l=DSEQ)
                            self.act(UXS[:, ch, :, 2:10], p3, AF.Copy, reads=[self.PSB[bank]], writes=[UXSB])
                            self.act(NFS[:, ch], p3[:, :, DSEQ - 2:DSEQ], AF.Copy, reads=[self.PSB[bank]], writes=[NFSB])
                            self.stt(T3, UXS[:, ch, :, 1:9], w1c, T3, ALU.mult, ALU.add, reads=[UXSB, self.FTB[ti], self.PRMB], writes=[self.FTB[ti]])
                            self.stt(T3, UXS[:, ch, :, 0:8], w0c, T3, ALU.mult, ALU.add, reads=[UXSB, self.FTB[ti], self.PRMB], writes=[self.FTB[ti]])
                        t0.append(ti)
                    tg, tv = t0
                    G_ = self.FT[tg][:, :n]
                    self.act(G_, G_, AF.Silu, reads=[self.FTB[tg]], writes=[self.FTB[tg]])
                    self.tt(A[:, jj, c0:c0 + n], self.FT[tv][:, :n], G_, ALU.mult, reads=[self.FTB[tv], self.FTB[tg]], writes=[AB[jj][t]])
                if not last_g:
                    for gv in range(2):
                        self.cp(self.CARRY_F[:, L, chs[gv]], UX[:, ub, gv, TP:TP + 2], reads=[UXB[ub][2]], writes=[self.CFB[L]])
            dn = []
            for oc2 in range(4):
                dn.append(self.slab(lambda d, L=L, hf=hf, oc2=oc2: [
                    (0, [NJ, 256], BF16, d["f_w_down"][L][NJ * 128 * hf:NJ * 128 * hf + NJ * 128, :].rearrange("(jj p) n -> p jj n", p=128)[:, :, 256 * oc2:256 * oc2 + 256], "pool")],
                    first=(hf == 0 or oc2 == 0)))
                if dry or hf == 1:
                    continue
                self._down(dn[oc2], oc2, list(range(NT)), hf, A, AB, NJ)
            if not dry and hf == 1:
                self.tail_begin()
                for t in range(NT):
                    for oc2 in range(4):
                        self._down(dn[oc2], oc2, [t], hf, A, AB, NJ, tail=True)
                self.tail_end()
        if not dry:
            if last_g:
                fw.dma("sp", lambda h: h.dma_start(out=d["nfp"][L].rearrange("(c p) k -> p c k", p=128), in_=NF), reads=[NFB])
            fw.dma("sp", lambda h: h.dma_start(out=d["nfs"][L, g], in_=NFS), reads=[NFSB])
        self.postnorm(L, 5)

    def _down(self, sh, oc2, tiles, hf, A, AB, NJ, tail=False):
        s, sb = sh
        w = self.slot(s, 0, [NJ, 256], BF16)
        for t in tiles:
            c0, n = TILES[t]
            for cc in range(2):
                c = 2 * oc2 + cc
                bank = self.mm("mm", n, [(w[:, jj, 128 * cc:128 * cc + 128], A[:, jj, c0:c0 + n]) for jj in range(NJ)],
                               [AB[jj][t] for jj in range(NJ)] + [sb])
                o = self.CO[:, c, c0:c0 + n]
                if hf == 0:
                    self.act(o, self.PS[bank][:, :n], AF.Copy, reads=[self.PSB[bank]], writes=[self.COB[c][t]])
                else:
                    self.tt(o, o, self.PS[bank][:, :n], ALU.add, reads=[self.PSB[bank], self.COB[c][t]], writes=[self.COB[c][t]])
                if tail:
                    self.tail_chunk(t, c, o, [self.COB[c][t]])

    def build(self, nstage=None, ngroups=G):
        self.init_consts()
        k = 0
        for g in range(ngroups):
            self.load_x(g)
            for L in range(DEPTH):
                for st in range(3):
                    if nstage is not None and k >= nstage:
                        continue
                    k += 1
                    self.marks.append(("g%d L%d %s" % (g, L, ("mix", "cross", "ffn")[st]), self.nmm))
                    if st == 0:
                        if L % 2 == 0:
                            self.a_mixer(L, g)
                        else:
                            self.b_mixer(L, g)
                    elif st == 1:
                        self.cross(L, g)
                    else:
                        self.ffn(L, g)
            self.boundary(None)
            self.store_x(g)
        self.marks.append(("end", self.nmm))
        if not self.dry:
            import os as _os2, json as _json
            if _os2.environ.get("KMARKS"):
                _json.dump(self.marks, open(_os2.environ["KMARKS"], "w"))
            self.fw.run()


def _declare(nc):
    d = {}

    def inp(name, shape):
        d[name] = nc.dram_tensor(name, list(shape), F32, kind="ExternalInput").ap()

    def outp(name, shape):
        d[name] = nc.dram_tensor(name, list(shape), F32, kind="ExternalOutput").ap()

    inp("xT", [D, SEQ + NSEQ * DSEQ])
    inp("memT", [D, NMEM])
    inp("sconv", [2, G, 128, NCH * SG_ * 30])
    inp("spool", [2, G, 128, NCH * SG_ * 15])
    inp("sffn", [DEPTH, G, 128, 44 * SG_ * 2])
    inp("kT", [DEPTH, NSEQ, D, NMEM])
    inp("v", [DEPTH, NSEQ, NMEM, D])
    inp("prm", [128, NPRM])
    inp("ident", [128, 128])
    inp("a_w_in", [2, D, 2 * D])
    inp("a_w_out", [2, D, D])
    inp("p_w_group", [2, 4, 256, 256])
    inp("c_w_q", [DEPTH, D, D])
    inp("c_w_kv", [DEPTH, D, 2 * D])
    inp("c_w_o", [DEPTH, D, D])
    inp("f_w_up", [DEPTH, D, F2])
    inp("f_w_down", [DEPTH, DFF, D])
    d["sconv4"] = d["sconv"].rearrange("j g p (c s k) -> j g p c s k", c=NCH, s=SG_)
    outp("yT", [D, SEQ + NSEQ * DSEQ])
    outp("ncp", [2, D, 30])
    outp("npp", [2, D, 15])
    outp("nfp", [DEPTH, F2, 2])
    outp("mk", [DEPTH, NMEM, D])
    outp("mv", [DEPTH, NMEM, D])
    outp("ncs_", [2, G, 128, NCH * SG_ * 30])
    outp("nps_", [2, G, 128, NCH * SG_ * 15])
    outp("nfs_", [DEPTH, G, 128, 44 * SG_ * 2])
    d["ncs"] = d["ncs_"].rearrange("j g p (c s k) -> j g p c s k", c=NCH, s=SG_)
    d["nps"] = d["nps_"].rearrange("j g p (c s k) -> j g p c s k", c=NCH, s=SG_)
    d["nfs"] = d["nfs_"].rearrange("l g p (c s k) -> l g p c s k", c=44, s=SG_)
    return d


_BF = None


def _bf16_dtype():
    import ml_dtypes
    return ml_dtypes.bfloat16


def build_program(nstage=None, ngroups=G):
    nc = bass.Bass("TRN2", target_bir_lowering=False)
    d = _declare(nc)
    log = []
    Builder(nc, d, True, log).build(nstage, ngroups)
    Builder(nc, d, False, log).build(nstage, ngroups)
    return nc


def _fm(vec):
    v = np.asarray(vec, np.float32)
    sh = v.shape
    v = v.reshape(sh[:-1] + (sh[-1] // 128, 128))
    return np.moveaxis(v, -1, 0)


def _pack_params(inp):
    P = np.zeros((128, NPRM), np.float32)
    P[:, 0:224] = _fm(inp["norm_gains"]).reshape(128, 224)
    P[:, PC_BIN:PC_BIN + 32] = _fm(inp["a_b_in"]).reshape(128, 32)
    for j in range(2):
        for o, nm in ((0, "a_b_dw"), (8, "a_ln_g"), (16, "a_ln_b"), (24, "a_b_out")):
            P[:, PC_A + j * 32 + o:PC_A + j * 32 + o + 8] = _fm(inp[nm][j])
    P[:, PC_PS:PC_PS + 16] = _fm(inp["p_scale"]).reshape(128, 16)
    P[:, PC_FB:PC_FB + 176] = _fm(inp["f_b_dw"]).reshape(128, 176)
    P[:, PC_FW:PC_FW + 528] = _fm(inp["f_w_dw"]).reshape(128, 528)
    aw = _fm(inp["a_w_dw"])
    P[:, PC_AW:PC_AW + 496] = np.transpose(aw, (0, 1, 3, 2)).reshape(128, 496)
    return P


def _state_fm(st, nch):
    J, S, K, F = st.shape
    v = st.reshape(J, G, SG_, K, nch, 128)
    v = np.transpose(v, (0, 1, 5, 4, 2, 3))
    return np.ascontiguousarray(v).reshape(J, G, 128, nch * SG_ * K)


def _state_back(dev, nch, K):
    J = dev.shape[0]
    v = dev.reshape(J, G, 128, nch, SG_, K)
    v = np.transpose(v, (0, 1, 4, 5, 3, 2))
    return v.reshape(J, G * SG_, K, nch * 128)


_PROG = None


def kernel(_ncore=8, _nstage=None, _ngroups=G, **inputs):
    global _PROG
    inp = {k: np.asarray(v) for k, v in inputs.items()}
    ncore = _ncore
    if _PROG is None:
        _PROG = build_program(_nstage, _ngroups)
    nc = _PROG
    prm = _pack_params(inp)
    ident = np.eye(128, dtype=np.float32)
    shared = {k: np.ascontiguousarray(inp[k], dtype=np.float32) for k in
              ("a_w_in", "a_w_out", "p_w_group", "c_w_q", "c_w_kv", "c_w_o", "f_w_up", "f_w_down")}
    in_maps = []
    import os as _os
    _same = int(_os.environ.get("SAME_DATA", "0"))
    for i_ in range(ncore):
        i = 0 if _same else i_
        sl = slice(NSEQ * i, NSEQ * i + NSEQ)
        xs = inp["x_sample"][sl].reshape(NSEQ * DSEQ, D)
        xT = np.ascontiguousarray(np.concatenate([inp["x_prompt"][i], xs], axis=0).T)
        m = {
            "xT": xT,
            "memT": np.ascontiguousarray(inp["mem_prompt"][i].T),
            "sconv": _state_fm(inp["state_conv"][:, sl], NCH),
            "spool": _state_fm(inp["state_pool"][:, sl], NCH),
            "sffn": _state_fm(inp["state_ffn"][:, sl], 44),
            "kT": np.ascontiguousarray(np.transpose(inp["cache_mem_k"][:, sl].reshape(DEPTH, NSEQ, NMEM, D), (0, 1, 3, 2))),
            "v": np.ascontiguousarray(inp["cache_mem_v"][:, sl].reshape(DEPTH, NSEQ, NMEM, D)),
            "prm": prm,
            "ident": ident,
        }
        m.update(shared)
        in_maps.append(m)
    res = run_bass_kernel_spmd(nc, in_maps, core_ids=list(range(ncore)))
    R = res.results
    y_prompt = np.stack([R[i]["yT"][:, :SEQ].T for i in range(ncore)])
    y_sample = np.concatenate([R[i]["yT"][:, SEQ:].T.reshape(NSEQ, DSEQ, D) for i in range(ncore)])
    ncp = np.stack([np.transpose(R[i]["ncp"], (0, 2, 1)) for i in range(ncore)], axis=1)
    npp = np.stack([np.transpose(R[i]["npp"], (0, 2, 1)) for i in range(ncore)], axis=1)
    nfp = np.stack([np.transpose(R[i]["nfp"], (0, 2, 1)) for i in range(ncore)], axis=1)
    mk = np.stack([R[i]["mk"] for i in range(ncore)], axis=1).reshape(DEPTH, ncore, NMEM, NH, D // NH)
    mv = np.stack([R[i]["mv"] for i in range(ncore)], axis=1).reshape(DEPTH, ncore, NMEM, NH, D // NH)
    ncs = np.concatenate([_state_back(R[i]["ncs_"], NCH, 30) for i in range(ncore)], axis=1)
    nps = np.concatenate([_state_back(R[i]["nps_"], NCH, 15) for i in range(ncore)], axis=1)
    nfs = np.concatenate([_state_back(R[i]["nfs_"], 44, 2) for i in range(ncore)], axis=1)
    f = lambda a: np.ascontiguousarray(a, dtype=np.float32)
    return (f(y_prompt), f(y_sample), f(ncp), f(npp), f(nfp), f(mk), f(mv), f(ncs), f(nps), f(nfs))
```

```python
import contextlib
import numpy as np
import concourse.bass as bass
import concourse.mybir as mybir
from concourse.bass_utils import run_bass_kernel_spmd

F32 = mybir.dt.float32
BF16 = mybir.dt.bfloat16
AF = mybir.ActivationFunctionType
ALU = mybir.AluOpType

D = 1024
NCH = 8
SEQ = 2048
DEPTH = 4
NSEQ = 16
DSEQ = 8
NMEM = 256
NH = 4
DFF = 2816
F2 = 5632
NFC = 22
CW = 31
G = 2
TP = SEQ // G
SG_ = NSEQ // G
TS = SG_ * DSEQ
TG = TP + TS
TILES = [(0, 512), (512, 512), (1024, TS)]
NT = len(TILES)
ETILES = [(0, 512, (0,)), (512, TG - 512, (1, 2))]
ST = 2
RMS_EPS = 1e-6
LN_EPS = 1e-5
NSLOT = 4
SLOT_BYTES = 8192

def _pc_gain(i, k, c): return (i * 7 + k) * 8 + c
PC_BIN = 224
PC_A = 256
PC_PS = 320
PC_FB = 336
PC_FW = 512
PC_AW = 1040
NPRM = 1536


class Buf:
    __slots__ = ("name", "lw", "rd", "dsem", "dcnt")

    def __init__(self, name):
        self.name = name
        self.lw = None
        self.rd = {}
        self.dsem = None
        self.dcnt = 0


class Eng:
    def __init__(self, name):
        self.name = name
        self.ops = []
        self.count = 0
        self.seen = {}
        self.semkey = "E_" + name


class FW:
    def __init__(self, nc, dry=False):
        self.nc = nc
        self.dry = dry
        self.eng = {n: Eng(n) for n in ("pe", "act", "dve", "pool", "sp")}
        self.semkeys = [e.semkey for e in self.eng.values()]
        self.ndsem = 0
        self.all_dma = {}

    def _deps(self, e, reads, writes):
        need = {}
        for b in reads:
            if b.lw is not None:
                k, v = b.lw
                if need.get(k, 0) < v:
                    need[k] = v
        for b in writes:
            if b.lw is not None:
                k, v = b.lw
                if need.get(k, 0) < v:
                    need[k] = v
            for k, v in b.rd.items():
                if need.get(k, 0) < v:
                    need[k] = v
        waits = []
        for k, v in need.items():
            if k == e.semkey and e.name == "pe":
                continue
            if e.seen.get(k, 0) < v:
                e.seen[k] = v
                waits.append((k, v))
        return waits

    def op(self, engname, fn, reads=(), writes=()):
        if self.dry:
            return
        e = self.eng[engname]
        waits = self._deps(e, reads, writes)
        e.count += 1
        t = (e.semkey, e.count)
        e.ops.append((fn, waits, e.semkey, 1, True))
        for b in writes:
            b.lw = t
            b.rd = {}
        for b in reads:
            if b.rd.get(t[0], 0) < t[1]:
                b.rd[t[0]] = t[1]

    def dma(self, qname, fn, reads=(), writes=(), sem_of=None):
        if self.dry:
            return
        e = self.eng[qname]
        owner = sem_of if sem_of is not None else (writes[0] if writes else reads[0])
        if owner.dsem is None:
            owner.dsem = "D_%d" % self.ndsem
            self.ndsem += 1
            self.semkeys.append(owner.dsem)
        waits = self._deps(e, reads, writes)
        owner.dcnt += 16
        t = (owner.dsem, owner.dcnt)
        e.ops.append((fn, waits, owner.dsem, 16, False))
        self.all_dma[owner.dsem] = owner.dcnt
        for b in writes:
            b.lw = t
            b.rd = {}
        for b in reads:
            if b.rd.get(t[0], 0) < t[1]:
                b.rd[t[0]] = t[1]

    def fence(self, bufs_old, bufs_new):
        need = {}
        for b in bufs_old:
            if b.lw is not None:
                k, v = b.lw
                need[k] = max(need.get(k, 0), v)
            for k, v in b.rd.items():
                need[k] = max(need.get(k, 0), v)
        for b in bufs_new:
            b.lw = None
            b.rd = dict(need)

    def run(self):
        nc = self.nc
        with contextlib.ExitStack() as st:
            sems = {}
            for k in self.semkeys:
                sems[k] = st.enter_context(nc.semaphore(k))
            block = st.enter_context(nc.Block())
            fin = self.eng["sp"]
            handles = {"pe": "tensor", "act": "scalar", "dve": "vector", "pool": "gpsimd", "sp": "sync"}

            marked = {}
            for e in self.eng.values():
                for (fn, waits, isem, iamt, attach) in e.ops:
                    for (k, v) in waits:
                        if k.startswith("E_"):
                            marked.setdefault(k, set()).add(v)
            rank = {k: {v: i + 1 for i, v in enumerate(sorted(vs))} for k, vs in marked.items()}

            def wv(k, v):
                return rank[k][v] if k.startswith("E_") else v

            def make(e):
                def body(h):
                    ordinal = 0
                    for (fn, waits, isem, iamt, attach) in e.ops:
                        if attach and waits:
                            for (k, v) in waits[:-1]:
                                h.wait_ge(sems[k], wv(k, v))
                            r = fn(h)
                            first, last = r if isinstance(r, tuple) else (r, r)
                            k, v = waits[-1]
                            first._wait_ge(sems[k], wv(k, v))
                        else:
                            for (k, v) in waits:
                                h.wait_ge(sems[k], wv(k, v))
                            r = fn(h)
                            first, last = r if isinstance(r, tuple) else (r, r)
                        if isem.startswith("E_"):
                            ordinal += 1
                            if ordinal in rank.get(isem, ()):
                                last.then_inc(sems[isem], 1)
                        else:
                            last.then_inc(sems[isem], iamt)
                    if e is fin:
                        for k, v in self.all_dma.items():
                            h.wait_ge(sems[k], v)
                return body

            for name, e in self.eng.items():
                if not e.ops and e is not fin:
                    continue
                getattr(block, handles[name])(make(e))


class Arena:
    def __init__(self, nc, nbytes):
        self.nbytes = nbytes
        self.t = nc.alloc_sbuf_tensor("arena", [128, nbytes // 2], BF16)
        self.top = 0

    def at(self, off, shape, dt):
        n = int(np.prod(shape))
        esz = 4 if dt == F32 else 2
        assert off % 4 == 0 and off + n * esz <= self.nbytes, (off, shape, self.nbytes)
        v = self.t[:, off // 2: off // 2 + n * esz // 2]
        if dt == F32:
            v = v.bitcast(F32)
        if len(shape) == 2:
            v = v.rearrange("p (a b) -> p a b", a=shape[0])
        elif len(shape) == 3:
            v = v.rearrange("p (a b c) -> p a b c", a=shape[0], b=shape[1])
        elif len(shape) == 4:
            v = v.rearrange("p (a b c d) -> p a b c d", a=shape[0], b=shape[1], c=shape[2])
        return v

    def alloc(self, shape, dt):
        n = int(np.prod(shape)) * (4 if dt == F32 else 2)
        off = self.top
        self.top = (off + n + 63) // 64 * 64
        assert self.top <= self.nbytes, ("arena overflow", self.top, self.nbytes)
        return self.at(off, shape, dt), off


class Builder:
    def __init__(self, nc, dram, dry, slab_log):
        self.nc = nc
        self.d = dram
        self.fw = FW(nc, dry=dry)
        self.dry = dry
        self.slab_log = slab_log
        self.slab_idx = 0
        self.slab_loaded = 0
        self.pending_post = None
        self.post_stats_done = False
        self.nmm = 0
        self.marks = []
        self._alloc()

    def _alloc(self):
        nc = self.nc
        self.ps_cls = {"st": [0, 1], "mm": [2, 3, 4, 5], "aux": [6, 7], "mm6": [2, 3, 4, 5, 6, 7]}
        self.ps_i = {"st": 0, "mm": 0, "aux": 0, "mm6": 0}
        if self.dry:
            return
        AR = Arena(nc, 211200)
        self.AR = AR
        B = Buf
        self.X, _ = AR.alloc([NCH, TG], F32)
        self.XB = [[B("x") for _ in range(NT)] for _ in range(NCH)]
        self.PRM, _ = AR.alloc([NPRM], F32)
        self.PRMB = B("prm")
        self.IDENT, _ = AR.alloc([128], BF16)
        self.ONESM, _ = AR.alloc([128], BF16)
        self.ONES1, _ = AR.alloc([128], BF16)
        self.CONB = B("const")
        self.RC, _ = AR.alloc([16], F32)
        self.CARRY_A, _ = AR.alloc([2, NCH, 30], BF16)
        self.CARRY_P, _ = AR.alloc([2, NCH, 16], F32)
        self.CARRY_F, _ = AR.alloc([DEPTH, 44, 2], BF16)
        self.CAB = [B("ca") for _ in range(2)]
        self.CPB = [B("cp") for _ in range(2)]
        self.CFB = [B("cf") for _ in range(DEPTH)]
        self.KT, _ = AR.alloc([NCH, NMEM], BF16)
        self.VV, _ = AR.alloc([2, D], BF16)
        self.KTB = B("kt")
        self.VVB = B("vv")
        self.slot_off = []
        for s in range(NSLOT):
            _, off = AR.alloc([SLOT_BYTES // 2], BF16)
            self.slot_off.append(off)
        self.SLB = [B("slot%d" % s) for s in range(NSLOT)]
        self.HB, _ = AR.alloc([NCH, TG], BF16)
        self.HBB = [[B("h") for _ in range(NT)] for _ in range(NCH)]
        self.CO, _ = AR.alloc([NCH, TG], F32)
        self.COB = [[B("co") for _ in range(NT)] for _ in range(NCH)]
        self.MS, _ = AR.alloc([TG], F32)
        self.R, _ = AR.alloc([TG], F32)
        self.MEAN, self.mean_off = AR.alloc([TG], F32)
        self.XTRA, self.xtra_off = AR.alloc([TG], F32)
        self.MSB = [B("ms") for _ in range(NT)]
        self.RB = [B("r") for _ in range(NT)]
        self.MEANB = [B("mean") for _ in range(NT)]
        self.XTRAB = B("xtra")
        self.NSQ = 4
        self.SQ = [AR.alloc([TG - 512], BF16)[0] for _ in range(self.NSQ)]
        self.SQB = [B("sq") for _ in range(self.NSQ)]
        self.sq_i = 0
        self.NTMP = 2
        self.TMP = [AR.alloc([512], F32)[0] for _ in range(self.NTMP)]
        self.TMPB = [B("tmp") for _ in range(self.NTMP)]
        self.tmp_i = 0
        self.NSTG = 2
        self.STG = [AR.alloc([512], F32)[0] for _ in range(self.NSTG)]
        self.STGB = [B("stg") for _ in range(self.NSTG)]
        self.stg_i = 0
        xt = [self.AR.at(self.xtra_off + 2048 * i, [512], F32) for i in range(2)]
        self.FT = self.TMP + self.STG + xt
        self.FTB = self.TMPB + self.STGB + [B("xt0"), B("xt1")]
        self.ft_i = 0
        self.WSIZE = 43008
        _, self.w_off = AR.alloc([self.WSIZE // 2], BF16)
        self.WB_cur = []
        self.PS = [nc.alloc_psum_tensor("ps%d" % i, [128, 512], F32) for i in range(8)]
        self.PSB = [B("ps%d" % i) for i in range(8)]
        self.DRB = B("dram_passthru")

    def W(self, off, shape, dt):
        return self.AR.at(self.w_off + off, shape, dt)

    def new_view(self, bufs):
        if self.dry:
            return
        shared = [self.MEANB[t] for t in range(NT)] + [self.XTRAB, self.FTB[4], self.FTB[5]]
        self.fw.fence(self.WB_cur + shared, list(bufs) + shared)
        self.WB_cur = list(bufs)

    def ps(self, cls):
        lst = self.ps_cls[cls]
        i = lst[self.ps_i[cls] % len(lst)]
        self.ps_i[cls] += 1
        return i

    def sq(self):
        i = self.sq_i % self.NSQ
        self.sq_i += 1
        return i

    def tmp(self):
        i = self.tmp_i % self.NTMP
        self.tmp_i += 1
        return i

    def ft(self):
        i = self.ft_i % len(self.FT)
        self.ft_i += 1
        return i

    def stg(self):
        i = self.stg_i % self.NSTG
        self.stg_i += 1
        return i

    def slab(self, spec, first=True):
        if self.dry:
            self.slab_log.append(spec)
            return 0, None
        idx = self.slab_idx
        self.slab_idx += 1
        if first:
            self.slab_base = idx
        while self.slab_loaded < min(len(self.slab_log), self.slab_base + NSLOT):
            self._load_slab(self.slab_loaded)
            self.slab_loaded += 1
        return idx % NSLOT, self.SLB[idx % NSLOT]

    def _load_slab(self, i):
        spec = self.slab_log[i]
        s = i % NSLOT
        for (boff, shape, dt, src, q) in spec(self.d):
            dst = self.AR.at(self.slot_off[s] + boff, shape, dt)
            self.fw.dma(q, (lambda h, dst=dst, src=src: h.dma_start(out=dst, in_=src)), writes=[self.SLB[s]])

    def slot(self, s, boff, shape, dt):
        return self.AR.at(self.slot_off[s] + boff, shape, dt)

    def prm(self, col):
        return self.PRM[:, col:col + 1]

    def mm(self, cls_or_bank, n, terms, reads, col0=0):
        bank = cls_or_bank if isinstance(cls_or_bank, int) else self.ps(cls_or_bank)
        if self.dry:
            return bank
        out = self.PS[bank][:, col0:col0 + n]
        nt = len(terms)
        self.nmm += nt

        def fn(h):
            first = None
            ins = None
            for i, (l, r) in enumerate(terms):
                ins = h.matmul(out, l, r, start=(i == 0), stop=(i == nt - 1))
                if first is None:
                    first = ins
            return first, ins
        self.fw.op("pe", fn, reads=reads, writes=[self.PSB[bank]])
        return bank

    def act(self, out, in_, func, reads, writes, bias=None, scale=None):
        kw = {}
        if bias is not None:
            kw["bias"] = bias
        if scale is not None:
            kw["scale"] = scale
        self.fw.op("act", lambda h: h.activation(out=out, in_=in_, func=func, **kw), reads=reads, writes=writes)

    def stt(self, out, in0, scalar, in1, op0, op1, reads, writes):
        self.fw.op("dve", lambda h: h.scalar_tensor_tensor(out=out, in0=in0, scalar=scalar, in1=in1, op0=op0, op1=op1),
                   reads=reads, writes=writes)

    def tt(self, out, in0, in1, op, reads, writes):
        self.fw.op("dve", lambda h: h.tensor_tensor(out=out, in0=in0, in1=in1, op=op), reads=reads, writes=writes)

    def ttp(self, out, in0, in1, op, reads, writes):
        self.fw.op("pool", lambda h: h.tensor_tensor(out=out, in0=in0, in1=in1, op=op), reads=reads, writes=writes)

    def ts(self, out, in0, s1, op0, reads, writes):
        self.fw.op("dve", lambda h: h.tensor_scalar(out=out, in0=in0, scalar1=s1, scalar2=None, op0=op0), reads=reads, writes=writes)

    def cp(self, out, in_, reads, writes):
        self.fw.op("dve", lambda h: h.tensor_copy(out=out, in_=in_), reads=reads, writes=writes)

    def rms_stats(self, src3, srcbufs, e):
        e0, en, tl = ETILES[e]
        banks = {t: self.ps("st") for t in tl}
        for c in range(NCH):
            q = self.sq()
            self.act(self.SQ[q][:, :en], src3[:, c, e0:e0 + en], AF.Square, reads=[srcbufs[c][t] for t in tl], writes=[self.SQB[q]])
            for t in tl:
                c0, n = TILES[t]
                out = self.PS[banks[t]][:, :n]
                rhs = self.SQ[q][:, c0 - e0:c0 - e0 + n]
                self.nmm += 1
                self.fw.op("pe", (lambda h, out=out, rhs=rhs, c=c: h.matmul(out, self.ONESM, rhs, start=(c == 0), stop=(c == NCH - 1))),
                           reads=[self.SQB[q], self.CONB], writes=[self.PSB[banks[t]]])
        for t in tl:
            c0, n = TILES[t]
            self.ts(self.MS[:, c0:c0 + n], self.PS[banks[t]][:, :n], RMS_EPS, ALU.add, reads=[self.PSB[banks[t]]], writes=[self.MSB[t]])
        self.ln_exp_rstd(e0, en, tl)

    def ln_exp_rstd(self, e0, en, tl):
        self.act(self.R[:, e0:e0 + en], self.MS[:, e0:e0 + en], AF.Ln, reads=[self.MSB[t] for t in tl], writes=[self.RB[t] for t in tl])
        self.act(self.R[:, e0:e0 + en], self.R[:, e0:e0 + en], AF.Exp, reads=[self.RB[t] for t in tl], writes=[self.RB[t] for t in tl], scale=-0.5)

    def tail_begin(self):
        self._tail = {"pend": None, "banks": {}}

    def tail_chunk(self, t, c, src, reads, **kw):
        self._tail_flush()
        q = self.sq()
        n = TILES[t][1]
        self.act(self.SQ[q][:, :n], src, AF.Square, reads=reads, writes=[self.SQB[q]], **kw)
        self._tail["pend"] = (t, c, q)

    def _tail_flush(self):
        p = self._tail["pend"]
        if p is None:
            return
        t, c, q = p
        c0, n = TILES[t]
        if t not in self._tail["banks"]:
            self._tail["banks"][t] = self.ps("st")
        bank = self._tail["banks"][t]
        out = self.PS[bank][:, :n]
        rhs = self.SQ[q][:, :n]
        self.nmm += 1
        self.fw.op("pe", (lambda h, out=out, rhs=rhs, c=c: h.matmul(out, self.ONESM, rhs, start=(c == 0), stop=(c == NCH - 1))),
                   reads=[self.SQB[q], self.CONB], writes=[self.PSB[bank]])
        if c == NCH - 1:
            self.ts(self.MS[:, c0:c0 + n], self.PS[bank][:, :n], RMS_EPS, ALU.add, reads=[self.PSB[bank]], writes=[self.MSB[t]])
            for (e0, en, tl) in ETILES:
                if tl[-1] == t:
                    self.ln_exp_rstd(e0, en, tl)
        self._tail["pend"] = None

    def tail_end(self):
        self._tail_flush()
        self.post_stats_done = True

    def post_e(self, L, k, e):
        e0, en, tl = ETILES[e]
        if not self.post_stats_done:
            self.rms_stats(self.CO, self.COB, e)
        for c in range(NCH):
            o = self.CO[:, c, e0:e0 + en]
            cb = [self.COB[c][t] for t in tl]
            xb = [self.XB[c][t] for t in tl]
            self.stt(o, o, self.prm(_pc_gain(L, k, c)), self.R[:, e0:e0 + en], ALU.mult, ALU.mult,
                     reads=cb + [self.RB[t] for t in tl] + [self.PRMB], writes=cb)
            x = self.X[:, c, e0:e0 + en]
            self.tt(x, x, o, ALU.add, reads=xb + cb, writes=xb)

    def pre_e_hb(self, L, k, e):
        e0, en, tl = ETILES[e]
        self.rms_stats(self.X, self.XB, e)
        for c in range(NCH):
            self.stt(self.HB[:, c, e0:e0 + en], self.X[:, c, e0:e0 + en], self.prm(_pc_gain(L, k, c)), self.R[:, e0:e0 + en],
                     ALU.mult, ALU.mult, reads=[self.XB[c][t] for t in tl] + [self.RB[t] for t in tl] + [self.PRMB],
                     writes=[self.HBB[c][t] for t in tl])

    def boundary(self, pre_fn):
        if self.dry:
            return
        pend = self.pending_post
        self.pending_post = None
        for e in range(len(ETILES)):
            if pend is not None:
                self.post_e(pend[0], pend[1], e)
            if pre_fn is not None:
                pre_fn(e)
        self.post_stats_done = False

    def postnorm(self, L, k):
        self.pending_post = (L, k)

    def hb(self, c, t):
        c0, n = TILES[t]
        return self.HB[:, c, c0:c0 + n]

    def init_consts(self):
        if self.dry:
            return
        fw = self.fw
        d = self.d
        fw.dma("sp", lambda h: h.dma_start(out=self.PRM, in_=d["prm"][:, :]), writes=[self.PRMB])
        fw.dma("pool", lambda h: h.dma_start(out=self.IDENT, in_=d["ident"][:, :]), writes=[self.CONB])
        fw.op("pool", lambda h: h.memset(self.ONESM, 1.0 / D), writes=[self.CONB])
        fw.op("pool", lambda h: h.memset(self.ONES1, 1.0), writes=[self.CONB])
        for i in range(16):
            fw.op("pool", (lambda h, i=i: h.memset(self.RC[:, i:i + 1], 1.0 / (i + 1))), writes=[self.CONB])
        for j in range(2):
            fw.op("pool", (lambda h, j=j: h.memset(self.CARRY_A[:, j], 0.0)), writes=[self.CAB[j]])
            fw.op("pool", (lambda h, j=j: h.memset(self.CARRY_P[:, j], 0.0)), writes=[self.CPB[j]])
        for i in range(DEPTH):
            fw.op("pool", (lambda h, i=i: h.memset(self.CARRY_F[:, i], 0.0)), writes=[self.CFB[i]])

    def load_x(self, g):
        if self.dry:
            return
        xT = self.d["xT"].rearrange("(c p) t -> p c t", p=128)
        allx = [self.XB[c][t] for c in range(NCH) for t in range(NT)]
        self.fw.dma("sp", lambda h: h.dma_start(out=self.X[:, :, 0:TP], in_=xT[:, :, TP * g:TP * g + TP]),
                    writes=allx, sem_of=self.XB[0][0])
        self.fw.dma("sp", lambda h: h.dma_start(out=self.X[:, :, TP:TG], in_=xT[:, :, SEQ + TS * g:SEQ + TS * g + TS]),
                    writes=allx, sem_of=self.XB[0][0])

    def store_x(self, g):
        if self.dry:
            return
        yT = self.d["yT"].rearrange("(c p) t -> p c t", p=128)
        allx = [self.XB[c][t] for c in range(NCH) for t in range(NT)]
        self.fw.dma("sp", lambda h: h.dma_start(out=yT[:, :, TP * g:TP * g + TP], in_=self.X[:, :, 0:TP]),
                    reads=allx, sem_of=self.XB[0][0])
        self.fw.dma("sp", lambda h: h.dma_start(out=yT[:, :, SEQ + TS * g:SEQ + TS * g + TS], in_=self.X[:, :, TP:TG]),
                    reads=allx, sem_of=self.XB[0][0])

    def a_mixer(self, L, g):
        j = L // 2
        fw = self.fw
        d = self.d
        dry = self.dry
        B = Buf
        if not dry:
            self.boundary(lambda e: self.pre_e_hb(L, 0, e))
            GLUX = self.W(0, [NCH, TP + 30], BF16)
            GLUS = self.W(16896, [NCH, SG_, 38], BF16)
            DG = self.W(21760, [2, CW, 128], BF16)
            GLUF = self.W(37632, [NCH, 30], F32)
            GLUSF = self.W(38592, [NCH, SG_, DSEQ], F32)
            GXH = [B("gxh") for _ in range(NCH)]
            GXB = [[B("gx") for _ in range(2)] for _ in range(NCH)]
            GSB = [B("gs") for _ in range(NCH)]
            DGB = [B("dg") for _ in range(2)]
            GFB = B("gluf")
            GSFB = B("glusf")
            self.new_view([b for l in GXB for b in l] + GXH + GSB + DGB + [GFB, GSFB])
            self.cp(GLUX[:, :, 0:30], self.CARRY_A[:, j], reads=[self.CAB[j]], writes=GXH)
        s, sb = self.slab(lambda d, j=j, g=g: [(0, [NCH * SG_ * 30], BF16, d["sconv"][j, g], "pool")])
        if not dry:
            src = self.slot(s, 0, [NCH, SG_, 30], BF16)
            self.cp(GLUS[:, :, :, 0:30], src, reads=[sb], writes=GSB)
            for c in range(NCH):
                fw.dma("sp", (lambda h, c=c: h.dma_start(out=d["ncs"][j, g, :, c, :, 0:22], in_=d["sconv4"][j, g, :, c, :, 8:30])),
                       sem_of=self.DRB)
        last_g = (g == G - 1)
        for sl in range(4):
            s, sb = self.slab(lambda d, j=j, sl=sl: [
                (0, [NCH, 256], BF16, d["a_w_in"][j].rearrange("(kc p) n -> p kc n", p=128)[:, :, 256 * sl:256 * sl + 256], "pool"),
                (NCH * 256 * 2, [NCH, 256], BF16, d["a_w_in"][j].rearrange("(kc p) n -> p kc n", p=128)[:, :, D + 256 * sl:D + 256 * sl + 256], "pool")])
            if dry:
                continue
            wa = self.slot(s, 0, [NCH, 256], BF16)
            wg = self.slot(s, NCH * 256 * 2, [NCH, 256], BF16)
            for cc in range(2):
                c = 2 * sl + cc
                banks = []
                for t, (c0, n) in enumerate(TILES):
                    hr = [self.HBB[kc][t] for kc in range(NCH)] + [sb]
                    ba = self.mm("mm6", n, [(wa[:, kc, 128 * cc:128 * cc + 128], self.hb(kc, t)) for kc in range(NCH)], hr)
                    bg = self.mm("mm6", n, [(wg[:, kc, 128 * cc:128 * cc + 128], self.hb(kc, t)) for kc in range(NCH)], hr)
                    if t == 1:
                        self._glu_tile(j, c, 0, banks[0][0], banks[0][1], GLUX, GLUS, GLUF, GLUSF, GXB, GSB, GFB, GSFB, last_g)
                    banks.append((ba, bg))
                self._glu_tile(j, c, 1, banks[1][0], banks[1][1], GLUX, GLUS, GLUF, GLUSF, GXB, GSB, GFB, GSFB, last_g)
                self._glu_tile(j, c, 2, banks[2][0], banks[2][1], GLUX, GLUS, GLUF, GLUSF, GXB, GSB, GFB, GSFB, last_g)
        if not dry:
            if last_g:
                fw.dma("sp", lambda h: h.dma_start(out=d["ncp"][j].rearrange("(c p) k -> p c k", p=128), in_=GLUF), reads=[GFB])
            fw.dma("sp", lambda h: h.dma_start(out=d["ncs"][j, g, :, :, :, 22:30], in_=GLUSF), reads=[GSFB])
            for c in range(NCH):
                db = c % 2
                for k in range(CW):
                    self.ts(DG[:, db, k], self.IDENT, self.prm(PC_AW + j * 248 + c * CW + k), ALU.mult,
                            reads=[self.CONB, self.PRMB], writes=[DGB[db]])
                for t, (c0, n) in enumerate(TILES):
                    if t < ST:
                        terms = [(DG[:, db, k], GLUX[:, c, c0 + k:c0 + k + n]) for k in range(CW)]
                        rd = [DGB[db], GXH[c], GXB[c][0]] + ([GXB[c][1]] if t == 1 else [])
                    else:
                        terms = [(DG[:, db, k], GLUS[:, c, :, k:k + DSEQ]) for k in range(CW)]
                        rd = [DGB[db], GSB[c]]
                    bank = self.mm("aux", n, terms, rd)
                    self.act(self.CO[:, c, c0:c0 + n], self.PS[bank][:, :n], AF.Identity, reads=[self.PSB[bank], self.PRMB],
                             writes=[self.COB[c][t]], bias=self.prm(PC_A + j * 32 + 0 + c))
            if not last_g:
                self.cp(self.CARRY_A[:, j], GLUX[:, :, TP:TP + 30], reads=[GXB[c][1] for c in range(NCH)], writes=[self.CAB[j]])
            for e, (e0, en, tl) in enumerate(ETILES):
                bms = {}
                bqs = {}
                for i_, t in enumerate(tl):
                    cls = "st" if i_ == 0 else "aux"
                    bms[t] = self.ps(cls)
                    bqs[t] = self.ps(cls)
                for c in range(NCH):
                    src = self.CO[:, c, e0:e0 + en]
                    cb = [self.COB[c][t] for t in tl]
                    q1 = self.sq()
                    self.act(self.SQ[q1][:, :en], src, AF.Copy, reads=cb, writes=[self.SQB[q1]])
                    q2 = self.sq()
                    self.act(self.SQ[q2][:, :en], src, AF.Square, reads=cb, writes=[self.SQB[q2]])
                    for t in tl:
                        c0, n = TILES[t]
                        for (bk, q) in ((bms[t], q1), (bqs[t], q2)):
                            out = self.PS[bk][:, :n]
                            rhs = self.SQ[q][:, c0 - e0:c0 - e0 + n]
                            self.nmm += 1
                            fw.op("pe", (lambda h, out=out, rhs=rhs, c=c: h.matmul(out, self.ONESM, rhs, start=(c == 0), stop=(c == NCH - 1))),
                                  reads=[self.SQB[q], self.CONB], writes=[self.PSB[bk]])
                for t in tl:
                    c0, n = TILES[t]
                    mean = self.MEAN[:, c0:c0 + n]
                    ms = self.MS[:, c0:c0 + n]
                    self.ts(mean, self.PS[bms[t]][:, :n], 0.0, ALU.add, reads=[self.PSB[bms[t]]], writes=[self.MEANB[t]])
                    self.stt(ms, mean, -1.0, mean, ALU.mult, ALU.mult, reads=[self.MEANB[t]], writes=[self.MSB[t]])
                    self.stt(ms, ms, LN_EPS, self.PS[bqs[t]][:, :n], ALU.add, ALU.add, reads=[self.MSB[t], self.PSB[bqs[t]]], writes=[self.MSB[t]])
                self.ln_exp_rstd(e0, en, tl)
                for c in range(NCH):
                    cc_ = self.CO[:, c, e0:e0 + en]
                    cb = [self.COB[c][t] for t in tl]
                    self.tt(cc_, cc_, self.MEAN[:, e0:e0 + en], ALU.subtract, reads=cb + [self.MEANB[t] for t in tl], writes=cb)
                    self.tt(cc_, cc_, self.R[:, e0:e0 + en], ALU.mult, reads=cb + [self.RB[t] for t in tl], writes=cb)
                    self.act(self.HB[:, c, e0:e0 + en], cc_, AF.Silu, reads=cb + [self.PRMB], writes=[self.HBB[c][t] for t in tl],
                             scale=self.prm(PC_A + j * 32 + 8 + c), bias=self.prm(PC_A + j * 32 + 16 + c))
        sl_h = []
        for sl in range(2):
            sl_h.append(self.slab(lambda d, j=j, sl=sl: [
                (0, [NCH, 512], BF16, d["a_w_out"][j].rearrange("(kc p) n -> p kc n", p=128)[:, :, 512 * sl:512 * sl + 512], "pool")], first=(sl == 0)))
        if not dry:
            self.tail_begin()
            for t, (c0, n) in enumerate(TILES):
                for sl in range(2):
                    s, sb = sl_h[sl]
                    w = self.slot(s, 0, [NCH, 512], BF16)
                    for cc in range(4):
                        c = 4 * sl + cc
                        bank = self.mm("mm", n, [(w[:, kc, 128 * cc:128 * cc + 128], self.hb(kc, t)) for kc in range(NCH)],
                                       [self.HBB[kc][t] for kc in range(NCH)] + [sb])
                        self.act(self.CO[:, c, c0:c0 + n], self.PS[bank][:, :n], AF.Identity, reads=[self.PSB[bank], self.PRMB],
                                 writes=[self.COB[c][t]], bias=self.prm(PC_A + j * 32 + 24 + c))
                        self.tail_chunk(t, c, self.PS[bank][:, :n], [self.PSB[bank], self.PRMB], bias=self.prm(PC_A + j * 32 + 24 + c))
            self.tail_end()
        self.postnorm(L, 1)

    def _glu_tile(self, j, c, t, ba, bg, GLUX, GLUS, GLUF, GLUSF, GXB, GSB, GFB, GSFB, last_g):
        c0, n = TILES[t]
        k = self.tmp()
        sg = self.TMP[k][:, :n]
        self.act(sg, self.PS[bg][:, :n], AF.Sigmoid, reads=[self.PSB[bg], self.PRMB], writes=[self.TMPB[k]],
                 bias=self.prm(PC_BIN + j * 16 + 8 + c))
        ba_col = self.prm(PC_BIN + j * 16 + c)
        if t < ST:
            self.stt(GLUX[:, c, 30 + c0:30 + c0 + n], self.PS[ba][:, :n], ba_col, sg, ALU.add, ALU.mult,
                     reads=[self.PSB[ba], self.TMPB[k], self.PRMB], writes=[GXB[c][t]])
            if last_g and t == ST - 1:
                self.stt(GLUF[:, c, :], self.PS[ba][:, n - 30:n], ba_col, sg[:, n - 30:n], ALU.add, ALU.mult,
                         reads=[self.PSB[ba], self.TMPB[k], self.PRMB], writes=[GFB])
        else:
            pa = self.PS[ba][:, :n].rearrange("p (s l) -> p s l", l=DSEQ)
            sg3 = sg.rearrange("p (s l) -> p s l", l=DSEQ)
            self.stt(GLUS[:, c, :, 30:38], pa, ba_col, sg3, ALU.add, ALU.mult,
                     reads=[self.PSB[ba], self.TMPB[k], self.PRMB], writes=[GSB[c]])
            self.stt(GLUSF[:, c], pa, ba_col, sg3, ALU.add, ALU.mult,
                     reads=[self.PSB[ba], self.TMPB[k], self.PRMB], writes=[GSFB])

    def b_mixer(self, L, g):
        j = L // 2
        fw = self.fw
        d = self.d
        dry = self.dry
        B = Buf
        XW = TP + 16
        if not dry:
            HFX = self.W(0, [NCH, XW], F32)
            HFS = self.W(33280, [NCH, SG_, 24], F32)
            T1S = self.W(39424, [SG_, 24], F32)
            T2S = self.W(40192, [SG_, 24], F32)
            T1 = self.AR.at(self.mean_off, [XW], F32)
            T2 = self.AR.at(self.xtra_off, [XW], F32)
            HXH = [B("hxh") for _ in range(NCH)]
            HXB = [[B("hx") for _ in range(2)] for _ in range(NCH)]
            HSB = [B("hs") for _ in range(NCH)]
            TB = B("t12")
            TSB = B("t12s")
            self.new_view([b for l in HXB for b in l] + HXH + HSB + [TB, TSB])
            self.cp(HFX[:, :, 0:16], self.CARRY_P[:, j], reads=[self.CPB[j]], writes=HXH)
        s, sb = self.slab(lambda d, j=j, g=g: [(0, [NCH * SG_ * 15], F32, d["spool"][j, g], "pool")])
        if not dry:
            src = self.slot(s, 0, [NCH, SG_, 15], F32)
            fw.op("dve", lambda h: h.memset(HFS[:, :, :, 0:1], 0.0), writes=HSB)
            self.cp(HFS[:, :, :, 1:16], src, reads=[sb], writes=HSB)

            def dst_of(c, t):
                c0, n = TILES[t]
                if t < ST:
                    return HFX[:, c, 16 + c0:16 + c0 + n]
                return HFS[:, c, :, 16:24]

            def dstb(c, t):
                return [HXB[c][t]] if t < ST else [HSB[c]]
            def pre_b(e):
                e0, en, tl = ETILES[e]
                self.rms_stats(self.X, self.XB, e)
                for t in tl:
                    c0, n = TILES[t]
                    for c in range(NCH):
                        x = self.X[:, c, c0:c0 + n]
                        r = self.R[:, c0:c0 + n]
                        if t == ST:
                            x = x.rearrange("p (s l) -> p s l", l=DSEQ)
                            r = r.rearrange("p (s l) -> p s l", l=DSEQ)
                        self.stt(dst_of(c, t), x, self.prm(_pc_gain(L, 0, c)), r, ALU.mult, ALU.mult,
                                 reads=[self.XB[c][t], self.RB[t], self.PRMB], writes=dstb(c, t))
            self.boundary(pre_b)
            last_g = (g == G - 1)
            if last_g:
                fw.dma("sp", lambda h: h.dma_start(out=d["npp"][j].rearrange("(c p) k -> p c k", p=128), in_=HFX[:, :, XW - 15:XW]),
                       reads=[HXB[c][1] for c in range(NCH)], sem_of=HXB[0][1])
            else:
                self.cp(self.CARRY_P[:, j], HFX[:, :, XW - 16:XW], reads=[HXB[c][1] for c in range(NCH)], writes=[self.CPB[j]])
            fw.dma("sp", lambda h: h.dma_start(out=d["nps"][j, g], in_=HFS[:, :, :, 9:24]), reads=HSB, sem_of=HSB[0])
            for c in range(NCH):
                lw = c // 2 + 1
                w = 1 << lw
                z = HFX[:, c, :]
                zs = HFS[:, c]
                rdz = [HXH[c], HXB[c][0], HXB[c][1]]
                cur, curs = z, zs
                bufs = [T1, T2]
                bufss = [T1S, T2S]
                for st in range(lw):
                    sh = 1 << st
                    o = bufs[st % 2]
                    os_ = bufss[st % 2]
                    lo = 2 * sh - 1
                    self.tt(o[:, lo:XW], cur[:, lo:XW], cur[:, lo - sh:XW - sh], ALU.add, reads=rdz + [TB], writes=[TB])
                    self.tt(os_[:, :, lo:24], curs[:, :, lo:24], curs[:, :, lo - sh:24 - sh], ALU.add, reads=[HSB[c], TSB], writes=[TSB])
                    cur, curs = o, os_
                inv = 1.0 / w
                self.stt(self.HB[:, c, 0:TP], cur[:, 16:XW], inv, z[:, 16:XW], ALU.mult, ALU.subtract,
                         reads=rdz + [TB], writes=[self.HBB[c][0], self.HBB[c][1]])
                if g == 0:
                    k = self.tmp()
                    tm = self.TMP[k][:, 0:w - 1]
                    self.tt(tm, cur[:, 16:16 + w - 1], self.RC[:, 0:w - 1], ALU.mult, reads=[TB, self.CONB], writes=[self.TMPB[k]])
                    self.tt(self.HB[:, c, 0:w - 1], tm, z[:, 16:16 + w - 1], ALU.subtract, reads=[self.TMPB[k]] + rdz, writes=[self.HBB[c][0]])
                self.stt(self.HB[:, c, TP:TG].rearrange("p (s l) -> p s l", l=DSEQ), curs[:, :, 16:24], inv, zs[:, :, 16:24],
                         ALU.mult, ALU.subtract, reads=[HSB[c], TSB], writes=[self.HBB[c][ST]])
        s, sb = self.slab(lambda d, j=j: [(0, [NCH, 256], BF16, d["p_w_group"][j].rearrange("g (kc p) n -> p (g kc) n", p=128), "pool")])
        if not dry:
            w = self.slot(s, 0, [NCH, 256], BF16)
            self.tail_begin()
            for t, (c0, n) in enumerate(TILES):
                for oc in range(NCH):
                    gi, oh = oc // 2, oc % 2
                    bank = self.mm("mm", n, [(w[:, 2 * gi + kc, 128 * oh:128 * oh + 128], self.hb(2 * gi + kc, t)) for kc in range(2)],
                                   [self.HBB[2 * gi][t], self.HBB[2 * gi + 1][t], sb])
                    self.act(self.CO[:, oc, c0:c0 + n], self.PS[bank][:, :n], AF.Copy, reads=[self.PSB[bank], self.PRMB],
                             writes=[self.COB[oc][t]], scale=self.prm(PC_PS + j * 8 + oc))
                    self.tail_chunk(t, oc, self.PS[bank][:, :n], [self.PSB[bank], self.PRMB], scale=self.prm(PC_PS + j * 8 + oc))
            self.tail_end()
        self.postnorm(L, 1)

    def cross(self, L, g):
        fw = self.fw
        d = self.d
        dry = self.dry
        B = Buf
        SC = 1.0 / 16.0
        if not dry:
            Q = self.W(0, [NCH, TG], BF16)
            E = self.W(17408, [NH, 2, 512], BF16)
            RS = self.W(25600, [NH, 512], F32)
            ES = self.W(33792, [2, NH, TS], BF16)
            RSS = self.W(34816, [NH, TS], F32)
            MEMN = self.W(35840, [NCH, NMEM], BF16)
            MSM = self.W(39936, [NMEM], F32)
            QB = [[B("q") for _ in range(NT)] for _ in range(NCH)]
            EB = [B("e") for _ in range(NH)]
            RSB = [B("rs") for _ in range(NH)]
            ESB = B("es")
            RSSB = B("rss")
            MNB = B("memn")
            MSMB = B("msm")
            self.new_view([b for l in QB for b in l] + EB + RSB + [ESB, RSSB, MNB, MSMB])
        s, sb = self.slab(lambda d: [(0, [NCH, NMEM], F32, d["memT"].rearrange("(c p) m -> p c m", p=128), "pool")])
        if not dry:
            MEMT = self.slot(s, 0, [NCH, NMEM], F32)
            bank = self.ps("st")
            for c in range(NCH):
                q = self.sq()
                self.act(self.SQ[q][:, :NMEM], MEMT[:, c], AF.Square, reads=[sb], writes=[self.SQB[q]])
                self.nmm += 1
                fw.op("pe", (lambda h, bank=bank, q=q, c=c: h.matmul(self.PS[bank][:, :NMEM], self.ONESM, self.SQ[q][:, :NMEM], start=(c == 0), stop=(c == NCH - 1))),
                      reads=[self.SQB[q], self.CONB], writes=[self.PSB[bank]])
            self.ts(MSM, self.PS[bank][:, :NMEM], RMS_EPS, ALU.add, reads=[self.PSB[bank]], writes=[MSMB])
            self.act(MSM, MSM, AF.Ln, reads=[MSMB], writes=[MSMB])
            self.act(MSM, MSM, AF.Exp, reads=[MSMB], writes=[MSMB], scale=-0.5)
            for c in range(NCH):
                self.stt(MEMN[:, c], MEMT[:, c], self.prm(_pc_gain(L, 6, c)), MSM, ALU.mult, ALU.mult, reads=[sb, MSMB, self.PRMB], writes=[MNB])
        import os
        CC = int(os.environ.get("CROSS_CUT", "99"))
        for sl in range(4):
            if CC <= 1:
                break
            s, sb = self.slab(lambda d, L=L, sl=sl: [
                (0, [NCH, 512], BF16, d["c_w_kv"][L].rearrange("(kc p) n -> p kc n", p=128)[:, :, 512 * sl:512 * sl + 512], "pool")])
            if dry:
                continue
            w = self.slot(s, 0, [NCH, 512], BF16)
            if sl < 2:
                for cc in range(4):
                    c = 4 * sl + cc
                    bank = self.mm("mm", NMEM, [(w[:, kc, 128 * cc:128 * cc + 128], MEMN[:, kc]) for kc in range(NCH)], [MNB, sb])
                    self.act(self.KT[:, c], self.PS[bank][:, :NMEM], AF.Copy, reads=[self.PSB[bank]], writes=[self.KTB])
            if sl >= 2 or g == 0:
                for mc in range(2):
                    bank = self.mm("mm", 512, [(MEMN[:, kc, 128 * mc:128 * mc + 128], w[:, kc]) for kc in range(NCH)], [MNB, sb])
                    if sl >= 2:
                        self.act(self.VV[:, mc, 512 * (sl - 2):512 * (sl - 2) + 512], self.PS[bank][:, :], AF.Copy, reads=[self.PSB[bank]], writes=[self.VVB])
                    if g == 0 and int(os.environ.get("NO_STG", "0")) == 0:
                        k = self.stg()
                        self.act(self.STG[k], self.PS[bank][:, :], AF.Copy, reads=[self.PSB[bank]], writes=[self.STGB[k]])
                        dst = d["mk"] if sl < 2 else d["mv"]
                        col = 512 * (sl % 2)
                        if int(os.environ.get("NO_STGDMA", "0")):
                            continue
                        fw.dma(os.environ.get("STG_Q", "sp"), (lambda h, dst=dst, mc=mc, col=col, k=k: h.dma_start(out=dst[L, 128 * mc:128 * mc + 128, col:col + 512], in_=self.STG[k])),
                               reads=[self.STGB[k]])
        self.boundary(lambda e: self.pre_e_hb(L, 2, e))
        for sl in range(2):
            if CC <= 2:
                break
            s, sb = self.slab(lambda d, L=L, sl=sl: [
                (0, [NCH, 512], BF16, d["c_w_q"][L].rearrange("(kc p) n -> p kc n", p=128)[:, :, 512 * sl:512 * sl + 512], "pool")])
            if dry:
                continue
            w = self.slot(s, 0, [NCH, 512], BF16)
            for t, (c0, n) in enumerate(TILES):
                for cc in range(4):
                    c = 4 * sl + cc
                    bank = self.mm("mm", n, [(w[:, kc, 128 * cc:128 * cc + 128], self.hb(kc, t)) for kc in range(NCH)],
                                   [self.HBB[kc][t] for kc in range(NCH)] + [sb])
                    self.act(Q[:, c, c0:c0 + n], self.PS[bank][:, :n], AF.Copy, reads=[self.PSB[bank]], writes=[QB[c][t]])
        if not dry and CC > 3:
            for t in range(ST):
                c0, n = TILES[t]
                for hh in range(NH):
                    for mc in range(2):
                        bank = self.mm("mm", n, [(self.KT[:, 2 * hh + dc, 128 * mc:128 * mc + 128], Q[:, 2 * hh + dc, c0:c0 + n]) for dc in range(2)],
                                       [self.KTB, QB[2 * hh][t], QB[2 * hh + 1][t]])
                        self.act(E[:, hh, mc, :n], self.PS[bank][:, :n], AF.Exp, reads=[self.PSB[bank]], writes=[EB[hh]], scale=SC)
                for hh in range(NH):
                    bos = []
                    for dc in range(2):
                        c = 2 * hh + dc
                        bos.append(self.mm("mm", n, [(self.VV[:, mc, 128 * c:128 * c + 128], E[:, hh, mc, :n]) for mc in range(2)], [self.VVB, EB[hh]]))
                    bs = self.mm("st", n, [(self.ONES1, E[:, hh, mc, :n]) for mc in range(2)], [EB[hh], self.CONB])
                    self.act(RS[:, hh, :n], self.PS[bs][:, :n], AF.Ln, reads=[self.PSB[bs]], writes=[RSB[hh]])
                    self.act(RS[:, hh, :n], RS[:, hh, :n], AF.Exp, reads=[RSB[hh]], writes=[RSB[hh]], scale=-1.0)
                    for dc in range(2):
                        c = 2 * hh + dc
                        self.tt(self.HB[:, c, c0:c0 + n], self.PS[bos[dc]][:, :n], RS[:, hh, :n], ALU.mult,
                                reads=[self.PSB[bos[dc]], RSB[hh]], writes=[self.HBB[c][t]])
        c0, n = TILES[ST]
        bsc = self.ps("mm")
        if CC <= 4:
            self.postnorm(L, 3)
            return
        for sq_ in range(SG_):
            if sq_ % 2 == 0:
                s, sb = self.slab(lambda d, L=L, g=g, sq_=sq_: [
                    (0, [NCH, NMEM], BF16, d["kT"][L, g * SG_ + sq_].rearrange("(c p) m -> p c m", p=128), "pool"),
                    (4096, [NCH, NMEM], BF16, d["kT"][L, g * SG_ + sq_ + 1].rearrange("(c p) m -> p c m", p=128), "pool")])
            if dry:
                continue
            kt = self.slot(s, 4096 * (sq_ % 2), [NCH, NMEM], BF16)

            def fn(h, kt=kt, sq_=sq_, c0=c0, bsc=bsc):
                first = None
                ins = None
                for hh in range(NH):
                    for mc in range(2):
                        col = mc * NH * TS + hh * TS + DSEQ * sq_
                        for dc in range(2):
                            ins = h.matmul(self.PS[bsc][:, col:col + DSEQ], kt[:, 2 * hh + dc, 128 * mc:128 * mc + 128],
                                           Q[:, 2 * hh + dc, c0 + DSEQ * sq_:c0 + DSEQ * sq_ + DSEQ], start=(dc == 0), stop=(dc == 1))
                            if first is None:
                                first = ins
                return first, ins
            self.nmm += 16
            fw.op("pe", fn, reads=[sb] + [QB[c][ST] for c in range(NCH)], writes=[self.PSB[bsc]])
        if not dry:
            self.act(ES, self.PS[bsc][:, :].rearrange("p (a b c) -> p a b c", a=2, b=NH), AF.Exp, reads=[self.PSB[bsc]], writes=[ESB], scale=SC)
            bs = self.mm("st", NH * TS, [(self.ONES1, ES[:, mc].rearrange("p a b -> p (a b)")) for mc in range(2)], [ESB, self.CONB])
            self.act(RSS, self.PS[bs][:, :NH * TS].rearrange("p (a b) -> p a b", a=NH), AF.Ln, reads=[self.PSB[bs]], writes=[RSSB])
            self.act(RSS, RSS, AF.Exp, reads=[RSSB], writes=[RSSB], scale=-1.0)
        bo = self.ps("aux")
        for sq_ in range(SG_):
            if sq_ % 2 == 0:
                s, sb = self.slab(lambda d, L=L, g=g, sq_=sq_: [
                    (0, [2, D], BF16, d["v"][L, g * SG_ + sq_].rearrange("(mc p) f -> p mc f", p=128), "pool"),
                    (4096, [2, D], BF16, d["v"][L, g * SG_ + sq_ + 1].rearrange("(mc p) f -> p mc f", p=128), "pool")])
            if dry:
                continue
            vs = self.slot(s, 4096 * (sq_ % 2), [2, D], BF16)

            def fn2(h, vs=vs, sq_=sq_, bo=bo):
                first = None
                ins = None
                for c in range(NCH):
                    hh = c // 2
                    col = c * TS + DSEQ * sq_
                    for mc in range(2):
                        ins = h.matmul(self.PS[bo][:, col:col + DSEQ], vs[:, mc, 128 * c:128 * c + 128],
                                       ES[:, mc, hh, DSEQ * sq_:DSEQ * sq_ + DSEQ], start=(mc == 0), stop=(mc == 1))
                        if first is None:
                            first = ins
                return first, ins
            self.nmm += 16
            fw.op("pe", fn2, reads=[sb, ESB], writes=[self.PSB[bo]])
        if not dry:
            for c in range(NCH):
                self.tt(self.HB[:, c, c0:c0 + n], self.PS[bo][:, c * TS:c * TS + TS], RSS[:, c // 2], ALU.mult,
                        reads=[self.PSB[bo], RSSB], writes=[self.HBB[c][ST]])
        sl_h = []
        for sl in range(2):
            sl_h.append(self.slab(lambda d, L=L, sl=sl: [
                (0, [NCH, 512], BF16, d["c_w_o"][L].rearrange("(kc p) n -> p kc n", p=128)[:, :, 512 * sl:512 * sl + 512], "pool")], first=(sl == 0)))
        if not dry:
            self.tail_begin()
            for t, (c0, n) in enumerate(TILES):
                for sl in range(2):
                    s, sb = sl_h[sl]
                    w = self.slot(s, 0, [NCH, 512], BF16)
                    for cc in range(4):
                        c = 4 * sl + cc
                        bank = self.mm("mm", n, [(w[:, kc, 128 * cc:128 * cc + 128], self.hb(kc, t)) for kc in range(NCH)],
                                       [self.HBB[kc][t] for kc in range(NCH)] + [sb])
                        self.act(self.CO[:, c, c0:c0 + n], self.PS[bank][:, :n], AF.Copy, reads=[self.PSB[bank]], writes=[self.COB[c][t]])
                        self.tail_chunk(t, c, self.PS[bank][:, :n], [self.PSB[bank]])
            self.tail_end()
        self.postnorm(L, 3)

    def ffn(self, L, g):
        fw = self.fw
        d = self.d
        dry = self.dry
        B = Buf
        NJ = NFC // 2
        last_g = (g == G - 1)
        if not dry:
            self.boundary(lambda e: self.pre_e_hb(L, 4, e))
            A = self.W(0, [NJ, TG], BF16)
            UX = self.W(23936, [2, 2, TP + 2], BF16)
            UXS = self.W(32192, [44, SG_, 10], BF16)
            DG3 = self.W(39232, [2, 6, 128], BF16)
            NF = self.W(42304, [44, 2], F32)
            NFS = self.AR.at(self.mean_off, [44, SG_, 2], F32)
            AB = [[B("a") for _ in range(NT)] for _ in range(NJ)]
            UXB = [[B("ux") for _ in range(3)] for _ in range(2)]
            UXSB = B("uxs")
            DG3B = [B("dg3") for _ in range(2)]
            NFB = B("nf")
            NFSB = B("nfs")
            self.new_view([b for l in AB for b in l] + [b for l in UXB for b in l] + [UXSB, NFB, NFSB] + DG3B)
        s, sb = self.slab(lambda d, L=L, g=g: [(0, [44 * SG_ * 2], BF16, d["sffn"][L, g], "pool")])
        if not dry:
            src = self.slot(s, 0, [44, SG_, 2], BF16)
            self.cp(UXS[:, :, :, 0:2], src, reads=[sb], writes=[UXSB])
        import os
        CUT = int(os.environ.get("FFN_CUT", "99"))
        for hf in range(2):
            if CUT <= 1:
                break
            for jj in range(NJ):
                if CUT <= 2 and jj >= 1:
                    break
                jf = NJ * hf + jj
                s, sb = self.slab(lambda d, L=L, jf=jf: [
                    (0, [NCH, 128], BF16, d["f_w_up"][L].rearrange("(kc p) n -> p kc n", p=128)[:, :, 128 * jf:128 * jf + 128], "pool"),
                    (NCH * 128 * 2, [NCH, 128], BF16, d["f_w_up"][L].rearrange("(kc p) n -> p kc n", p=128)[:, :, DFF + 128 * jf:DFF + 128 * jf + 128], "pool")])
                if dry:
                    continue
                wg = self.slot(s, 0, [NCH, 128], BF16)
                wv = self.slot(s, NCH * 128 * 2, [NCH, 128], BF16)
                ub = jf % 2
                chs = (jf, NFC + jf)
                for gv in range(2):
                    self.cp(UX[:, ub, gv, 0:2], self.CARRY_F[:, L, chs[gv]], reads=[self.CFB[L]], writes=[UXB[ub][0]])
                for t, (c0, n) in enumerate(TILES):
                    hr = [self.HBB[kc][t] for kc in range(NCH)] + [sb]
                    t0 = []
                    for gv, wsl in enumerate((wg, wv)):
                        ch = chs[gv]
                        bank = self.mm("mm6", n, [(wsl[:, kc], self.hb(kc, t)) for kc in range(NCH)], hr)
                        p = self.PS[bank][:, :n]
                        ti = self.ft()
                        T0 = self.FT[ti][:, :n]
                        w0c = self.prm(PC_FW + (L * 3 + 0) * 44 + ch)
                        w1c = self.prm(PC_FW + (L * 3 + 1) * 44 + ch)
                        w2c = self.prm(PC_FW + (L * 3 + 2) * 44 + ch)
                        self.act(T0, p, AF.Identity, reads=[self.PSB[bank], self.PRMB], writes=[self.FTB[ti]],
                                 scale=w2c, bias=self.prm(PC_FB + L * 44 + ch))
                        if t < ST:
                            self.act(UX[:, ub, gv, 2 + c0:2 + c0 + n], p, AF.Copy, reads=[self.PSB[bank]], writes=[UXB[ub][1 + t]])
                            if last_g and t == ST - 1:
                                self.act(NF[:, ch], p[:, n - 2:n], AF.Copy, reads=[self.PSB[bank]], writes=[NFB])
                            rd = [UXB[ub][0], UXB[ub][1]] + ([UXB[ub][2]] if t == 1 else [])
                            self.stt(T0, UX[:, ub, gv, c0 + 1:c0 + 1 + n], w1c, T0, ALU.mult, ALU.add, reads=rd + [self.FTB[ti], self.PRMB], writes=[self.FTB[ti]])
                            self.stt(T0, UX[:, ub, gv, c0:c0 + n], w0c, T0, ALU.mult, ALU.add, reads=rd + [self.FTB[ti], self.PRMB], writes=[self.FTB[ti]])
                        else:
                            p3 = p.rearrange("p (s l) -> p s l", l=DSEQ)
                            T3 = T0.rearrange("p (s l) -> p s l", l=DSEQ)
                            self.act(UXS[:, ch, :, 2:10], p3, AF.Copy, reads=[self.PSB[bank]], writes=[UXSB])
                            self.act(NFS[:, ch], p3[:, :, DSEQ - 2:DSEQ], AF.Copy, reads=[self.PSB[bank]], writes=[NFSB])
                            self.stt(T3, UXS[:, ch, :, 1:9], w1c, T3, ALU.mult, ALU.add, reads=[UXSB, self.FTB[ti], self.PRMB], writes=[self.FTB[ti]])
                            self.stt(T3, UXS[:, ch, :, 0:8], w0c, T3, ALU.mult, ALU.add, reads=[UXSB, self.FTB[ti], self.PRMB], writes=[self.FTB[ti]])
                        t0.append(ti)
                    tg, tv = t0
                    G_ = self.FT[tg][:, :n]
                    self.act(G_, G_, AF.Silu, reads=[self.FTB[tg]], writes=[self.FTB[tg]])
                    self.tt(A[:, jj, c0:c0 + n], self.FT[tv][:, :n], G_, ALU.mult, reads=[self.FTB[tv], self.FTB[tg]], writes=[AB[jj][t]])
                if not last_g:
                    for gv in range(2):
                        self.cp(self.CARRY_F[:, L, chs[gv]], UX[:, ub, gv, TP:TP + 2], reads=[UXB[ub][2]], writes=[self.CFB[L]])
            dn = []
            for oc2 in range(4):
                dn.append(self.slab(lambda d, L=L, hf=hf, oc2=oc2: [
                    (0, [NJ, 256], BF16, d["f_w_down"][L][NJ * 128 * hf:NJ * 128 * hf + NJ * 128, :].rearrange("(jj p) n -> p jj n", p=128)[:, :, 256 * oc2:256 * oc2 + 256], "pool")],
                    first=(hf == 0 or oc2 == 0)))
                if dry or hf == 1:
                    continue
                self._down(dn[oc2], oc2, list(range(NT)), hf, A, AB, NJ)
            if not dry and hf == 1:
                self.tail_begin()
                for t in range(NT):
                    for oc2 in range(4):
                        self._down(dn[oc2], oc2, [t], hf, A, AB, NJ, tail=True)
                self.tail_end()
        if not dry:
            if last_g:
                fw.dma("sp", lambda h: h.dma_start(out=d["nfp"][L].rearrange("(c p) k -> p c k", p=128), in_=NF), reads=[NFB])
            fw.dma("sp", lambda h: h.dma_start(out=d["nfs"][L, g], in_=NFS), reads=[NFSB])
        self.postnorm(L, 5)

    def _down(self, sh, oc2, tiles, hf, A, AB, NJ, tail=False):
        s, sb = sh
        w = self.slot(s, 0, [NJ, 256], BF16)
        for t in tiles:
            c0, n = TILES[t]
            for cc in range(2):
                c = 2 * oc2 + cc
                bank = self.mm("mm", n, [(w[:, jj, 128 * cc:128 * cc + 128], A[:, jj, c0:c0 + n]) for jj in range(NJ)],
                               [AB[jj][t] for jj in range(NJ)] + [sb])
                o = self.CO[:, c, c0:c0 + n]
                if hf == 0:
                    self.act(o, self.PS[bank][:, :n], AF.Copy, reads=[self.PSB[bank]], writes=[self.COB[c][t]])
                else:
                    self.tt(o, o, self.PS[bank][:, :n], ALU.add, reads=[self.PSB[bank], self.COB[c][t]], writes=[self.COB[c][t]])
                if tail:
                    self.tail_chunk(t, c, o, [self.COB[c][t]])

    def build(self, nstage=None, ngroups=G):
        self.init_consts()
        k = 0
        for g in range(ngroups):
            self.load_x(g)
            for L in range(DEPTH):
                for st in range(3):
                    if nstage is not None and k >= nstage:
                        continue
                    k += 1
                    self.marks.append(("g%d L%d %s" % (g, L, ("mix", "cross", "ffn")[st]), self.nmm))
                    if st == 0:
                        if L % 2 == 0:
                            self.a_mixer(L, g)
                        else:
                            self.b_mixer(L, g)
                    elif st == 1:
                        self.cross(L, g)
                    else:
                        self.ffn(L, g)
            self.boundary(None)
            self.store_x(g)
        self.marks.append(("end", self.nmm))
        if not self.dry:
            import os as _os2, json as _json
            if _os2.environ.get("KMARKS"):
                _json.dump(self.marks, open(_os2.environ["KMARKS"], "w"))
            self.fw.run()


def _declare(nc):
    d = {}

    def inp(name, shape):
        d[name] = nc.dram_tensor(name, list(shape), F32, kind="ExternalInput").ap()

    def outp(name, shape):
        d[name] = nc.dram_tensor(name, list(shape), F32, kind="ExternalOutput").ap()

    inp("xT", [D, SEQ + NSEQ * DSEQ])
    inp("memT", [D, NMEM])
    inp("sconv", [2, G, 128, NCH * SG_ * 30])
    inp("spool", [2, G, 128, NCH * SG_ * 15])
    inp("sffn", [DEPTH, G, 128, 44 * SG_ * 2])
    inp("kT", [DEPTH, NSEQ, D, NMEM])
    inp("v", [DEPTH, NSEQ, NMEM, D])
    inp("prm", [128, NPRM])
    inp("ident", [128, 128])
    inp("a_w_in", [2, D, 2 * D])
    inp("a_w_out", [2, D, D])
    inp("p_w_group", [2, 4, 256, 256])
    inp("c_w_q", [DEPTH, D, D])
    inp("c_w_kv", [DEPTH, D, 2 * D])
    inp("c_w_o", [DEPTH, D, D])
    inp("f_w_up", [DEPTH, D, F2])
    inp("f_w_down", [DEPTH, DFF, D])
    d["sconv4"] = d["sconv"].rearrange("j g p (c s k) -> j g p c s k", c=NCH, s=SG_)
    outp("yT", [D, SEQ + NSEQ * DSEQ])
    outp("ncp", [2, D, 30])
    outp("npp", [2, D, 15])
    outp("nfp", [DEPTH, F2, 2])
    outp("mk", [DEPTH, NMEM, D])
    outp("mv", [DEPTH, NMEM, D])
    outp("ncs_", [2, G, 128, NCH * SG_ * 30])
    outp("nps_", [2, G, 128, NCH * SG_ * 15])
    outp("nfs_", [DEPTH, G, 128, 44 * SG_ * 2])
    d["ncs"] = d["ncs_"].rearrange("j g p (c s k) -> j g p c s k", c=NCH, s=SG_)
    d["nps"] = d["nps_"].rearrange("j g p (c s k) -> j g p c s k", c=NCH, s=SG_)
    d["nfs"] = d["nfs_"].rearrange("l g p (c s k) -> l g p c s k", c=44, s=SG_)
    return d


_BF = None


def _bf16_dtype():
    import ml_dtypes
    return ml_dtypes.bfloat16


def build_program(nstage=None, ngroups=G):
    nc = bass.Bass("TRN2", target_bir_lowering=False)
    d = _declare(nc)
    log = []
    Builder(nc, d, True, log).build(nstage, ngroups)
    Builder(nc, d, False, log).build(nstage, ngroups)
    return nc


def _fm(vec):
    v = np.asarray(vec, np.float32)
    sh = v.shape
    v = v.reshape(sh[:-1] + (sh[-1] // 128, 128))
    return np.moveaxis(v, -1, 0)


def _pack_params(inp):
    P = np.zeros((128, NPRM), np.float32)
    P[:, 0:224] = _fm(inp["norm_gains"]).reshape(128, 224)
    P[:, PC_BIN:PC_BIN + 32] = _fm(inp["a_b_in"]).reshape(128, 32)
    for j in range(2):
        for o, nm in ((0, "a_b_dw"), (8, "a_ln_g"), (16, "a_ln_b"), (24, "a_b_out")):
            P[:, PC_A + j * 32 + o:PC_A + j * 32 + o + 8] = _fm(inp[nm][j])
    P[:, PC_PS:PC_PS + 16] = _fm(inp["p_scale"]).reshape(128, 16)
    P[:, PC_FB:PC_FB + 176] = _fm(inp["f_b_dw"]).reshape(128, 176)
    P[:, PC_FW:PC_FW + 528] = _fm(inp["f_w_dw"]).reshape(128, 528)
    aw = _fm(inp["a_w_dw"])
    P[:, PC_AW:PC_AW + 496] = np.transpose(aw, (0, 1, 3, 2)).reshape(128, 496)
    return P


def _state_fm(st, nch):
    J, S, K, F = st.shape
    v = st.reshape(J, G, SG_, K, nch, 128)
    v = np.transpose(v, (0, 1, 5, 4, 2, 3))
    return np.ascontiguousarray(v).reshape(J, G, 128, nch * SG_ * K)


def _state_back(dev, nch, K):
    J = dev.shape[0]
    v = dev.reshape(J, G, 128, nch, SG_, K)
    v = np.transpose(v, (0, 1, 4, 5, 3, 2))
    return v.reshape(J, G * SG_, K, nch * 128)


_PROG = None


def kernel(_ncore=8, _nstage=None, _ngroups=G, **inputs):
    global _PROG
    inp = {k: np.asarray(v) for k, v in inputs.items()}
    ncore = _ncore
    if _PROG is None:
        _PROG = build_program(_nstage, _ngroups)
    nc = _PROG
    prm = _pack_params(inp)
    ident = np.eye(128, dtype=np.float32)
    shared = {k: np.ascontiguousarray(inp[k], dtype=np.float32) for k in
              ("a_w_in", "a_w_out", "p_w_group", "c_w_q", "c_w_kv", "c_w_o", "f_w_up", "f_w_down")}
    in_maps = []
    import os as _os
    _same = int(_os.environ.get("SAME_DATA", "0"))
    for i_ in range(ncore):
        i = 0 if _same else i_
        sl = slice(NSEQ * i, NSEQ * i + NSEQ)
        xs = inp["x_sample"][sl].reshape(NSEQ * DSEQ, D)
        xT = np.ascontiguousarray(np.concatenate([inp["x_prompt"][i], xs], axis=0).T)
        m = {
            "xT": xT,
            "memT": np.ascontiguousarray(inp["mem_prompt"][i].T),
            "sconv": _state_fm(inp["state_conv"][:, sl], NCH),
            "spool": _state_fm(inp["state_pool"][:, sl], NCH),
            "sffn": _state_fm(inp["state_ffn"][:, sl], 44),
            "kT": np.ascontiguousarray(np.transpose(inp["cache_mem_k"][:, sl].reshape(DEPTH, NSEQ, NMEM, D), (0, 1, 3, 2))),
            "v": np.ascontiguousarray(inp["cache_mem_v"][:, sl].reshape(DEPTH, NSEQ, NMEM, D)),
            "prm": prm,
            "ident": ident,
        }
        m.update(shared)
        in_maps.append(m)
    res = run_bass_kernel_spmd(nc, in_maps, core_ids=list(range(ncore)))
    R = res.results
    y_prompt = np.stack([R[i]["yT"][:, :SEQ].T for i in range(ncore)])
    y_sample = np.concatenate([R[i]["yT"][:, SEQ:].T.reshape(NSEQ, DSEQ, D) for i in range(ncore)])
    ncp = np.stack([np.transpose(R[i]["ncp"], (0, 2, 1)) for i in range(ncore)], axis=1)
    npp = np.stack([np.transpose(R[i]["npp"], (0, 2, 1)) for i in range(ncore)], axis=1)
    nfp = np.stack([np.transpose(R[i]["nfp"], (0, 2, 1)) for i in range(ncore)], axis=1)
    mk = np.stack([R[i]["mk"] for i in range(ncore)], axis=1).reshape(DEPTH, ncore, NMEM, NH, D // NH)
    mv = np.stack([R[i]["mv"] for i in range(ncore)], axis=1).reshape(DEPTH, ncore, NMEM, NH, D // NH)
    ncs = np.concatenate([_state_back(R[i]["ncs_"], NCH, 30) for i in range(ncore)], axis=1)
    nps = np.concatenate([_state_back(R[i]["nps_"], NCH, 15) for i in range(ncore)], axis=1)
    nfs = np.concatenate([_state_back(R[i]["nfs_"], 44, 2) for i in range(ncore)], axis=1)
    f = lambda a: np.ascontiguousarray(a, dtype=np.float32)
    return (f(y_prompt), f(y_sample), f(ncp), f(npp), f(nfp), f(mk), f(mv), f(ncs), f(nps), f(nfs))
```
